# Optimizing a Trainium2 kernel written in Bass

```python
import math
import jax
import jax.numpy as jnp
from jax import lax
import numpy as np

D_MODEL = 1024
BATCH = 8
SEQ = 2048
DEPTH = 2

GRID_W = 64
CTX_LEN = 256
N_EVEN = (DEPTH + 1) // 2
N_ODD = DEPTH // 2
N_MOD = 6
EPS = 1e-6
F32 = jnp.float32

A_HEADS = 8
A_KV_HEADS = 2
A_GROUP = A_HEADS // A_KV_HEADS
A_HEAD_DIM = 64
A_Q = A_HEADS * A_HEAD_DIM
A_KV = A_KV_HEADS * A_HEAD_DIM
A_IN = A_Q + 2 * A_KV
A_BLOCK = 128
ROPE_THETA = 10000.0

B_HEADS = 8
B_HEAD_DIM = 64
B_WIDTH = B_HEADS * B_HEAD_DIM
B_DECAY_LORA = 64
B_AAA_LORA = 64
B_GATE_LORA = 128
B_IN = 3 * B_WIDTH + B_DECAY_LORA + B_AAA_LORA + B_GATE_LORA
B_GN_EPS = 64e-5

EVEN_IN = A_IN + B_IN
EVEN_MIX = A_Q + B_WIDTH

C_INNER = 2 * D_MODEL
C_HEAD_DIM = 64
C_HEADS = C_INNER // C_HEAD_DIM
C_GROUPS = 4
C_STATE = 128
C_CONV = 3
C_CHUNK = 128
C_CONV_DIM = C_INNER + 2 * C_GROUPS * C_STATE
ODD_IN = C_INNER + C_CONV_DIM + 2 * C_HEADS

P_HEADS = 8
P_KEYS = 128
P_EXPERTS = P_KEYS * P_KEYS
P_KEY_DIM = 128
P_TOPK = 16
P_BLOCK = 128

kernel_name = 'hybrid_gqa_rwkv7_ssd_peer_dit'


def _rmsnorm(x, gain):
    xf = x.astype(F32)
    y = xf * lax.rsqrt(jnp.mean(xf * xf, axis=-1, keepdims=True) + EPS)
    return (y * gain.astype(F32)).astype(x.dtype)


def _axial_angles(rows):
    t = jnp.arange(rows * GRID_W)
    row = (t // GRID_W).astype(F32)
    col = (t % GRID_W).astype(F32)
    m = A_HEAD_DIM // 4
    inv = ROPE_THETA ** (-jnp.arange(m, dtype=F32) / m)
    return row[:, None] * inv, col[:, None] * inv


def _rope_half(x, ang):
    m = x.shape[-1] // 2
    x1, x2 = x[..., :m], x[..., m:]
    cos, sin = jnp.cos(ang), jnp.sin(ang)
    return jnp.concatenate([x1 * cos - x2 * sin, x2 * cos + x1 * sin], axis=-1)


def _rope_2d(x, ang_r, ang_c):
    shape = (ang_r.shape[0],) + (1,) * (x.ndim - 3) + (ang_r.shape[1],)
    xf = x.astype(F32)
    half = A_HEAD_DIM // 2
    out = jnp.concatenate([_rope_half(xf[..., :half], ang_r.reshape(shape)),
                           _rope_half(xf[..., half:], ang_c.reshape(shape))], axis=-1)
    return out.astype(x.dtype)


def _attend(q, k, v):
    s = jnp.einsum('bqhgd,bkhd->bhgqk', q, k, preferred_element_type=F32) * (A_HEAD_DIM ** -0.5)
    p = jax.nn.softmax(s, axis=-1).astype(v.dtype)
    return jnp.einsum('bhgqk,bkhd->bqhgd', p, v)


def _gqa_mixer(zc, zl, q_gain, k_gain, ang_r, ang_c, with_ctx):
    bsz, lc, _ = zc.shape
    seq = zl.shape[1]

    def qkv(z):
        n = z.shape[1]
        q = _rmsnorm(z[..., :A_Q].reshape(bsz, n, A_KV_HEADS, A_GROUP, A_HEAD_DIM), q_gain)
        k = _rmsnorm(z[..., A_Q:A_Q + A_KV].reshape(bsz, n, A_KV_HEADS, A_HEAD_DIM), k_gain)
        v = z[..., A_Q + A_KV:].reshape(bsz, n, A_KV_HEADS, A_HEAD_DIM)
        return q, k, v

    qc, kc, vc = qkv(zc)
    ql, kl, vl = qkv(zl)
    ql = _rope_2d(ql, ang_r, ang_c)
    kl = _rope_2d(kl, ang_r, ang_c)
    k_all = jnp.concatenate([kc, kl], axis=1)
    v_all = jnp.concatenate([vc, vl], axis=1)
    nb = seq // A_BLOCK
    qb = jnp.moveaxis(ql.reshape(bsz, nb, A_BLOCK, A_KV_HEADS, A_GROUP, A_HEAD_DIM), 1, 0)
    ol = lax.map(lambda qblk: _attend(qblk, k_all, v_all), qb)
    ol = jnp.moveaxis(ol, 0, 1).reshape(bsz, seq, A_Q)
    oc = _attend(qc, kc, vc).reshape(bsz, lc, A_Q) if with_ctx else None
    return oc, ol


def _centred_shift(z):
    zp = jnp.pad(z, ((0, 0), (1, 1), (0, 0)))
    return 0.5 * (zp[:, :-2] + zp[:, 2:])


def _seg_reverse(t, lc):
    return jnp.concatenate([jnp.flip(t[:, :lc], axis=1), jnp.flip(t[:, lc:], axis=1)], axis=1)


def _wkv7_scan(r, w, k, v, kk, a):
    def step(s, inp):
        r_t, w_t, k_t, v_t, kk_t, a_t = inp
        sa = jnp.einsum('bhvk,bhk->bhv', s, -kk_t)
        s = (s * w_t[:, :, None, :] + sa[..., None] * (kk_t * a_t)[:, :, None, :]
             + v_t[..., None] * k_t[:, :, None, :])
        return s, jnp.einsum('bhvk,bhk->bhv', s, r_t)
    s0 = jnp.zeros(r.shape[1:] + (r.shape[-1],), F32)
    _, out = lax.scan(step, s0, (r, w, k, v, kk, a))
    return out


def _head_groupnorm(y, w, b):
    mu = jnp.mean(y, axis=-1, keepdims=True)
    var = jnp.mean(jnp.square(y - mu), axis=-1, keepdims=True)
    return (y - mu) * lax.rsqrt(var + B_GN_EPS) * w + b


def _rwkv7_mixer(zc, zl, mu, w0, w2, a0, a2, g2, k_k, k_a, r_k, gn_w, gn_b):
    bsz, lc, _ = zc.shape
    z = jnp.concatenate([zc + mu * (_centred_shift(zc) - zc),
                         zl + mu * (_centred_shift(zl) - zl)], axis=1)
    ln = z.shape[1]
    o1, o2, o3 = B_WIDTH, 2 * B_WIDTH, 3 * B_WIDTH
    o4 = o3 + B_DECAY_LORA
    o5 = o4 + B_AAA_LORA
    r, k, v = z[..., :o1], z[..., o1:o2], z[..., o2:o3]
    wl, al, gl = z[..., o3:o4], z[..., o4:o5], z[..., o5:]

    def heads(t):
        return t.reshape(bsz, ln, B_HEADS, B_HEAD_DIM).astype(F32)

    rh, vh = heads(r), heads(v)
    kk = heads(k * k_k)
    kk = kk / jnp.maximum(jnp.sqrt(jnp.sum(kk * kk, axis=-1, keepdims=True)), 1e-12)
    rk = r_k.reshape(B_HEADS, B_HEAD_DIM).astype(F32)
    gw = gn_w.reshape(B_HEADS, B_HEAD_DIM).astype(F32)
    gb = gn_b.reshape(B_HEADS, B_HEAD_DIM).astype(F32)
    g = jax.nn.sigmoid(gl) @ g2
    wt = jnp.tanh(wl)
    y = jnp.zeros((bsz, ln, B_HEADS, B_HEAD_DIM), F32)
    for d in range(2):
        w = -jax.nn.softplus(-(w0[d] + wt @ w2[d])) - 0.5
        decay = heads(jnp.exp(-jnp.exp(w.astype(F32))))
        a_lin = jax.nn.sigmoid(a0[d] + al @ a2[d])
        kd = k * (1 + (a_lin - 1) * k_a)
        ah, kdh = heads(a_lin), heads(kd)
        if d == 0:
            order = lambda t: t
        else:
            order = lambda t: _seg_reverse(t, lc)
        seqs = [jnp.moveaxis(order(t), 1, 0) for t in (rh, decay, kdh, vh, kk, ah)]
        out = order(jnp.moveaxis(_wkv7_scan(*seqs), 0, 1))
        bonus = jnp.sum(rh * kdh * rk, axis=-1, keepdims=True) * vh
        y = y + _head_groupnorm(out, gw, gb) + bonus
    y = y.reshape(bsz, ln, B_WIDTH).astype(zc.dtype) * g
    return y[:, :lc], y[:, lc:]


def _even_mixer(zc, zl, q_gain, k_gain, mu, w0, w2, a0, a2, g2, k_k, k_a, r_k, gn_w, gn_b,
                ang_r, ang_c, with_ctx):
    oc, ol = _gqa_mixer(zc[..., :A_IN], zl[..., :A_IN], q_gain, k_gain, ang_r, ang_c, with_ctx)
    rc, rl = _rwkv7_mixer(zc[..., A_IN:], zl[..., A_IN:], mu, w0, w2, a0, a2, g2,
                          k_k, k_a, r_k, gn_w, gn_b)
    yl = jnp.concatenate([ol, rl], axis=-1)
    yc = jnp.concatenate([oc, rc], axis=-1) if with_ctx else None
    return yc, yl


def _dwconv_centred(z, w, b):
    k, ch = w.shape
    y = lax.conv_general_dilated(z, w[:, None, :].astype(z.dtype), window_strides=(1,),
                                 padding=[(k // 2, k // 2)],
                                 dimension_numbers=('NWC', 'WIO', 'NWC'),
                                 feature_group_count=ch)
    return y + b.astype(z.dtype)


def _ssd_chunked(x, dt, a, bm, cm):
    bsz, ln, nh, hp = x.shape
    ng, ns = bm.shape[2], bm.shape[3]
    nr = nh // ng
    q = C_CHUNK
    nc = ln // q
    xd = (x.astype(F32) * dt[..., None]).reshape(bsz, nc, q, ng, nr, hp)
    cum = jnp.cumsum((dt * a).reshape(bsz, nc, q, ng, nr), axis=2)
    bc = bm.astype(F32).reshape(bsz, nc, q, ng, ns)
    cc = cm.astype(F32).reshape(bsz, nc, q, ng, ns)
    lower = jnp.tril(jnp.ones((q, q), bool))[None, None, :, :, None, None]
    seg = cum[:, :, :, None] - cum[:, :, None, :]
    decay_ij = jnp.exp(jnp.where(lower, seg, -jnp.inf))
    cb = jnp.einsum('bcign,bcjgn->bcijg', cc, bc)
    y_diag = jnp.einsum('bcijgr,bcjgrp->bcigrp', cb[..., None] * decay_ij, xd)
    decay_end = jnp.exp(cum[:, :, -1:] - cum)
    states = jnp.einsum('bcjgn,bcjgr,bcjgrp->bcgrpn', bc, decay_end, xd)
    chunk_decay = jnp.exp(cum[:, :, -1])

    def step(s, inp):
        st, dec = inp
        return s * dec[..., None, None] + st, s

    s0 = jnp.zeros((bsz, ng, nr, hp, ns), F32)
    _, s_prev = lax.scan(step, s0, (jnp.moveaxis(states, 1, 0), jnp.moveaxis(chunk_decay, 1, 0)))
    s_prev = jnp.moveaxis(s_prev, 0, 1)
    y_off = jnp.einsum('bcign,bcgrpn,bcigr->bcigrp', cc, s_prev, jnp.exp(cum))
    return (y_diag + y_off).reshape(bsz, ln, nh, hp)


def _mamba2_mixer(zc, zl, conv_w, conv_b, dt_bias, a_log, d_skip, norm_w, with_ctx):
    bsz, lc, _ = zc.shape
    o1 = C_INNER
    o2 = C_INNER + C_CONV_DIM
    gn = C_GROUPS * C_STATE
    z_gate = jnp.concatenate([zc[..., :o1], zl[..., :o1]], axis=1)
    xbc = jnp.concatenate([jax.nn.silu(_dwconv_centred(zc[..., o1:o2], conv_w, conv_b)),
                           jax.nn.silu(_dwconv_centred(zl[..., o1:o2], conv_w, conv_b))], axis=1)
    dt_raw = jnp.concatenate([zc[..., o2:], zl[..., o2:]], axis=1).astype(F32)
    ln = xbc.shape[1]
    xs = xbc[..., :C_INNER].reshape(bsz, ln, C_HEADS, C_HEAD_DIM)
    bm = xbc[..., C_INNER:C_INNER + gn].reshape(bsz, ln, C_GROUPS, C_STATE)
    cm = xbc[..., C_INNER + gn:].reshape(bsz, ln, C_GROUPS, C_STATE)
    y = d_skip.astype(F32)[:, None] * xs.astype(F32)
    for d in range(2):
        dt = jax.nn.softplus(dt_raw[..., d * C_HEADS:(d + 1) * C_HEADS] + dt_bias[d].astype(F32))
        a = -jnp.exp(a_log[d].astype(F32))
        if d == 0:
            order = lambda t: t
        else:
            order = lambda t: _seg_reverse(t, lc)
        y = y + order(_ssd_chunked(order(xs), order(dt), a, order(bm), order(cm)))
    y = y.reshape(bsz, ln, C_INNER).astype(zc.dtype)
    if not with_ctx:
        y, z_gate = y[:, lc:], z_gate[:, lc:]
    n = y.shape[1]
    yg = (y * jax.nn.silu(z_gate)).reshape(bsz, n, C_GROUPS, C_INNER // C_GROUPS)
    yn = _rmsnorm(yg, norm_w.reshape(C_GROUPS, C_INNER // C_GROUPS)).reshape(bsz, n, C_INNER)
    if with_ctx:
        return yn[:, :lc], yn[:, lc:]
    return None, yn


def _peer(h, w_q, sub_keys, u_tab, v_tab):
    t_all, dm = h.shape

    def block(hb):
        n = hb.shape[0]
        q = (hb @ w_q).reshape(n, P_HEADS, 2, P_KEY_DIM)
        s = jnp.einsum('thpd,hpnd->thpn', q, sub_keys).astype(F32)
        sv, si = lax.top_k(s, P_TOPK)
        cand_s = (sv[:, :, 0, :, None] + sv[:, :, 1, None, :]).reshape(n, P_HEADS, P_TOPK * P_TOPK)
        cand_i = (si[:, :, 0, :, None] * P_KEYS + si[:, :, 1, None, :]).reshape(n, P_HEADS, P_TOPK * P_TOPK)
        best_s, pos = lax.top_k(cand_s, P_TOPK)
        idx = jnp.take_along_axis(cand_i, pos, axis=-1)
        gate = jax.nn.softmax(best_s, axis=-1)
        act = jax.nn.gelu(jnp.einsum('td,thkd->thk', hb, u_tab[idx]).astype(F32), approximate=False)
        coef = (gate * act).astype(hb.dtype)
        return jnp.einsum('thk,thkd->td', coef, v_tab[idx])

    out = lax.map(block, h.reshape(t_all // P_BLOCK, P_BLOCK, dm))
    return out.reshape(t_all, dm)


def setup_inputs(seed: int = 0) -> dict:
    key = jax.random.key(seed)
    ks = list(jax.random.split(key, 40))

    def nrm(shape, scale):
        return jax.random.normal(ks.pop(), shape, F32) * scale

    def unif(shape, lo, hi):
        return jax.random.uniform(ks.pop(), shape, F32, lo, hi)

    def gain(shape):
        return 1.0 + nrm(shape, 0.02)

    dt0 = jnp.exp(unif((N_ODD, 2, C_HEADS), math.log(1e-3), math.log(1e-1)))
    return {
        'x': nrm((BATCH, SEQ, D_MODEL), 1.0),
        'c': nrm((BATCH, D_MODEL), 1.0),
        'ctx': nrm((BATCH, CTX_LEN, D_MODEL), 1.0),
        'c_ctx': nrm((D_MODEL,), 1.0),
        'mod_w': nrm((DEPTH, D_MODEL, N_MOD * D_MODEL), 0.3 * D_MODEL ** -0.5),
        'mod_b': nrm((DEPTH, N_MOD * D_MODEL), 0.02),
        'norm1_g': gain((DEPTH, D_MODEL)),
        'norm2_g': gain((DEPTH, D_MODEL)),
        'ev_w_in': nrm((N_EVEN, D_MODEL, EVEN_IN), D_MODEL ** -0.5),
        'ev_w_out': nrm((N_EVEN, EVEN_MIX, D_MODEL), EVEN_MIX ** -0.5),
        'attn_q_gain': gain((N_EVEN, A_HEAD_DIM)),
        'attn_k_gain': gain((N_EVEN, A_HEAD_DIM)),
        'rw_mu': unif((N_EVEN, B_IN), 0.0, 1.0),
        'rw_w0': unif((N_EVEN, 2, B_WIDTH), -6.5, -1.0),
        'rw_w2': nrm((N_EVEN, 2, B_DECAY_LORA, B_WIDTH), 0.1),
        'rw_a0': nrm((N_EVEN, 2, B_WIDTH), 0.1),
        'rw_a2': nrm((N_EVEN, 2, B_AAA_LORA, B_WIDTH), 0.5 * B_AAA_LORA ** -0.5),
        'rw_g2': nrm((N_EVEN, B_GATE_LORA, B_WIDTH), B_GATE_LORA ** -0.5),
        'rw_k_k': 0.85 + nrm((N_EVEN, B_WIDTH), 0.02),
        'rw_k_a': gain((N_EVEN, B_WIDTH)),
        'rw_r_k': nrm((N_EVEN, B_WIDTH), 0.1),
        'rw_gn_w': gain((N_EVEN, B_WIDTH)),
        'rw_gn_b': nrm((N_EVEN, B_WIDTH), 0.02),
        'ssd_w_in': nrm((N_ODD, D_MODEL, ODD_IN), D_MODEL ** -0.5),
        'ssd_conv_w': nrm((N_ODD, C_CONV, C_CONV_DIM), C_CONV ** -0.5),
        'ssd_conv_b': nrm((N_ODD, C_CONV_DIM), 0.02),
        'ssd_dt_bias': dt0 + jnp.log(-jnp.expm1(-dt0)),
        'ssd_a_log': jnp.log(unif((N_ODD, 2, C_HEADS), 1.0, 16.0)),
        'ssd_d': gain((N_ODD, C_HEADS)),
        'ssd_norm_w': gain((N_ODD, C_INNER)),
        'ssd_w_out': nrm((N_ODD, C_INNER, D_MODEL), C_INNER ** -0.5),
        'peer_w_q': nrm((DEPTH, D_MODEL, P_HEADS * 2 * P_KEY_DIM), D_MODEL ** -0.5),
        'peer_keys': nrm((DEPTH, P_HEADS, 2, P_KEYS, P_KEY_DIM), P_KEY_DIM ** -0.5),
        'peer_u': nrm((DEPTH, P_EXPERTS, D_MODEL), D_MODEL ** -0.5),
        'peer_v': nrm((DEPTH, P_EXPERTS, D_MODEL), P_HEADS ** -0.5),
    }


def reference(x, c, ctx, c_ctx, mod_w, mod_b, norm1_g, norm2_g, ev_w_in, ev_w_out,
              attn_q_gain, attn_k_gain, rw_mu, rw_w0, rw_w2, rw_a0, rw_a2, rw_g2, rw_k_k,
              rw_k_a, rw_r_k, rw_gn_w, rw_gn_b, ssd_w_in, ssd_conv_w, ssd_conv_b, ssd_dt_bias,
              ssd_a_log, ssd_d, ssd_norm_w, ssd_w_out, peer_w_q, peer_keys, peer_u, peer_v):
    bsz, seq, dm = x.shape
    lc = ctx.shape[1]
    rows = seq // GRID_W
    ang_r, ang_c = _axial_angles(rows)
    xl, xc = x, ctx
    for i in range(DEPTH):
        last = i == DEPTH - 1
        j = i // 2
        ml = (jax.nn.silu(c) @ mod_w[i] + mod_b[i]).reshape(bsz, 1, N_MOD, dm)
        mc = (jax.nn.silu(c_ctx) @ mod_w[i] + mod_b[i]).reshape(1, 1, N_MOD, dm)
        hl = _rmsnorm(xl, norm1_g[i]) * (1 + ml[:, :, 1]) + ml[:, :, 0]
        hc = _rmsnorm(xc, norm1_g[i]) * (1 + mc[:, :, 1]) + mc[:, :, 0]
        if i % 2 == 0:
            yc, yl = _even_mixer(hc @ ev_w_in[j], hl @ ev_w_in[j], attn_q_gain[j], attn_k_gain[j],
                                 rw_mu[j], rw_w0[j], rw_w2[j], rw_a0[j], rw_a2[j], rw_g2[j],
                                 rw_k_k[j], rw_k_a[j], rw_r_k[j], rw_gn_w[j], rw_gn_b[j],
                                 ang_r, ang_c, not last)
            w_out = ev_w_out[j]
        else:
            yc, yl = _mamba2_mixer(hc @ ssd_w_in[j], hl @ ssd_w_in[j], ssd_conv_w[j], ssd_conv_b[j],
                                   ssd_dt_bias[j], ssd_a_log[j], ssd_d[j], ssd_norm_w[j], not last)
            w_out = ssd_w_out[j]
        xl = xl + ml[:, :, 2] * (yl @ w_out)
        hl2 = _rmsnorm(xl, norm2_g[i]) * (1 + ml[:, :, 4]) + ml[:, :, 3]
        if last:
            f = _peer(hl2.reshape(-1, dm), peer_w_q[i], peer_keys[i], peer_u[i], peer_v[i])
            xl = xl + ml[:, :, 5] * f.reshape(bsz, seq, dm)
        else:
            xc = xc + mc[:, :, 2] * (yc @ w_out)
            hc2 = _rmsnorm(xc, norm2_g[i]) * (1 + mc[:, :, 4]) + mc[:, :, 3]
            f = _peer(jnp.concatenate([hc2.reshape(-1, dm), hl2.reshape(-1, dm)], axis=0),
                      peer_w_q[i], peer_keys[i], peer_u[i], peer_v[i])
            n_ctx_tok = bsz * lc
            xc = xc + mc[:, :, 5] * f[:n_ctx_tok].reshape(bsz, lc, dm)
            xl = xl + ml[:, :, 5] * f[n_ctx_tok:].reshape(bsz, seq, dm)
    return xl
```

```python
import math
import os
from contextlib import ExitStack
import numpy as np
import concourse.bass as bass
import concourse.mybir as mybir
from concourse.bass_utils import run_bass_kernel_spmd

F32 = mybir.dt.float32
U32 = mybir.dt.uint32
I32 = mybir.dt.int32
ALU = mybir.AluOpType
AF = mybir.ActivationFunctionType
AX = mybir.AxisListType

ENGS = ("pe", "act", "dve", "pool", "sp")
EPOCH = 30000
NDMASEM = {"sp": 24, "pool": 10, "act": 6}

D = 1024
LC = 256
SEQ = 2048
L = LC + SEQ
NT = L // 128
EPS = 1e-6
EV_IN = 2560
ODD_IN = 5184


class Op:
    __slots__ = ("eng", "fn", "reads", "writes", "dma", "deps", "sig", "idx", "epos")

    def __init__(self, eng, fn, reads, writes, dma):
        self.eng = eng
        self.fn = fn
        self.reads = reads
        self.writes = writes
        self.dma = dma
        self.deps = []
        self.sig = None


class Prog:
    def __init__(self, nc, stack):
        self.nc = nc
        self.stack = stack
        self.ops = []
        self.done = 0
        self.last_w = {}
        self.readers = {}
        self.epos = {e: 0 for e in ENGS}
        self.cnt = {e: 0 for e in ENGS}
        self.dcnt = {e: 0 for e in ENGS}
        self.sems = {}
        self.seen = {e: {} for e in ENGS}
        self.pending = {e: [] for e in ENGS}

    def add(self, eng, fn, r=(), w=(), dma=False):
        op = Op(eng, fn, tuple(r), tuple(w), dma)
        op.idx = len(self.ops)
        self.ops.append(op)
        return op

    def pe(self, fn, r=(), w=()):
        return self.add("pe", fn, r, w)

    def act(self, fn, r=(), w=()):
        return self.add("act", fn, r, w)

    def dve(self, fn, r=(), w=()):
        return self.add("dve", fn, r, w)

    def pool(self, fn, r=(), w=()):
        return self.add("pool", fn, r, w)

    def dma(self, fn, r=(), w=(), q="sp"):
        return self.add(q, fn, r, w, dma=True)

    def sem(self, key):
        if key not in self.sems:
            self.sems[key] = self.stack.enter_context(self.nc.semaphore("s_%s_%s_%d" % key))
        return self.sems[key]

    def flush(self):
        nc = self.nc
        ops = self.ops
        base = self.done
        new = ops[base:]
        if not new:
            return
        last_w, readers = self.last_w, self.readers
        for op in new:
            op.epos = self.epos[op.eng]
            self.epos[op.eng] += 1
            deps = set()
            for k in op.reads:
                if k in last_w:
                    deps.add(last_w[k])
            for k in op.writes:
                if k in last_w:
                    deps.add(last_w[k])
                for rr in readers.get(k, ()):
                    deps.add(rr)
            deps.discard(op.idx)
            for k in op.writes:
                last_w[k] = op.idx
                readers[k] = []
            for k in op.reads:
                if k not in op.writes:
                    readers.setdefault(k, []).append(op.idx)
            fd = []
            for d in deps:
                if d < base:
                    continue
                p = ops[d]
                if (not p.dma) and (not op.dma) and p.eng == op.eng:
                    if p.eng == "pe":
                        continue
                    if op.epos - p.epos > 3:
                        continue
                fd.append(d)
            if self.pending[op.eng]:
                fd.extend(self.pending[op.eng])
                self.pending[op.eng] = []
            op.deps = sorted(set(fd))
        needs = set()
        for op in new:
            for d in op.deps:
                needs.add(d)
        frontier = []
        lastc = {}
        for op in new:
            if op.dma:
                frontier.append(op.idx)
            elif op.fn is not None:
                lastc[op.eng] = op.idx
        for e, i in lastc.items():
            needs.add(i)
            frontier.append(i)
        for op in new:
            if op.dma:
                slot = self.dcnt[op.eng] % NDMASEM[op.eng]
                op.sig = ("d", op.eng, slot, 16 * (self.dcnt[op.eng] // NDMASEM[op.eng] + 1))
                self.dcnt[op.eng] += 1
            elif op.idx in needs:
                ep = self.cnt[op.eng] // EPOCH
                op.sig = ("c", op.eng, ep, self.cnt[op.eng] % EPOCH + 1)
                self.cnt[op.eng] += 1
            if op.sig is not None:
                self.sem(op.sig[:3])
        sems = self.sems
        seen_all = self.seen

        def run_engine(ename, eng):
            seen = seen_all[ename]
            for op in new:
                if op.eng != ename:
                    continue
                waits = {}
                if op.dma and op.sig[3] > 16:
                    waits[op.sig[:3]] = op.sig[3] - 16
                for d in op.deps:
                    sg = ops[d].sig
                    key = sg[:3]
                    if waits.get(key, 0) < sg[3]:
                        waits[key] = sg[3]
                for key, v in waits.items():
                    if seen.get(key, 0) >= v:
                        continue
                    seen[key] = v
                    eng.wait_ge(sems[key], v)
                ins = op.fn(eng)
                if op.sig is not None and ins is not None:
                    ins.then_inc(sems[op.sig[:3]], 16 if op.dma else 1)

        with nc.Block() as block:
            @block.tensor
            def _(e):
                run_engine("pe", e)

            @block.scalar
            def _(e):
                run_engine("act", e)

            @block.vector
            def _(e):
                run_engine("dve", e)

            @block.gpsimd
            def _(e):
                run_engine("pool", e)

            @block.sync
            def _(e):
                run_engine("sp", e)
        self.done = len(ops)
        for e in ENGS:
            self.pending[e] = list(frontier)


INPUT_SHAPES = {
    "x": [SEQ, D], "c": [D], "ctx": [LC, D], "c_ctx": [D],
    "mod_w": [2, D, 6 * D], "mod_b": [2, 6 * D], "norm1_g": [2, D], "norm2_g": [2, D],
    "ev_w_in": [1, D, EV_IN], "ev_w_out": [1, D, D], "attn_q_gain": [1, 64], "attn_k_gain": [1, 64],
    "rw_mu": [1, 1792], "rw_w0": [1, 2, 512], "rw_w2": [1, 2, 64, 512], "rw_a0": [1, 2, 512],
    "rw_a2": [1, 2, 64, 512], "rw_g2": [1, 128, 512], "rw_k_k": [1, 512], "rw_k_a": [1, 512],
    "rw_r_k": [1, 512], "rw_gn_w": [1, 512], "rw_gn_b": [1, 512],
    "ssd_w_in": [1, D, ODD_IN], "ssd_conv_w": [1, 3, 3072], "ssd_conv_b": [1, 3072],
    "ssd_dt_bias": [1, 2, 32], "ssd_a_log": [1, 2, 32], "ssd_d": [1, 32], "ssd_norm_w": [1, 2048],
    "ssd_w_out": [1, 2048, D], "peer_w_q": [2, D, 2048], "peer_keys": [2, 8, 2, 128, 128],
    "peer_u": [2, 16384, D], "peer_v": [2, 16384, D],
    "k_ident": [128, 128], "k_flip": [128, 128], "k_rope": [L, 128],
    "k_tri": [128, 128], "k_neg4": [128, 512], "k_iota": [128, 256],
}


class Builder:
    def __init__(self, debug_outs=(), stop_after=None):
        self.nc = bass.Bass("TRN2", target_bir_lowering=False)
        self.stack = ExitStack()
        self.sstack = None
        self.P = Prog(self.nc, self.stack)
        self.debug_outs = set(debug_outs)
        self.stop_after = stop_after
        self.inp = {}
        for k, shp in INPUT_SHAPES.items():
            self.inp[k] = self.nc.dram_tensor(k, shp, F32, kind="ExternalInput").ap()
        self.out = self.nc.dram_tensor("out", [SEQ, D], F32, kind="ExternalOutput").ap()
        self.uid = 0

    def sb(self, name, shape, dt=F32):
        st = self.sstack if self.sstack is not None else self.stack
        return st.enter_context(self.nc.sbuf_tensor(name, shape, dt))

    def begin(self):
        self.sstack = ExitStack()

    def end(self):
        self.P.flush()
        self.sstack.close()
        self.sstack = None

    def ps(self, name, shape, dt=F32):
        return self.stack.enter_context(self.nc.psum_tensor(name, shape, dt))

    def dk(self, name):
        return [("dbg", name)] if name in self.debug_outs else []

    def dram(self, name, shape, dt=F32):
        kind = "ExternalOutput" if name in self.debug_outs else "Internal"
        return self.nc.dram_tensor(name, shape, dt, kind=kind).ap()

    def tt(self, out, a, b, op, r, w):
        self.P.dve(lambda e: e.tensor_tensor(out, a, b, op), r, w)

    def ts(self, out, a, s1, s2, op0, op1, r, w, accum=None):
        if op1 is None:
            self.P.dve(lambda e: e.tensor_scalar(out, a, s1, None, op0), r, w)
        elif accum is None:
            self.P.dve(lambda e: e.tensor_scalar(out, a, s1, s2, op0, op1), r, w)
        else:
            self.P.dve(lambda e: e.tensor_scalar(out, a, s1, s2, op0, op1, accum), r, w)

    def stt(self, out, a, s, b, op0, op1, r, w, accum=None):
        if accum is None:
            self.P.dve(lambda e: e.scalar_tensor_tensor(out, a, s, b, op0, op1), r, w)
        else:
            self.P.dve(lambda e: e.scalar_tensor_tensor(out, a, s, b, op0, op1, accum), r, w)

    def actf(self, out, in_, func, r, w, bias=0.0, scale=1.0, accum=None):
        if accum is None:
            self.P.act(lambda e: e.activation(out, in_, func, bias=bias, scale=scale), r, w)
        else:
            self.P.act(lambda e: e.activation(out, in_, func, bias=bias, scale=scale, accum_out=accum), r, w)

    def rsqrt(self, out, in_, scale, bias, r, w):
        self.actf(out, in_, AF.Sqrt, r, w, bias=bias, scale=scale)
        self.P.dve(lambda e: e.reciprocal(out, out), w, w)

    def mm(self, out, lhsT, rhs, start, stop, r, w):
        self.P.pe(lambda e: e.matmul(out, lhsT, rhs, start=start, stop=stop), r, w)

    def tr(self, out, in_, r, w):
        ident = self.ident
        n = in_.shape[0]
        self.P.pe(lambda e: e.transpose(out, in_, ident[0:n, 0:n]), r, w)

    def ld(self, out, in_, r, w, q="sp"):
        self.P.dma(lambda e: e.dma_start(out=out, in_=in_), r, w, q=q)

    def build(self):
        P = self.P
        self.identt = self.sb("identt", [128, 128])
        self.ident = self.identt[:, :]
        self.ld(self.ident, self.inp["k_ident"], [], ["ident"])
        self.flipt = self.sb("flipt", [128, 128])
        self.ld(self.flipt[:, :], self.inp["k_flip"], [], ["flip"])
        self.psA = self.ps("psA", [128, 2048])
        self.psB = self.ps("psB", [128, 1024])
        self.psC = self.ps("psC", [128, 1024])
        self.psA_i = 0
        self.XR = self.dram("XR", [L, D])
        self.MODR = self.dram("MODR", [2, 2, 6 * D])
        self.Z0 = self.dram("Z0", [L, EV_IN])
        self.MIX0 = self.dram("MIX0", [L, D])
        self.NS = 3096
        self.SEQ = [self.dram(f"SEQ{d}", [L, self.NS]) for d in range(2)]
        self.HIST = [self.dram(f"HIST{d}", [L, 1024]) for d in range(2)]
        self.Y = [self.dram(f"Y{d}", [L, 512]) for d in range(2)]
        self.G = self.dram("G", [L, 512])
        self.H2 = self.dram("H2", [L, D])
        self.Z1 = self.dram("Z1", [L, ODD_IN])
        self.XBC = self.dram("XBC", [L, 3072])
        self.YD = [self.dram(f"YD{d}", [L, 2048]) for d in range(2)]
        self.MIX1 = self.dram("MIX1", [SEQ, 2048])
        self.IDX = self.dram("IDX", [L, 128], U32)
        self.GATE = self.dram("GATE", [L, 128])

        self.colmod = self.sb("colmod", [128, 2, 6, 8])
        self.gate = [[self.sb(f"gate{w}{m}", [128, D]) for m in range(2)] for w in range(2)]
        self.ng = self.sb("ng", [128, 2, 8])
        self.Acol = self.sb("Acol", [128, 2, 2, 8])
        self.nm_junk = self.sb("nm_junk", [128, D])
        self.nm_ss = self.sb("nm_ss", [128, 1])
        self.nm_xn = self.sb("nm_xn", [128, D])
        self.cur_x_is_input = True
        stages = [
            ("mod", lambda: self.stage_mod()),
            ("inproj0", lambda: self.stage_inproj(0, self.inp["ev_w_in"][0], EV_IN, self.Z0)),
            ("attn", lambda: self.stage_attn()),
            ("rwprep", lambda: self.stage_rw_prep()),
            ("rwscan", lambda: self.stage_rw_scan()),
            ("rwpost", lambda: self.stage_rw_post()),
            ("rwcomb", lambda: self.stage_rw_comb()),
            ("outproj0", lambda: self.stage_outproj(0, self.MIX0, ("MIXa", "MIXb"), D, self.inp["ev_w_out"][0], 0, True)),
            ("route0", lambda: self.stage_peer_route(0, 0)),
            ("expert0", lambda: self.stage_peer_expert(0, 0, False)),
            ("inproj1", lambda: self.stage_inproj(1, self.inp["ssd_w_in"][0], ODD_IN, self.Z1)),
            ("ssdconv", lambda: self.stage_ssd_conv()),
            ("ssdscan", lambda: self.stage_ssd_scan()),
            ("ssdcomb", lambda: self.stage_ssd_comb()),
            ("outproj1", lambda: self.stage_outproj(1, self.MIX1, ("MIX1",), 2048, self.inp["ssd_w_out"][0], 2, False)),
            ("route1", lambda: self.stage_peer_route(1, 2)),
            ("expert1", lambda: self.stage_peer_expert(1, 2, True)),
        ]
        for name, fn in stages:
            self.begin()
            fn()
            self.end()
            if name == "outproj0":
                self.cur_x_is_input = False
            if self.stop_after == name:
                break
        return self.finish()

    def finish(self):
        P = self.P
        P.flush()
        P.add("sp", lambda e: None, r=[], w=[])
        P.flush()
        return self.nc

    def xsrc(self, t):
        if self.cur_x_is_input:
            if t < 2:
                return self.inp["ctx"][t * 128:(t + 1) * 128, :]
            return self.inp["x"][(t - 2) * 128:(t - 1) * 128, :]
        return self.XR[t * 128:(t + 1) * 128, :]

    def stage_mod(self):
        P = self.P
        craw = self.sb("craw", [128, 2, 8])
        sc2 = self.sb("sc2", [128, 8, 2])
        self.ld(craw[:, 0, :], self.inp["c"].rearrange("(p c) -> p c", c=8), [], ["craw"])
        self.ld(craw[:, 1, :], self.inp["c_ctx"].rearrange("(p c) -> p c", c=8), [], ["craw"])
        self.actf(sc2[:, :, :].rearrange("p c w -> p w c"), craw[:, :, :], AF.Silu, ["craw"], ["sc2"])
        wt = [self.sb(f"modw{s}", [128, 3072]) for s in range(2)]
        mb = self.sb("modb", [2, 3072])
        rr = self.sb("modr", [2, 3072])
        n = 0
        for i in range(2):
            wv = self.inp["mod_w"][i].rearrange("(p c) n -> c p n", c=8)
            for hf in range(2):
                cs = slice(hf * 3072, (hf + 1) * 3072)
                self.ld(mb[:, :], self.inp["mod_b"][i:i + 1, cs].to_broadcast([2, 3072]), [], ["modb"])
                for kc in range(8):
                    s = n % 2
                    n += 1
                    self.ld(wt[s][:, :], wv[kc][:, cs], [], [f"modw{s}"], q=("sp" if s == 0 else "pool"))
                    for j in range(6):
                        o = self.psA[0:2, j * 512:(j + 1) * 512] if j < 4 else self.psB[0:2, (j - 4) * 512:(j - 3) * 512]
                        key = f"psA{j}" if j < 4 else f"psB{j - 4}"
                        self.mm(o, sc2[:, kc, :], wt[s][:, j * 512:(j + 1) * 512], kc == 0, kc == 7,
                                ["sc2", f"modw{s}"], [key])
                for j in range(6):
                    o = self.psA[0:2, j * 512:(j + 1) * 512] if j < 4 else self.psB[0:2, (j - 4) * 512:(j - 3) * 512]
                    key = f"psA{j}" if j < 4 else f"psB{j - 4}"
                    self.tt(rr[:, j * 512:(j + 1) * 512], o, mb[:, j * 512:(j + 1) * 512], ALU.add,
                            [key, "modb"], ["modr"])
                self.ld(self.MODR[i][:, cs], rr[:, :], ["modr"], [("MODR", i)] + self.dk("MODR"))

    def load_mod(self, i):
        for who in range(2):
            self.ld(self.colmod[:, who, :, :],
                    self.MODR[i, who].rearrange("(m p c) -> p m c", m=6, p=128, c=8),
                    [("MODR", i)], ["colmod"])
            for m in range(2):
                self.ld(self.gate[who][m][:, :],
                        self.MODR[i, who:who + 1, (2 + 3 * m) * D:(3 + 3 * m) * D].to_broadcast([128, D]),
                        [("MODR", i)], [f"gate{who}{m}"], q="pool")
        self.ld(self.ng[:, 0, :], self.inp["norm1_g"][i].rearrange("(p c) -> p c", c=8), [], ["ng"])
        self.ld(self.ng[:, 1, :], self.inp["norm2_g"][i].rearrange("(p c) -> p c", c=8), [], ["ng"])
        for k in range(2):
            for who in range(2):
                self.stt(self.Acol[:, k, who, :], self.colmod[:, who, 1 + 3 * k, :], 1.0, self.ng[:, k, :],
                         ALU.add, ALU.mult, ["colmod", "ng"], ["Acol"])

    def norm_mod_T(self, xt, xkey, k, who, hT, hkey, pskey="psB"):
        junk, ss, xn = self.nm_junk, self.nm_ss, self.nm_xn
        self.actf(junk[:, :], xt, AF.Square, [xkey], ["nm_junk", "nm_ss"], accum=ss[:, :])
        self.rsqrt(ss[:, :], ss[:, :], 1.0 / D, EPS, ["nm_ss"], ["nm_ss"])
        self.ts(xn[:, :], xt, ss[:, 0:1], None, ALU.mult, None, [xkey, "nm_ss"], ["nm_xn"])
        pst = self.psB if pskey == "psB" else self.psC
        xv = xn[:, :].rearrange("p (q c) -> p c q", c=8)
        for c in range(8):
            self.tr(pst[:, c * 128:(c + 1) * 128], xv[:, c, :], ["nm_xn", "ident"], [pskey + str(c // 4)])
        for c in range(8):
            A = self.Acol[:, k, who, c:c + 1]
            B = self.colmod[:, who, 3 * k, c:c + 1]
            self.actf(hT[:, c, :], pst[:, c * 128:(c + 1) * 128], AF.Identity,
                      [pskey + str(c // 4), "Acol", "colmod"], [hkey], bias=B, scale=A)

    def stage_inproj(self, i, W, N, Z):
        P = self.P
        self.load_mod(i)
        GW = 2560 if N == EV_IN else 2592
        ng = N // GW
        wsb = self.sb(f"win{i}", [128, 8, GW])
        xts = [self.sb(f"ip_x{i}{s}", [128, D]) for s in range(2)]
        hT = self.sb(f"ip_hT{i}", [128, 8, 128])
        zt = [self.sb(f"ip_z{i}{s}", [128, GW]) for s in range(2)]
        Wv = W.rearrange("(p c) n -> p c n", c=8)
        for g in range(ng):
            for c in range(8):
                self.ld(wsb[:, c, :], Wv[:, c, g * GW:(g + 1) * GW], [], ["wsb"], q=("sp" if c % 2 == 0 else "pool"))
            for t in range(NT):
                s = t % 2
                who = 1 if t < 2 else 0
                self.ld(xts[s][:, :], self.xsrc(t), [("XR", t)], [f"ip_x{s}"])
                self.norm_mod_T(xts[s][:, :], f"ip_x{s}", 0, who, hT, "ip_hT")
                nch = (GW + 511) // 512
                for j in range(nch):
                    w = min(512, GW - j * 512)
                    b = self.psA_i % 4
                    self.psA_i += 1
                    for c in range(8):
                        self.mm(self.psA[:, b * 512:b * 512 + w], hT[:, c, :], wsb[:, c, j * 512:j * 512 + w],
                                c == 0, c == 7, ["ip_hT", "wsb"], [f"psA{b}"])
                    if j % 2 == 0:
                        self.P.dve(lambda e, o=zt[s][:, j * 512:j * 512 + w], a=self.psA[:, b * 512:b * 512 + w]:
                                   e.tensor_copy(o, a), [f"psA{b}"], [f"ip_z{s}"])
                    else:
                        self.P.act(lambda e, o=zt[s][:, j * 512:j * 512 + w], a=self.psA[:, b * 512:b * 512 + w]:
                                   e.copy(o, a), [f"psA{b}"], [f"ip_z{s}"])
                self.ld(Z[t * 128:(t + 1) * 128, g * GW:(g + 1) * GW], zt[s][:, :], [f"ip_z{s}"],
                        [("Z", i, t)] + self.dk("Z%d" % i), q="pool")

    def stage_attn(self):
        P = self.P
        Z = self.Z0
        QT = self.sb("at_QT", [64, NT, 8, 128])
        KT = self.sb("at_KT", [64, 2, L])
        V1 = self.sb("at_V1", [128, NT, 2, 65])
        gq = self.sb("at_g", [128, 10, 64])
        zq = [self.sb(f"at_zq{s}", [128, 768]) for s in range(2)]
        rt = [self.sb(f"at_rt{s}", [128, 128]) for s in range(2)]
        sq = self.sb("at_sq", [128, 640])
        ss = self.sb("at_ss", [128, 10])
        qa = self.sb("at_qa", [128, 640])
        qb = self.sb("at_qb", [128, 640])
        for h in range(10):
            src = self.inp["attn_q_gain"] if h < 8 else self.inp["attn_k_gain"]
            self.ld(gq[:, h, :], src[0:1, :].to_broadcast([128, 64]), [], ["at_g"], q="pool")
        P.dve(lambda e: e.memset(V1[:, :, :, 64:65], 1.0), [], ["at_V1"])
        for t in range(NT):
            s = t % 2
            self.ld(zq[s][:, :], Z[t * 128:(t + 1) * 128, 0:768], [("Z", 0, t)], [f"at_zq{s}"])
            self.ld(rt[s][:, :], self.inp["k_rope"][t * 128:(t + 1) * 128, :], [], [f"at_rt{s}"], q="pool")
            qk = zq[s][:, 0:640]
            qk3 = qk.rearrange("p (h d) -> p h d", d=64)
            self.tt(sq[:, :], qk, qk, ALU.mult, [f"at_zq{s}"], ["at_sq"])
            P.dve(lambda e, o=ss[:, :], i=sq[:, :].rearrange("p (h d) -> p h d", d=64): e.tensor_reduce(o, i, AX.X, ALU.add),
                  ["at_sq"], ["at_ss"])
            self.rsqrt(ss[:, :], ss[:, :], 1.0 / 64, EPS, ["at_ss"], ["at_ss"])
            qa3 = qa[:, :].rearrange("p (h d) -> p h d", d=64)
            self.tt(qa3, qk3, ss[:, :].unsqueeze(2).to_broadcast([128, 10, 64]), ALU.mult, [f"at_zq{s}", "at_ss"], ["at_qa"])
            self.tt(qa3, qa3, gq[:, :, :], ALU.mult, ["at_qa", "at_g"], ["at_qa"])
            Cb = rt[s][:, 0:64].unsqueeze(1).to_broadcast([128, 10, 64])
            qb3 = qb[:, :].rearrange("p (h d) -> p h d", d=64)
            self.tt(qb3, qa3, Cb, ALU.mult, ["at_qa", f"at_rt{s}"], ["at_qb"])
            qa5 = qa[:, :].rearrange("p (h a x m) -> p h a x m", a=2, x=2, m=16)
            sq5 = sq[:, :].rearrange("p (h a x m) -> p h a x m", a=2, x=2, m=16)
            S4 = rt[s][:, 64:128].rearrange("p (a x m) -> p a x m", a=2, x=2)
            for x in range(2):
                Sb = S4[:, :, x, :].unsqueeze(1).to_broadcast([128, 10, 2, 16])
                self.tt(sq5[:, :, :, x, :], qa5[:, :, :, 1 - x, :], Sb, ALU.mult, ["at_qa", f"at_rt{s}"], ["at_sq"])
            self.tt(qb[:, :], qb[:, :], sq[:, :], ALU.add, ["at_qb", "at_sq"], ["at_qb"])
            for h in range(8):
                self.tr(self.psB[0:64, h * 128:(h + 1) * 128], qb[:, h * 64:(h + 1) * 64], ["at_qb", "ident"], [f"psB{h // 4}"])
            for g in range(2):
                self.tr(self.psC[0:64, g * 128:(g + 1) * 128], qb[:, 512 + g * 64:576 + g * 64], ["at_qb", "ident"], ["psC0"])
            P.act(lambda e, o=QT[:, t, :, :], i=self.psB[0:64, :].rearrange("p (h q) -> p h q", q=128): e.copy(o, i),
                  ["psB0", "psB1"], [("at_QT", t)])
            P.dve(lambda e, o=KT[:, :, t * 128:(t + 1) * 128], i=self.psC[0:64, 0:256].rearrange("p (g q) -> p g q", q=128):
                  e.tensor_copy(o, i), ["psC0"], [("at_KT", t)])
            P.act(lambda e, o=V1[:, t, :, 0:64], i=zq[s][:, 640:768].rearrange("p (g d) -> p g d", d=64): e.copy(o, i),
                  [f"at_zq{s}", "at_V1"], [("at_V1", t)])
        pt = [self.sb(f"at_pt{s}", [128, 512]) for s in range(3)]
        ot = [self.sb(f"at_ot{s}", [128, 512]) for s in range(2)]
        rc = self.sb("at_rc", [128, 8])
        accs = [(self.psB, 0, "psB0"), (self.psB, 512, "psB1"), (self.psC, 0, "psC0"), (self.psC, 512, "psC1")]
        n = 0
        for t in range(NT):
            kcs = [0, 1] if t < 2 else list(range(NT))
            so = t % 2
            for g in range(2):
                for ki, kc in enumerate(kcs):
                    b = self.psA_i % 4
                    self.psA_i += 1
                    sp_ = n % 3
                    n += 1
                    self.mm(self.psA[:, b * 512:(b + 1) * 512], KT[:, g, kc * 128:(kc + 1) * 128],
                            QT[:, t, g * 4:(g + 1) * 4, :], True, True, [("at_KT", kc), ("at_QT", t)], [f"psA{b}"])
                    self.actf(pt[sp_][:, :], self.psA[:, b * 512:(b + 1) * 512], AF.Exp, [f"psA{b}"], [f"at_pt{sp_}"], scale=0.125)
                    for h in range(4):
                        pst, off, key = accs[h]
                        self.mm(pst[:, off:off + 65], pt[sp_][:, h * 128:(h + 1) * 128], V1[:, kc, g, :],
                                ki == 0, ki == len(kcs) - 1, [f"at_pt{sp_}", ("at_V1", kc)], [key])
                for h in range(4):
                    pst, off, key = accs[h]
                    hh = g * 4 + h
                    P.dve(lambda e, o=rc[:, hh:hh + 1], i=pst[:, off + 64:off + 65]: e.reciprocal(o, i), [key], ["at_rc"])
                    self.ts(ot[so][:, hh * 64:(hh + 1) * 64], pst[:, off:off + 64], rc[:, hh:hh + 1], None,
                            ALU.mult, None, [key, "at_rc"], [f"at_ot{so}"])
            self.ld(self.MIX0[t * 128:(t + 1) * 128, 0:512], ot[so][:, :], [f"at_ot{so}"],
                    [("MIXa", t)] + self.dk("MIX0"), q="pool")

    @staticmethod
    def tmap(d, j):
        if d == 0:
            return j
        return 1 - j if j < 2 else 19 - j

    def bc(self, src_row, n):
        return src_row.to_broadcast([128, n])

    def stage_rw_prep(self):
        P = self.P
        ZO = 768
        Z = self.Z0
        NS = self.NS
        mu_b = self.sb("rw_mub", [128, 1792])
        kkb = self.sb("rw_kkb", [128, 512])
        kab = self.sb("rw_kab", [128, 512])
        rkb = self.sb("rw_rkb", [128, 512])
        g2 = self.sb("rw_g2s", [128, 512])
        W2 = [self.sb(f"rw_W2{d}", [65, 512]) for d in range(2)]
        A2 = [self.sb(f"rw_A2{d}", [65, 512]) for d in range(2)]
        self.ld(mu_b[:, :], self.bc(self.inp["rw_mu"][0:1, :], 1792), [], ["rwc"])
        self.ld(kkb[:, :], self.bc(self.inp["rw_k_k"][0:1, :], 512), [], ["rwc"], q="pool")
        self.ld(kab[:, :], self.bc(self.inp["rw_k_a"][0:1, :], 512), [], ["rwc"])
        self.ld(rkb[:, :], self.bc(self.inp["rw_r_k"][0:1, :], 512), [], ["rwc"], q="pool")
        self.ld(g2[:, :], self.inp["rw_g2"][0], [], ["rwc"])
        for d in range(2):
            self.ld(W2[d][0:64, :], self.inp["rw_w2"][0, d], [], ["rwc"], q="pool")
            self.ld(W2[d][64:65, :], self.inp["rw_w0"][0, d:d + 1, :], [], ["rwc"])
            self.ld(A2[d][0:64, :], self.inp["rw_a2"][0, d], [], ["rwc"], q="pool")
            self.ld(A2[d][64:65, :], self.inp["rw_a0"][0, d:d + 1, :], [], ["rwc"])
        wtT = self.sb("rw_wtT", [65, 128])
        alT = self.sb("rw_alT", [65, 128])
        sgT = self.sb("rw_sgT", [128, 128])
        P.dve(lambda e: e.memset(wtT[:, :], 1.0), [], ["rw_wtT"])
        P.dve(lambda e: e.memset(alT[:, :], 1.0), [], ["rw_alT"])
        zt = [self.sb(f"rw_zt{s}", [128, 1792]) for s in range(2)]
        zm = [self.sb(f"rw_zm{s}", [128, 1792]) for s in range(2)]
        zp = [self.sb(f"rw_zp{s}", [128, 1792]) for s in range(2)]
        zf = self.sb("rw_zf", [128, 1792])
        seqt = [self.sb(f"rw_seq{s}", [128, NS]) for s in range(2)]
        at = self.sb("rw_at", [128, 512])
        t1 = self.sb("rw_t1", [128, 512])
        t2 = self.sb("rw_t2", [128, 512])
        ss = self.sb("rw_ss", [128, 8])
        gt = [self.sb(f"rw_gt{s}", [128, 512]) for s in range(2)]
        n = 0
        for d in range(2):
            for j in range(NT):
                T = self.tmap(d, j)
                s = n % 2
                n += 1
                r0 = T * 128
                self.ld(zt[s][:, :], Z[r0:r0 + 128, ZO:ZO + 1792], [], [f"rw_zt{s}"])
                P.pool(lambda e, o=zm[s][:, :]: e.memset(o, 0.0), [], [f"rw_zm{s}"])
                P.pool(lambda e, o=zp[s][:, :]: e.memset(o, 0.0), [], [f"rw_zp{s}"])
                if T in (0, 2):
                    self.ld(zm[s][1:128, :], Z[r0:r0 + 127, ZO:ZO + 1792], [], [f"rw_zm{s}"], q="pool")
                else:
                    self.ld(zm[s][:, :], Z[r0 - 1:r0 + 127, ZO:ZO + 1792], [], [f"rw_zm{s}"], q="pool")
                if T in (1, NT - 1):
                    self.ld(zp[s][0:127, :], Z[r0 + 1:r0 + 128, ZO:ZO + 1792], [], [f"rw_zp{s}"])
                else:
                    self.ld(zp[s][:, :], Z[r0 + 1:r0 + 129, ZO:ZO + 1792], [], [f"rw_zp{s}"])
                self.tt(zm[s][:, :], zm[s][:, :], zp[s][:, :], ALU.add, [f"rw_zm{s}", f"rw_zp{s}"], [f"rw_zm{s}"])
                self.stt(zm[s][:, :], zm[s][:, :], 0.5, zt[s][:, :], ALU.mult, ALU.subtract, [f"rw_zm{s}", f"rw_zt{s}"], [f"rw_zm{s}"])
                self.tt(zm[s][:, :], zm[s][:, :], mu_b[:, :], ALU.mult, [f"rw_zm{s}", "rwc"], [f"rw_zm{s}"])
                self.tt(zt[s][:, :], zt[s][:, :], zm[s][:, :], ALU.add, [f"rw_zm{s}", f"rw_zt{s}"], [f"rw_zt{s}"])
                zz, zk = zt[s], f"rw_zt{s}"
                if d == 1:
                    for q in range(4):
                        w = 512 if q < 3 else 256
                        b = self.psA_i % 4
                        self.psA_i += 1
                        self.mm(self.psA[:, b * 512:b * 512 + w], self.flipt[:, :], zt[s][:, q * 512:q * 512 + w], True, True,
                                [zk, "flip"], [f"psA{b}"])
                        if q % 2 == 0:
                            P.act(lambda e, o=zf[:, q * 512:q * 512 + w], i=self.psA[:, b * 512:b * 512 + w]: e.copy(o, i),
                                  [f"psA{b}"], ["rw_zf"])
                        else:
                            P.dve(lambda e, o=zf[:, q * 512:q * 512 + w], i=self.psA[:, b * 512:b * 512 + w]: e.tensor_copy(o, i),
                                  [f"psA{b}"], ["rw_zf"])
                    zz, zk = zf, "rw_zf"
                r_ = zz[:, 0:512]
                k_ = zz[:, 512:1024]
                v_ = zz[:, 1024:1536]
                sq = seqt[s]
                sk = f"rw_seq{s}"
                self.tr(self.psB[0:64, 0:128], zz[:, 1536:1600], [zk, "ident"], ["psB0"])
                self.tr(self.psB[0:64, 128:256], zz[:, 1600:1664], [zk, "ident"], ["psB0"])
                self.tr(self.psB[:, 256:384], zz[:, 1664:1792], [zk, "ident"], ["psB0"])
                self.actf(wtT[0:64, :], self.psB[0:64, 0:128], AF.Tanh, ["psB0"], ["rw_wtT"])
                P.dve(lambda e, o=alT[0:64, :], i=self.psB[0:64, 128:256]: e.tensor_copy(o, i), ["psB0"], ["rw_alT"])
                self.actf(sgT[:, :], self.psB[:, 256:384], AF.Sigmoid, ["psB0"], ["rw_sgT"])
                b1 = self.psA_i % 4
                b2 = (self.psA_i + 1) % 4
                b3 = (self.psA_i + 2) % 4
                self.psA_i += 3
                self.mm(self.psA[:, b1 * 512:(b1 + 1) * 512], wtT[:, :], W2[d][:, :], True, True, ["rw_wtT", "rwc"], [f"psA{b1}"])
                self.mm(self.psA[:, b2 * 512:(b2 + 1) * 512], alT[:, :], A2[d][:, :], True, True, ["rw_alT", "rwc"], [f"psA{b2}"])
                self.actf(t1[:, :], self.psA[:, b1 * 512:(b1 + 1) * 512], AF.Sigmoid, [f"psA{b1}"], ["rw_t1"])
                self.actf(sq[:, 1024:1536], t1[:, :], AF.Exp, ["rw_t1"], [sk], scale=-math.exp(-0.5))
                self.actf(at[:, :], self.psA[:, b2 * 512:(b2 + 1) * 512], AF.Sigmoid, [f"psA{b2}"], ["rw_at"])
                if d == 0:
                    sg = j % 2
                    self.mm(self.psA[:, b3 * 512:(b3 + 1) * 512], sgT[:, :], g2[:, :], True, True, ["rw_sgT", "rwc"], [f"psA{b3}"])
                    P.act(lambda e, o=gt[sg][:, :], i=self.psA[:, b3 * 512:(b3 + 1) * 512]: e.copy(o, i), [f"psA{b3}"], [f"rw_gt{sg}"])
                    self.ld(self.G[r0:r0 + 128, :], gt[sg][:, :], [f"rw_gt{sg}"], [("G", T)], q="pool")
                self.tt(t1[:, :], k_, kkb[:, :], ALU.mult, [zk, "rwc", "rw_t1"], ["rw_t1"])
                self.tt(t2[:, :], t1[:, :], t1[:, :], ALU.mult, ["rw_t1"], ["rw_t2"])
                P.dve(lambda e, o=ss[:, :], i=t2[:, :].rearrange("p (h d) -> p h d", d=64): e.tensor_reduce(o, i, AX.X, ALU.add),
                      ["rw_t2"], ["rw_ss"])
                self.actf(ss[:, :], ss[:, :], AF.Sqrt, ["rw_ss"], ["rw_ss"])
                self.ts(ss[:, :], ss[:, :], 1e-12, None, ALU.max, None, ["rw_ss"], ["rw_ss"])
                P.dve(lambda e, o=ss[:, :]: e.reciprocal(o, o), ["rw_ss"], ["rw_ss"])
                self.tt(sq[:, 0:512].rearrange("p (h d) -> p h d", d=64), t1[:, :].rearrange("p (h d) -> p h d", d=64),
                        ss[:, :].unsqueeze(2).to_broadcast([128, 8, 64]), ALU.mult, ["rw_t1", "rw_ss"], [sk])
                self.tt(sq[:, 512:1024], sq[:, 1024:1536], r_, ALU.mult, [sk, zk], [sk])
                self.tt(sq[:, 1536:2048], sq[:, 0:512], at[:, :], ALU.mult, [sk, "rw_at"], [sk])
                self.stt(t2[:, :], at[:, :], -1.0, kab[:, :], ALU.add, ALU.mult, ["rw_at", "rwc", "rw_t2"], ["rw_t2"])
                self.stt(sq[:, 2048:2560], t2[:, :], 1.0, k_, ALU.add, ALU.mult, ["rw_t2", zk], [sk])
                P.act(lambda e, o=sq[:, 2560:3072], i=v_: e.copy(o, i), [zk], [sk])
                self.tt(t1[:, :], sq[:, 1536:2048], r_, ALU.mult, [sk, zk, "rw_t1"], ["rw_t1"])
                P.dve(lambda e, o=sq[:, 3072:3080], i=t1[:, :].rearrange("p (h d) -> p h d", d=64): e.tensor_reduce(o, i, AX.X, ALU.add),
                      ["rw_t1"], [sk])
                self.tt(t1[:, :], sq[:, 2048:2560], r_, ALU.mult, [sk, zk, "rw_t1"], ["rw_t1"])
                P.dve(lambda e, o=sq[:, 3080:3088], i=t1[:, :].rearrange("p (h d) -> p h d", d=64): e.tensor_reduce(o, i, AX.X, ALU.add),
                      ["rw_t1"], [sk])
                self.tt(t1[:, :], t1[:, :], rkb[:, :], ALU.mult, ["rw_t1", "rwc"], ["rw_t1"])
                P.dve(lambda e, o=sq[:, 3088:3096], i=t1[:, :].rearrange("p (h d) -> p h d", d=64): e.tensor_reduce(o, i, AX.X, ALU.add),
                      ["rw_t1"], [sk])
                self.ld(self.SEQ[d][j * 128:(j + 1) * 128, :], sq[:, :], [sk], [("SEQ", d, j)] + self.dk("SEQ%d" % d), q="pool")

    def stage_rw_scan(self):
        P = self.P
        NS = self.NS
        S = self.sb("sc_S", [128, 512])
        prod = self.sb("sc_prod", [128, 1024])
        tmp = self.sb("sc_tmp", [128, 512])
        hist = [self.sb(f"sc_hist{s}", [128, 128, 16]) for s in range(2)]
        CS = 2
        NB = 4
        Bt = [self.sb(f"sc_B{s}", [128, CS, 2560]) for s in range(NB)]
        VK = [self.sb(f"sc_VK{s}", [128, CS, 512]) for s in range(NB)]
        vcat = self.sb("sc_vcat", [128, 8, 2, 64])
        vT = self.sb("sc_vT", [128, 8, 128])
        HT = self.sb("sc_HT", [128, 16, 128])
        P.dve(lambda e: e.memset(S[:, :], 0.0), [], ["sc_S"])
        nchunk = 0
        for jb in range(NT):
            hb = jb % 2
            H = hist[hb]
            hk = f"sc_hist{hb}"
            for d in range(2):
                self.ld(vcat[:, :, d, :], self.SEQ[d][jb * 128:(jb + 1) * 128, 2560:3072].rearrange("p (h v) -> p h v", v=64),
                        [], ["sc_vcat"], q=("sp" if d == 0 else "pool"))
            for h in range(8):
                self.tr(self.psB[:, h * 128:(h + 1) * 128], vcat[:, h, :, :].rearrange("p d v -> p (d v)"),
                        ["sc_vcat", "ident"], [f"psB{h // 4}"])
            for hh in range(2):
                P.act(lambda e, o=vT[:, hh * 4:(hh + 1) * 4, :], i=self.psB[:, hh * 512:(hh + 1) * 512].rearrange("p (h q) -> p h q", q=128):
                      e.copy(o, i), [f"psB{hh}"], ["sc_vT"])
            for c4 in range(128 // CS):
                sl = nchunk % NB
                nchunk += 1
                pos = jb * 128 + c4 * CS
                for d in range(2):
                    src = self.SEQ[d][pos:pos + CS, 0:2560]
                    srcb = bass.AP(src.tensor, src.offset, [[0, 64], [NS, CS], [1, 2560]])
                    self.ld(Bt[sl][d * 64:(d + 1) * 64, :, :], srcb, [], [("sc_B", sl)], q=("sp" if d == 0 else "pool"))
                s0 = c4 * CS
                self.tt(VK[sl][:, :, :].rearrange("p s (h k) -> p s h k", k=64),
                        vT[:, :, s0:s0 + CS].rearrange("p h s -> p s h").unsqueeze(3).to_broadcast([128, CS, 8, 64]),
                        Bt[sl][:, :, 2048:2560].rearrange("p s (h k) -> p s h k", k=64), ALU.mult,
                        ["sc_vT", ("sc_B", sl)], [("sc_VK", sl)])
                for k in range(CS):
                    st = s0 + k
                    Bk = Bt[sl][:, k, :]
                    self.tt(prod[:, :].rearrange("p (a n) -> p a n", a=2), S[:, :].unsqueeze(1).to_broadcast([128, 2, 512]),
                            Bk[:, 0:1024].rearrange("p (a n) -> p a n", a=2), ALU.mult, ["sc_S", ("sc_B", sl)], ["sc_prod"])
                    P.dve(lambda e, o=H[:, st, :], i=prod[:, :].rearrange("p (a k) -> p a k", k=64): e.tensor_reduce(o, i, AX.X, ALU.add),
                          ["sc_prod"], [hk])
                    self.tt(S[:, :], S[:, :], Bk[:, 1024:1536], ALU.mult, ["sc_S", ("sc_B", sl)], ["sc_S"])
                    self.tt(tmp[:, :].rearrange("p (h k) -> p h k", k=64), H[:, st, 0:8].unsqueeze(2).to_broadcast([128, 8, 64]),
                            Bk[:, 1536:2048].rearrange("p (h k) -> p h k", k=64), ALU.mult, [hk, ("sc_B", sl)], ["sc_tmp"])
                    self.tt(S[:, :], S[:, :], tmp[:, :], ALU.subtract, ["sc_S", "sc_tmp"], ["sc_S"])
                    self.tt(S[:, :], S[:, :], VK[sl][:, k, :], ALU.add, ["sc_S", ("sc_VK", sl)], ["sc_S"])
            for q in range(16):
                self.tr(self.psA[:, q * 128:(q + 1) * 128], H[:, :, q], [hk, "ident"], [f"psA{q // 4}"])
            for b in range(4):
                src = self.psA[:, b * 512:(b + 1) * 512].rearrange("p (q m) -> p q m", m=128)
                if b % 2 == 0:
                    P.act(lambda e, o=HT[:, b * 4:(b + 1) * 4, :], i=src: e.copy(o, i), [f"psA{b}"], ["sc_HT"])
                else:
                    P.dve(lambda e, o=HT[:, b * 4:(b + 1) * 4, :], i=src: e.tensor_copy(o, i), [f"psA{b}"], ["sc_HT"])
            for d in range(2):
                self.ld(self.HIST[d][jb * 128:(jb + 1) * 128, :].rearrange("p (q v) -> p q v", v=64),
                        HT[:, :, d * 64:(d + 1) * 64], ["sc_HT"], [("HIST", d, jb)] + self.dk("HIST%d" % d), q="pool")

    def stage_rw_post(self):
        P = self.P
        gwb = self.sb("rp_gw", [128, 512])
        gbb = self.sb("rp_gb", [128, 512])
        self.ld(gwb[:, :], self.bc(self.inp["rw_gn_w"][0:1, :], 512), [], ["rpc"])
        self.ld(gbb[:, :], self.bc(self.inp["rw_gn_b"][0:1, :], 512), [], ["rpc"], q="pool")
        ht = [self.sb(f"rp_ht{s}", [128, 16, 64]) for s in range(2)]
        sv = [self.sb(f"rp_sv{s}", [128, 536]) for s in range(2)]
        o = self.sb("rp_o", [128, 8, 64])
        t = self.sb("rp_t", [128, 8, 64])
        m = self.sb("rp_m", [128, 8])
        yt = [self.sb(f"rp_y{s}", [128, 512]) for s in range(2)]
        n = 0

        def b8(ap):
            return ap.unsqueeze(2).to_broadcast([128, 8, 64])

        for d in range(2):
            for j in range(NT):
                s = n % 2
                n += 1
                hk, vk, yk = f"rp_ht{s}", f"rp_sv{s}", f"rp_y{s}"
                self.ld(ht[s][:, :, :], self.HIST[d][j * 128:(j + 1) * 128, :].rearrange("p (q v) -> p q v", v=64), [], [hk])
                self.ld(sv[s][:, :], self.SEQ[d][j * 128:(j + 1) * 128, 2560:3096], [], [vk], q="pool")
                v3 = sv[s][:, 0:512].rearrange("p (h v) -> p h v", v=64)
                c1, c2, bs = sv[s][:, 512:520], sv[s][:, 520:528], sv[s][:, 528:536]
                y3 = yt[s][:, :].rearrange("p (h v) -> p h v", v=64)
                self.tt(t[:, :, :], ht[s][:, 0:8, :], b8(c1), ALU.mult, [hk, vk], ["rp_t"])
                self.tt(o[:, :, :], ht[s][:, 8:16, :], t[:, :, :], ALU.subtract, [hk, "rp_t"], ["rp_o"])
                self.tt(t[:, :, :], v3, b8(c2), ALU.mult, [vk, "rp_t"], ["rp_t"])
                self.tt(o[:, :, :], o[:, :, :], t[:, :, :], ALU.add, ["rp_o", "rp_t"], ["rp_o"])
                P.dve(lambda e, o_=m[:, :], i=o[:, :, :]: e.tensor_reduce(o_, i, AX.X, ALU.add), ["rp_o"], ["rp_m"])
                self.ts(m[:, :], m[:, :], 1.0 / 64, None, ALU.mult, None, ["rp_m"], ["rp_m"])
                self.tt(o[:, :, :], o[:, :, :], b8(m[:, :]), ALU.subtract, ["rp_o", "rp_m"], ["rp_o"])
                self.tt(t[:, :, :], o[:, :, :], o[:, :, :], ALU.mult, ["rp_o", "rp_t"], ["rp_t"])
                P.dve(lambda e, o_=m[:, :], i=t[:, :, :]: e.tensor_reduce(o_, i, AX.X, ALU.add), ["rp_t", "rp_m"], ["rp_m"])
                self.rsqrt(m[:, :], m[:, :], 1.0 / 64, 64e-5, ["rp_m"], ["rp_m"])
                self.tt(o[:, :, :], o[:, :, :], b8(m[:, :]), ALU.mult, ["rp_o", "rp_m"], ["rp_o"])
                self.tt(o[:, :, :], o[:, :, :], gwb[:, :].rearrange("p (h v) -> p h v", v=64), ALU.mult, ["rp_o", "rpc"], ["rp_o"])
                self.tt(o[:, :, :], o[:, :, :], gbb[:, :].rearrange("p (h v) -> p h v", v=64), ALU.add, ["rp_o", "rpc"], ["rp_o"])
                self.tt(t[:, :, :], v3, b8(bs), ALU.mult, [vk, "rp_t"], ["rp_t"])
                self.tt(y3, o[:, :, :], t[:, :, :], ALU.add, ["rp_o", "rp_t"], [yk])
                self.ld(self.Y[d][j * 128:(j + 1) * 128, :], yt[s][:, :], [yk], [("Y", d, j)] + self.dk("Y%d" % d), q="pool")

    def stage_rw_comb(self):
        P = self.P
        yf = [self.sb(f"rc_f{s}", [128, 512]) for s in range(2)]
        yb = [self.sb(f"rc_b{s}", [128, 512]) for s in range(2)]
        gg = [self.sb(f"rc_g{s}", [128, 512]) for s in range(2)]
        for T in range(NT):
            s = T % 2
            jb = self.tmap(1, T)
            self.ld(yf[s][:, :], self.Y[0][T * 128:(T + 1) * 128, :], [], [f"rc_f{s}"])
            self.ld(yb[s][:, :], self.Y[1][jb * 128:(jb + 1) * 128, :], [], [f"rc_b{s}"], q="pool")
            self.ld(gg[s][:, :], self.G[T * 128:(T + 1) * 128, :], [], [f"rc_g{s}"])
            b = self.psA_i % 4
            self.psA_i += 1
            self.mm(self.psA[:, b * 512:(b + 1) * 512], self.flipt[:, :], yb[s][:, :], True, True, [f"rc_b{s}", "flip"], [f"psA{b}"])
            self.tt(yf[s][:, :], yf[s][:, :], self.psA[:, b * 512:(b + 1) * 512], ALU.add, [f"rc_f{s}", f"psA{b}"], [f"rc_f{s}"])
            self.tt(yf[s][:, :], yf[s][:, :], gg[s][:, :], ALU.mult, [f"rc_f{s}", f"rc_g{s}"], [f"rc_f{s}"])
            self.ld(self.MIX0[T * 128:(T + 1) * 128, 512:1024], yf[s][:, :], [f"rc_f{s}"], [("MIXb", T)] + self.dk("MIX0"), q="pool")

    def stage_peer_route(self, i, t0):
        P = self.P
        wq = self.sb(f"pr_wq{i}", [128, 8, 2048])
        Wv = self.inp["peer_w_q"][i].rearrange("(p c) n -> p c n", c=8)
        for c in range(8):
            self.ld(wq[:, c, :], Wv[:, c, :], [], ["pr_wq"], q=("sp" if c % 2 == 0 else "pool"))
        kraw = self.sb(f"pr_kraw{i}", [128, 16, 128])
        keysT = self.sb(f"pr_kT{i}", [128, 16, 128])
        self.ld(kraw[:, :, :], self.inp["peer_keys"][i].rearrange("h p n d -> n (h p) d"), [], ["pr_kraw"])
        for hp in range(16):
            pst, key = (self.psB, f"psB{(hp % 8) // 4}")
            self.tr(pst[:, (hp % 8) * 128:(hp % 8 + 1) * 128], kraw[:, hp, :], ["pr_kraw", "ident"], [key])
            if hp % 8 == 7:
                P.act(lambda e, o=keysT[:, hp - 7:hp + 1, :], a=pst[:, :].rearrange("p (c q) -> p c q", q=128): e.copy(o, a),
                      ["psB0", "psB1"], ["pr_kT"])
        xt = [self.sb(f"pr_x{i}{s}", [128, D]) for s in range(2)]
        hT = self.sb(f"pr_hT{i}", [128, 8, 128])
        h2 = [self.sb(f"pr_h2{i}{s}", [128, D]) for s in range(2)]
        qT = self.sb(f"pr_qT{i}", [128, 16, 128])
        sc = self.sb(f"pr_sc{i}", [128, 16, 128])
        sc2 = self.sb(f"pr_sc2{i}", [128, 16, 128])
        sv = self.sb(f"pr_sv{i}", [128, 16, 16])
        si = self.sb(f"pr_si{i}", [128, 16, 16], U32)
        sif = self.sb(f"pr_sif{i}", [128, 16, 16])
        cs = self.sb(f"pr_cs{i}", [128, 8, 256])
        cs2 = self.sb(f"pr_cs2{i}", [128, 8, 256])
        ci = self.sb(f"pr_ci{i}", [128, 8, 256])
        junk = self.sb(f"pr_junk{i}", [128, 256])
        bs = self.sb(f"pr_bs{i}", [128, 8, 16])
        idxf = [self.sb(f"pr_idxf{i}{s}", [128, 128]) for s in range(2)]
        idxu = [self.sb(f"pr_idxu{i}{s}", [128, 128], U32) for s in range(2)]
        gt = [self.sb(f"pr_gt{i}{s}", [128, 8, 16]) for s in range(2)]
        gs = self.sb(f"pr_gs{i}", [128, 8])
        pos = self.sb(f"pr_pos{i}", [128, 8, 16], U32)
        posf = self.sb(f"pr_posf{i}", [128, 8, 16])
        iota = self.sb(f"pr_iota{i}", [128, 256])
        self.ld(iota[:, :], self.inp["k_iota"], [], ["pr_iota"])
        for t in range(t0, NT):
            s = t % 2
            who = 1 if t < 2 else 0
            row = (t - t0) * 128
            self.ld(xt[s][:, :], self.xsrc(t), [("XR", t)], [f"pr_x{s}"])
            self.norm_mod_T(xt[s][:, :], f"pr_x{s}", 1, who, hT, "pr_hT")
            for c in range(8):
                self.tr(self.psC[:, c * 128:(c + 1) * 128], hT[:, c, :], ["pr_hT", "ident"], [f"psC{c // 4}"])
            P.act(lambda e, o=h2[s][:, :].rearrange("t (p c) -> t c p", c=8), a=self.psC[:, :].rearrange("t (c p) -> t c p", p=128):
                  e.copy(o, a), ["psC0", "psC1"], [f"pr_h2{s}"])
            self.ld(self.H2[row:row + 128, :], h2[s][:, :], [f"pr_h2{s}"], [("H2", t)] + self.dk("H2"), q="pool")
            for hp in range(16):
                b = hp // 4
                for c in range(8):
                    self.mm(self.psA[:, hp * 128:(hp + 1) * 128], wq[:, c, hp * 128:(hp + 1) * 128], hT[:, c, :],
                            c == 0, c == 7, ["pr_wq", "pr_hT"], [f"psA{b}"])
            for b in range(4):
                P.act(lambda e, o=qT[:, b * 4:(b + 1) * 4, :], a=self.psA[:, b * 512:(b + 1) * 512].rearrange("p (c q) -> p c q", q=128):
                      e.copy(o, a), [f"psA{b}"], ["pr_qT"])
            for hp in range(16):
                pst, key = (self.psB, f"psB{(hp % 8) // 4}") if hp < 8 else (self.psC, f"psC{(hp % 8) // 4}")
                self.mm(pst[:, (hp % 8) * 128:(hp % 8 + 1) * 128], qT[:, hp, :], keysT[:, hp, :], True, True,
                        ["pr_qT", "pr_kT"], [key])
            for q in range(4):
                pst, key = (self.psB, f"psB{q % 2}") if q < 2 else (self.psC, f"psC{q % 2}")
                P.act(lambda e, o=sc[:, q * 4:(q + 1) * 4, :], a=pst[:, (q % 2) * 512:(q % 2 + 1) * 512].rearrange("p (c q) -> p c q", q=128):
                      e.copy(o, a), [key], ["pr_sc"])
            for hp in range(16):
                P.dve(lambda e, o=sv[:, hp, 0:8], a=sc[:, hp, :]: e.max(o, a), ["pr_sc"], ["pr_sv"])
                P.dve(lambda e, o=si[:, hp, 0:8], m=sv[:, hp, 0:8], a=sc[:, hp, :]: e.max_index(o, m, a), ["pr_sc", "pr_sv"], ["pr_si"])
                P.dve(lambda e, o=sc2[:, hp, :], m=sv[:, hp, 0:8], a=sc[:, hp, :]: e.match_replace(o, m, a, -1e30),
                      ["pr_sc", "pr_sv"], ["pr_sc2"])
                P.dve(lambda e, o=sv[:, hp, 8:16], a=sc2[:, hp, :]: e.max(o, a), ["pr_sc2"], ["pr_sv"])
                P.dve(lambda e, o=si[:, hp, 8:16], m=sv[:, hp, 8:16], a=sc2[:, hp, :]: e.max_index(o, m, a), ["pr_sc2", "pr_sv"], ["pr_si"])
            P.dve(lambda e: e.tensor_copy(sif[:, :, :], si[:, :, :]), ["pr_si"], ["pr_sif"])
            sv4 = sv[:, :, :].rearrange("p (h a) k -> p h a k", a=2)
            sf4 = sif[:, :, :].rearrange("p (h a) k -> p h a k", a=2)
            cs4 = cs[:, :, :].rearrange("p h (a b) -> p h a b", b=16)
            ci4 = ci[:, :, :].rearrange("p h (a b) -> p h a b", b=16)
            self.tt(cs4, sv4[:, :, 0, :].unsqueeze(3).to_broadcast([128, 8, 16, 16]),
                    sv4[:, :, 1, :].unsqueeze(2).to_broadcast([128, 8, 16, 16]), ALU.add, ["pr_sv"], ["pr_cs"])
            self.ts(sf4[:, :, 0, :], sf4[:, :, 0, :], 128.0, None, ALU.mult, None, ["pr_sif"], ["pr_sif"])
            self.tt(ci4, sf4[:, :, 0, :].unsqueeze(3).to_broadcast([128, 8, 16, 16]),
                    sf4[:, :, 1, :].unsqueeze(2).to_broadcast([128, 8, 16, 16]), ALU.add, ["pr_sif"], ["pr_ci"])
            for h in range(8):
                P.dve(lambda e, o=bs[:, h, 0:8], a=cs[:, h, :]: e.max(o, a), ["pr_cs"], ["pr_bs"])
                P.dve(lambda e, o=cs2[:, h, :], m=bs[:, h, 0:8], a=cs[:, h, :]: e.match_replace(o, m, a, -1e30),
                      ["pr_cs", "pr_bs"], ["pr_cs2"])
                P.dve(lambda e, o=bs[:, h, 8:16], a=cs2[:, h, :]: e.max(o, a), ["pr_cs2"], ["pr_bs"])
            for h in range(8):
                P.dve(lambda e, o=pos[:, h, 0:8], m=bs[:, h, 0:8], a=cs[:, h, :]: e.max_index(o, m, a), ["pr_cs", "pr_bs"], ["pr_pos"])
                P.dve(lambda e, o=pos[:, h, 8:16], m=bs[:, h, 8:16], a=cs2[:, h, :]: e.max_index(o, m, a), ["pr_cs2", "pr_bs"], ["pr_pos"])
            P.dve(lambda e: e.tensor_copy(posf[:, :, :], pos[:, :, :]), ["pr_pos"], ["pr_posf"])
            for h in range(8):
                for k in range(16):
                    self.stt(junk[:, :], iota[:, :], posf[:, h, k:k + 1], ci[:, h, :], ALU.is_equal, ALU.mult,
                             ["pr_iota", "pr_posf", "pr_ci"], ["pr_junk", f"pr_idxf{s}"], accum=idxf[s][:, h * 16 + k:h * 16 + k + 1])
            if i > 0:
                self.ts(idxf[s][:, :], idxf[s][:, :], float(16384 * i), None, ALU.add, None, [f"pr_idxf{s}"], [f"pr_idxf{s}"])
            P.dve(lambda e, o=idxu[s][:, :], a=idxf[s][:, :]: e.tensor_copy(o, a), [f"pr_idxf{s}"], [f"pr_idxu{s}"])
            self.ld(self.IDX[row:row + 128, :], idxu[s][:, :], [f"pr_idxu{s}"], [("IDX", t)] + self.dk("IDX"), q="pool")
            self.tt(gt[s][:, :, :], bs[:, :, :], bs[:, :, 0:1].to_broadcast([128, 8, 16]), ALU.subtract, ["pr_bs"], [f"pr_gt{s}"])
            self.actf(gt[s][:, :, :], gt[s][:, :, :], AF.Exp, [f"pr_gt{s}"], [f"pr_gt{s}"])
            P.dve(lambda e, o=gs[:, :], a=gt[s][:, :, :]: e.tensor_reduce(o, a, AX.X, ALU.add), [f"pr_gt{s}"], ["pr_gs"])
            P.dve(lambda e: e.reciprocal(gs[:, :], gs[:, :]), ["pr_gs"], ["pr_gs"])
            self.tt(gt[s][:, :, :], gt[s][:, :, :], gs[:, :].unsqueeze(2).to_broadcast([128, 8, 16]), ALU.mult,
                    [f"pr_gt{s}", "pr_gs"], [f"pr_gt{s}"])
            self.ld(self.GATE[row:row + 128, :], gt[s][:, :, :].rearrange("p h k -> p (h k)"), [f"pr_gt{s}"],
                    [("GATE", t)] + self.dk("GATE"), q="pool")

    def stage_peer_expert(self, i, t0, final):
        P = self.P
        GS = 8
        Ug = [self.sb(f"pe_U{i}{s}", [128, GS, D]) for s in range(2)]
        Vg = [self.sb(f"pe_V{i}{s}", [128, GS, D]) for s in range(2)]
        h2 = [self.sb(f"pe_h2{i}{s}", [128, D]) for s in range(2)]
        xt = [self.sb(f"pe_x{i}{s}", [128, D]) for s in range(2)]
        idx = [self.sb(f"pe_idx{i}{s}", [128, 128], U32) for s in range(2)]
        gate = [self.sb(f"pe_gate{i}{s}", [128, 128]) for s in range(2)]
        act = self.sb(f"pe_act{i}", [128, 128])
        coef = self.sb(f"pe_coef{i}", [128, 128])
        junk = self.sb(f"pe_junk{i}", [128, D])
        acc = self.sb(f"pe_acc{i}", [128, D])
        U = self.inp["peer_u"].rearrange("l e d -> (l e) d")
        V = self.inp["peer_v"].rearrange("l e d -> (l e) d")
        ng = 0
        for t in range(t0, NT):
            s = t % 2
            who = 1 if t < 2 else 0
            row = (t - t0) * 128
            self.ld(h2[s][:, :], self.H2[row:row + 128, :], [], [f"pe_h2{s}"])
            self.ld(xt[s][:, :], self.xsrc(t), [("XR", t)], [f"pe_x{s}"])
            self.ld(idx[s][:, :], self.IDX[row:row + 128, :], [], [f"pe_idx{s}"])
            self.ld(gate[s][:, :], self.GATE[row:row + 128, :], [], [f"pe_gate{s}"])
            for g in range(128 // GS):
                gb = ng % 2
                ng += 1
                for k in range(GS):
                    sl = g * GS + k
                    P.dma(lambda e, o=Ug[gb][:, k, :], ix=idx[s][:, sl:sl + 1]: e.indirect_dma_start(
                        out=o, out_offset=None, in_=U, in_offset=bass.IndirectOffsetOnAxis(ap=ix, axis=0)),
                        [f"pe_idx{s}"], [("pe_U", gb, k)], q="pool")
                for k in range(GS):
                    sl = g * GS + k
                    P.dma(lambda e, o=Vg[gb][:, k, :], ix=idx[s][:, sl:sl + 1]: e.indirect_dma_start(
                        out=o, out_offset=None, in_=V, in_offset=bass.IndirectOffsetOnAxis(ap=ix, axis=0)),
                        [f"pe_idx{s}"], [("pe_V", gb, k)], q="pool")
                for k in range(GS):
                    sl = g * GS + k
                    self.stt(junk[:, :], Ug[gb][:, k, :], 1.0, h2[s][:, :], ALU.mult, ALU.mult,
                             [("pe_U", gb, k), f"pe_h2{s}"], ["pe_junk", "pe_act"], accum=act[:, sl:sl + 1])
                sl0 = g * GS
                self.actf(coef[:, sl0:sl0 + GS], act[:, sl0:sl0 + GS], AF.Gelu, ["pe_act"], ["pe_coef"])
                self.tt(coef[:, sl0:sl0 + GS], coef[:, sl0:sl0 + GS], gate[s][:, sl0:sl0 + GS], ALU.mult,
                        ["pe_coef", f"pe_gate{s}"], ["pe_coef"])
                for k in range(GS):
                    sl = g * GS + k
                    if sl == 0:
                        self.ts(acc[:, :], Vg[gb][:, k, :], coef[:, 0:1], None, ALU.mult, None, [("pe_V", gb, k), "pe_coef"], ["pe_acc"])
                    else:
                        self.stt(acc[:, :], Vg[gb][:, k, :], coef[:, sl:sl + 1], acc[:, :], ALU.mult, ALU.add,
                                 [("pe_V", gb, k), "pe_coef", "pe_acc"], ["pe_acc"])
            self.tt(acc[:, :], acc[:, :], self.gate[who][1][:, :], ALU.mult, ["pe_acc", f"gate{who}1"], ["pe_acc"])
            self.tt(xt[s][:, :], xt[s][:, :], acc[:, :], ALU.add, [f"pe_x{s}", "pe_acc"], [f"pe_x{s}"])
            if final:
                self.ld(self.out[(t - 2) * 128:(t - 1) * 128, :], xt[s][:, :], [f"pe_x{s}"], [("out", t)], q="sp")
            else:
                self.ld(self.XR[t * 128:(t + 1) * 128, :], xt[s][:, :], [f"pe_x{s}"], [("XR", t)] + self.dk("XR"), q="sp")

    def stage_ssd_conv(self):
        P = self.P
        Z = self.Z1
        W = 1536
        wb = [self.sb(f"sv_w{k}", [128, W]) for k in range(3)]
        bb = self.sb("sv_b", [128, W])
        x0 = [self.sb(f"sv_x0{s}", [128, W]) for s in range(2)]
        xm = [self.sb(f"sv_xm{s}", [128, W]) for s in range(2)]
        xp = [self.sb(f"sv_xp{s}", [128, W]) for s in range(2)]
        n = 0
        for hf in range(2):
            CO = 2048 + hf * W
            for k in range(3):
                self.ld(wb[k][:, :], self.bc(self.inp["ssd_conv_w"][0, k:k + 1, hf * W:(hf + 1) * W], W), [], ["svc"],
                        q=("sp" if k % 2 == 0 else "pool"))
            self.ld(bb[:, :], self.bc(self.inp["ssd_conv_b"][0:1, hf * W:(hf + 1) * W], W), [], ["svc"], q="pool")
            for T in range(NT):
                s = n % 2
                n += 1
                r0 = T * 128
                self.ld(x0[s][:, :], Z[r0:r0 + 128, CO:CO + W], [], [f"sv_x0{s}"])
                P.pool(lambda e, o=xm[s][:, :]: e.memset(o, 0.0), [], [f"sv_xm{s}"])
                P.pool(lambda e, o=xp[s][:, :]: e.memset(o, 0.0), [], [f"sv_xp{s}"])
                if T in (0, 2):
                    self.ld(xm[s][1:128, :], Z[r0:r0 + 127, CO:CO + W], [], [f"sv_xm{s}"], q="pool")
                else:
                    self.ld(xm[s][:, :], Z[r0 - 1:r0 + 127, CO:CO + W], [], [f"sv_xm{s}"], q="pool")
                if T in (1, NT - 1):
                    self.ld(xp[s][0:127, :], Z[r0 + 1:r0 + 128, CO:CO + W], [], [f"sv_xp{s}"])
                else:
                    self.ld(xp[s][:, :], Z[r0 + 1:r0 + 129, CO:CO + W], [], [f"sv_xp{s}"])
                self.tt(xm[s][:, :], xm[s][:, :], wb[0][:, :], ALU.mult, [f"sv_xm{s}", "svc"], [f"sv_xm{s}"])
                self.tt(xp[s][:, :], xp[s][:, :], wb[2][:, :], ALU.mult, [f"sv_xp{s}", "svc"], [f"sv_xp{s}"])
                self.tt(x0[s][:, :], x0[s][:, :], wb[1][:, :], ALU.mult, [f"sv_x0{s}", "svc"], [f"sv_x0{s}"])
                self.tt(xm[s][:, :], xm[s][:, :], xp[s][:, :], ALU.add, [f"sv_xm{s}", f"sv_xp{s}"], [f"sv_xm{s}"])
                self.tt(x0[s][:, :], x0[s][:, :], xm[s][:, :], ALU.add, [f"sv_x0{s}", f"sv_xm{s}"], [f"sv_x0{s}"])
                self.tt(x0[s][:, :], x0[s][:, :], bb[:, :], ALU.add, [f"sv_x0{s}", "svc"], [f"sv_x0{s}"])
                self.actf(x0[s][:, :], x0[s][:, :], AF.Silu, [f"sv_x0{s}"], [f"sv_x0{s}"])
                self.ld(self.XBC[r0:r0 + 128, hf * W:(hf + 1) * W], x0[s][:, :], [f"sv_x0{s}"],
                        [("XBC", T, hf)] + self.dk("XBC"), q="pool")

    def stage_ssd_scan(self):
        P = self.P
        tri = self.sb("ss_tri", [128, 128])
        neg4 = self.sb("ss_neg4", [128, 512])
        ones = self.sb("ss_ones", [128, 128])
        self.ld(tri[:, :], self.inp["k_tri"], [], ["ssc"])
        self.ld(neg4[:, :], self.inp["k_neg4"], [], ["ssc"], q="pool")
        P.dve(lambda e: e.memset(ones[:, :], 1.0), [], ["ss_ones"])
        dtb = [self.sb(f"ss_dtb{d}", [128, 32]) for d in range(2)]
        ab = [self.sb(f"ss_ab{d}", [128, 32]) for d in range(2)]
        for d in range(2):
            self.ld(dtb[d][:, :], self.bc(self.inp["ssd_dt_bias"][0, d:d + 1, :], 32), [], ["ssc"])
            self.ld(ab[d][:, :], self.bc(self.inp["ssd_a_log"][0, d:d + 1, :], 32), [], ["ssc2"], q="pool")
            self.actf(ab[d][:, :], ab[d][:, :], AF.Exp, ["ssc2"], ["ssc2"])
        xbc = [self.sb(f"ss_xbc{s}", [128, 3072]) for s in range(2)]
        xf = self.sb("ss_xf", [128, 3072])
        dtrt = [self.sb(f"ss_dtr{s}", [128, 512]) for s in range(2)]
        dtr = [x[:, 0:32] for x in dtrt]
        dt = self.sb("ss_dt", [128, 32])
        dat = self.sb("ss_da", [128, 512])
        tot = self.sb("ss_tot", [128, 32])
        da = dat[:, 0:32]
        P.dve(lambda e: e.memset(dat[:, :], 0.0), [], ["ss_da"])
        for s_ in range(2):
            P.dve(lambda e, o=dtrt[s_][:, :]: e.memset(o, 0.0), [], [f"ss_dtr{s_}"])
        cum = self.sb("ss_cum", [128, 32])
        ecum = self.sb("ss_ecum", [128, 32])
        de = self.sb("ss_de", [128, 32])
        cd = self.sb("ss_cd", [128, 32])
        xd = self.sb("ss_xd", [128, 2048])
        xde = self.sb("ss_xde", [128, 2048])
        BCT = self.sb("ss_BCT", [128, 8, 128])
        CBm = self.sb("ss_CBm", [128, 4, 128])
        rseg = self.sb("ss_rseg", [128, 4, 128])
        seg = self.sb("ss_seg", [128, 4, 128])
        M = [self.sb(f"ss_M{s}", [128, 4, 128]) for s in range(2)]
        Sst = self.sb("ss_S", [128, 2048])
        yt = [self.sb(f"ss_y{s}", [128, 2048]) for s in range(2)]
        n = 0
        for d in range(2):
            P.dve(lambda e: e.memset(Sst[:, :], 0.0), [], ["ss_S"])
            for j in range(NT):
                T = self.tmap(d, j)
                s = n % 2
                n += 1
                r0 = T * 128
                xk = f"ss_xbc{s}"
                self.ld(xbc[s][:, 0:1536], self.XBC[r0:r0 + 128, 0:1536], [], [xk])
                self.ld(xbc[s][:, 1536:3072], self.XBC[r0:r0 + 128, 1536:3072], [], [xk])
                self.ld(dtr[s], self.Z1[r0:r0 + 128, 5120 + d * 32:5152 + d * 32], [], [f"ss_dtr{s}"], q="pool")
                X, dk = xbc[s], f"ss_dtr{s}"
                if d == 1:
                    for q in range(6):
                        b = self.psA_i % 4
                        self.psA_i += 1
                        self.mm(self.psA[:, b * 512:(b + 1) * 512], self.flipt[:, :], xbc[s][:, q * 512:(q + 1) * 512], True, True,
                                [xk, "flip"], [f"psA{b}"])
                        if q % 2 == 0:
                            P.act(lambda e, o=xf[:, q * 512:(q + 1) * 512], i=self.psA[:, b * 512:(b + 1) * 512]: e.copy(o, i), [f"psA{b}"], ["ss_xf"])
                        else:
                            P.dve(lambda e, o=xf[:, q * 512:(q + 1) * 512], i=self.psA[:, b * 512:(b + 1) * 512]: e.tensor_copy(o, i), [f"psA{b}"], ["ss_xf"])
                    self.mm(self.psC[:, 0:512], self.flipt[:, :], dtrt[s][:, :], True, True, [dk, "flip"], ["psC0"])
                    self.tt(dt[:, :], self.psC[:, 0:32], dtb[d][:, :], ALU.add, ["psC0", "ssc"], ["ss_dt"])
                    X, xk = xf, "ss_xf"
                else:
                    self.tt(dt[:, :], dtr[s], dtb[d][:, :], ALU.add, [dk, "ssc"], ["ss_dt"])
                if float(os.environ.get("SSD_CUT", "9")) < 0.3:
                    continue
                self.actf(dt[:, :], dt[:, :], AF.Exp, ["ss_dt"], ["ss_dt"])
                self.actf(dt[:, :], dt[:, :], AF.Ln, ["ss_dt"], ["ss_dt"], bias=1.0)
                self.stt(da, dt[:, :], -1.0, ab[d][:, :], ALU.mult, ALU.mult, ["ss_dt", "ssc2"], ["ss_da"])
                if float(os.environ.get("SSD_CUT", "9")) < 0.5:
                    continue
                self.mm(self.psC[:, 512:1024], tri[:, :], dat[:, :], True, True, ["ss_da", "ssc"], ["psC1"])
                self.mm(self.psC[:, 0:512], ones[:, :], dat[:, :], True, True, ["ss_da", "ss_ones"], ["psC0"])
                P.dve(lambda e: e.tensor_copy(cum[:, :], self.psC[:, 512:544]), ["psC1"], ["ss_cum"])
                P.dve(lambda e: e.tensor_copy(tot[:, :], self.psC[:, 0:32]), ["psC0"], ["ss_tot"])
                self.actf(ecum[:, :], cum[:, :], AF.Exp, ["ss_cum"], ["ss_ecum"])
                self.actf(cd[:, :], tot[:, :], AF.Exp, ["ss_tot"], ["ss_cd"])
                self.tt(de[:, :], tot[:, :], cum[:, :], ALU.subtract, ["ss_tot", "ss_cum"], ["ss_de"])
                self.actf(de[:, :], de[:, :], AF.Exp, ["ss_de"], ["ss_de"])
                if float(os.environ.get("SSD_CUT", "9")) < 0.7:
                    continue
                xs3 = X[:, 0:2048].rearrange("p (h q) -> p h q", q=64)
                self.tt(xd[:, :].rearrange("p (h q) -> p h q", q=64), xs3, dt[:, :].unsqueeze(2).to_broadcast([128, 32, 64]),
                        ALU.mult, [xk, "ss_dt"], ["ss_xd"])
                self.tt(xde[:, :].rearrange("p (h q) -> p h q", q=64), xd[:, :].rearrange("p (h q) -> p h q", q=64),
                        de[:, :].unsqueeze(2).to_broadcast([128, 32, 64]), ALU.mult, ["ss_xd", "ss_de"], ["ss_xde"])
                if float(os.environ.get("SSD_CUT", "9")) < 2:
                    continue
                for q in range(8):
                    self.tr(self.psB[:, q * 128:(q + 1) * 128], X[:, 2048 + q * 128:2176 + q * 128], [xk, "ident"], [f"psB{q // 4}"])
                P.act(lambda e, o=BCT[:, 0:4, :], i=self.psB[:, 0:512].rearrange("p (g q) -> p g q", q=128): e.copy(o, i), ["psB0"], ["ss_BCT"])
                P.act(lambda e, o=BCT[:, 4:8, :], i=self.psB[:, 512:1024].rearrange("p (g q) -> p g q", q=128): e.copy(o, i), ["psB1"], ["ss_BCT"])
                for g in range(4):
                    self.mm(self.psB[:, g * 128:(g + 1) * 128], BCT[:, g, :], BCT[:, 4 + g, :], True, True, ["ss_BCT"], ["psB0"])
                self.tt(CBm[:, :, :], self.psB[:, 0:512].rearrange("p (g q) -> p g q", q=128),
                        tri[:, :].unsqueeze(1).to_broadcast([128, 4, 128]), ALU.mult, ["psB0", "ssc"], ["ss_CBm"])
                for g in range(4):
                    self.mm(self.psA[:, g * 512:(g + 1) * 512], BCT[:, 4 + g, :], Sst[:, g * 512:(g + 1) * 512], True, True,
                            ["ss_BCT", "ss_S"], [f"psA{g}"])
                yk = f"ss_y{s}"
                for g in range(4):
                    self.tt(yt[s][:, g * 512:(g + 1) * 512].rearrange("p (h q) -> p h q", q=64),
                            self.psA[:, g * 512:(g + 1) * 512].rearrange("p (h q) -> p h q", q=64),
                            ecum[:, g * 8:(g + 1) * 8].unsqueeze(2).to_broadcast([128, 8, 64]), ALU.mult,
                            [f"psA{g}", "ss_ecum"], [yk])
                if float(os.environ.get("SSD_CUT", "9")) < 3:
                    continue
                for hb in range(8):
                    g = hb // 2
                    h0 = hb * 4
                    ms = hb % 2
                    cb = hb % 2
                    self.tt(rseg[:, :, :], tri[:, :].unsqueeze(1).to_broadcast([128, 4, 128]),
                            dat[:, h0:h0 + 4].unsqueeze(2).to_broadcast([128, 4, 128]), ALU.mult, ["ssc", "ss_da"], ["ss_rseg"])
                    pss = self.psC[:, cb * 512:(cb + 1) * 512]
                    self.mm(pss, ones[:, :], rseg[:, :, :].rearrange("p a b -> p (a b)"), True, False, ["ss_ones", "ss_rseg"], [f"psC{cb}"])
                    self.mm(pss, self.ident, neg4[:, :], False, True, ["ident", "ssc"], [f"psC{cb}"])
                    self.tt(seg[:, :, :], pss.rearrange("p (a b) -> p a b", b=128),
                            cum[:, h0:h0 + 4].unsqueeze(2).to_broadcast([128, 4, 128]), ALU.subtract, [f"psC{cb}", "ss_cum"], ["ss_seg"])
                    self.actf(seg[:, :, :], seg[:, :, :], AF.Exp, ["ss_seg"], ["ss_seg"])
                    self.tt(M[ms][:, :, :], seg[:, :, :], CBm[:, g:g + 1, :].to_broadcast([128, 4, 128]), ALU.mult,
                            ["ss_seg", "ss_CBm"], [f"ss_M{ms}"])
                    for q in range(4):
                        h = h0 + q
                        self.mm(self.psA[:, h * 64:(h + 1) * 64], M[ms][:, q, :], xd[:, h * 64:(h + 1) * 64], True, True,
                                [f"ss_M{ms}", "ss_xd"], [f"psA{h // 8}"])
                if float(os.environ.get("SSD_CUT", "9")) < 4:
                    continue
                for g in range(4):
                    self.tt(yt[s][:, g * 512:(g + 1) * 512], yt[s][:, g * 512:(g + 1) * 512], self.psA[:, g * 512:(g + 1) * 512],
                            ALU.add, [yk, f"psA{g}"], [yk])
                self.ld(self.YD[d][j * 128:(j + 1) * 128, :], yt[s][:, :], [yk], [("YD", d, j)] + self.dk("YD%d" % d), q="pool")
                for g in range(4):
                    self.mm(self.psA[:, g * 512:(g + 1) * 512], X[:, 2048 + g * 128:2176 + g * 128], xde[:, g * 512:(g + 1) * 512],
                            True, True, [xk, "ss_xde"], [f"psA{g}"])
                for g in range(4):
                    S3 = Sst[:, g * 512:(g + 1) * 512].rearrange("p (h q) -> p h q", q=64)
                    self.tt(S3, S3, cd[:, g * 8:(g + 1) * 8].unsqueeze(2).to_broadcast([128, 8, 64]), ALU.mult, ["ss_S", "ss_cd"], ["ss_S"])
                    self.tt(Sst[:, g * 512:(g + 1) * 512], Sst[:, g * 512:(g + 1) * 512], self.psA[:, g * 512:(g + 1) * 512], ALU.add,
                            ["ss_S", f"psA{g}"], ["ss_S"])

    def stage_ssd_comb(self):
        P = self.P
        dsk = self.sb("sm_dsk", [128, 32])
        nw = self.sb("sm_nw", [128, 2048])
        self.ld(dsk[:, :], self.bc(self.inp["ssd_d"][0:1, :], 32), [], ["smc"])
        self.ld(nw[:, :], self.bc(self.inp["ssd_norm_w"][0:1, :], 2048), [], ["smc"], q="pool")
        yf = [self.sb(f"sm_f{s}", [128, 2048]) for s in range(2)]
        yb = [self.sb(f"sm_b{s}", [128, 2048]) for s in range(2)]
        xs = [self.sb(f"sm_x{s}", [128, 2048]) for s in range(2)]
        zg = [self.sb(f"sm_z{s}", [128, 2048]) for s in range(2)]
        sq = self.sb("sm_sq", [128, 2048])
        ss = self.sb("sm_ss", [128, 4])
        for T in range(2, NT):
            s = T % 2
            jb = self.tmap(1, T)
            fk = f"sm_f{s}"
            self.ld(yf[s][:, :], self.YD[0][T * 128:(T + 1) * 128, :], [], [fk])
            self.ld(yb[s][:, :], self.YD[1][jb * 128:(jb + 1) * 128, :], [], [f"sm_b{s}"], q="pool")
            self.ld(xs[s][:, :], self.XBC[T * 128:(T + 1) * 128, 0:2048], [], [f"sm_x{s}"])
            self.ld(zg[s][:, :], self.Z1[T * 128:(T + 1) * 128, 0:2048], [], [f"sm_z{s}"], q="pool")
            for q in range(4):
                self.mm(self.psA[:, q * 512:(q + 1) * 512], self.flipt[:, :], yb[s][:, q * 512:(q + 1) * 512], True, True,
                        [f"sm_b{s}", "flip"], [f"psA{q}"])
                self.tt(yf[s][:, q * 512:(q + 1) * 512], yf[s][:, q * 512:(q + 1) * 512], self.psA[:, q * 512:(q + 1) * 512], ALU.add,
                        [fk, f"psA{q}"], [fk])
            x3 = xs[s][:, :].rearrange("p (h q) -> p h q", q=64)
            self.tt(x3, x3, dsk[:, :].unsqueeze(2).to_broadcast([128, 32, 64]), ALU.mult, [f"sm_x{s}", "smc"], [f"sm_x{s}"])
            self.tt(yf[s][:, :], yf[s][:, :], xs[s][:, :], ALU.add, [fk, f"sm_x{s}"], [fk])
            self.actf(zg[s][:, :], zg[s][:, :], AF.Silu, [f"sm_z{s}"], [f"sm_z{s}"])
            self.tt(yf[s][:, :], yf[s][:, :], zg[s][:, :], ALU.mult, [fk, f"sm_z{s}"], [fk])
            self.tt(sq[:, :], yf[s][:, :], yf[s][:, :], ALU.mult, [fk], ["sm_sq"])
            P.dve(lambda e, o=ss[:, :], i=sq[:, :].rearrange("p (g q) -> p g q", q=512): e.tensor_reduce(o, i, AX.X, ALU.add),
                  ["sm_sq"], ["sm_ss"])
            self.rsqrt(ss[:, :], ss[:, :], 1.0 / 512, EPS, ["sm_ss"], ["sm_ss"])
            y3 = yf[s][:, :].rearrange("p (g q) -> p g q", q=512)
            self.tt(y3, y3, ss[:, :].unsqueeze(2).to_broadcast([128, 4, 512]), ALU.mult, [fk, "sm_ss"], [fk])
            self.tt(yf[s][:, :], yf[s][:, :], nw[:, :], ALU.mult, [fk, "smc"], [fk])
            self.ld(self.MIX1[(T - 2) * 128:(T - 1) * 128, :], yf[s][:, :], [fk], [("MIX1", T)] + self.dk("MIX1"), q="pool")

    def stage_outproj(self, i, MIX, mixkeys, K, W, t0, first_x_from_input):
        P = self.P
        nk = K // 128
        wsb = self.sb(f"op_w{i}", [128, nk, D])
        mt = [self.sb(f"op_m{i}{s}", [128, K]) for s in range(2)]
        mT = self.sb(f"op_mT{i}", [128, nk, 128])
        xt = [self.sb(f"op_x{i}{s}", [128, D]) for s in range(2)]
        tmp = self.sb(f"op_t{i}", [128, D])
        Wv = W.rearrange("(c p) n -> p c n", p=128)
        for c in range(nk):
            self.ld(wsb[:, c, :], Wv[:, c, :], [], ["op_w"], q=("sp" if c % 2 == 0 else "pool"))
        for t in range(t0, NT):
            s = t % 2
            who = 1 if t < 2 else 0
            self.ld(mt[s][:, :], MIX[(t - t0) * 128:(t - t0 + 1) * 128, :], [(k, t) for k in mixkeys], [f"op_m{s}"])
            self.ld(xt[s][:, :], self.xsrc(t), [("XR", t)], [f"op_x{s}"], q="pool")
            for c in range(nk):
                pst, key = (self.psB, f"psB{(c % 8) // 4}")
                self.tr(pst[:, (c % 8) * 128:(c % 8 + 1) * 128], mt[s][:, c * 128:(c + 1) * 128], [f"op_m{s}", "ident"], [key])
                if c % 8 == 7 or c == nk - 1:
                    c0 = c - (c % 8)
                    nn = c - c0 + 1
                    P.act(lambda e, o=mT[:, c0:c0 + nn, :], a=pst[:, 0:nn * 128].rearrange("p (c q) -> p c q", q=128): e.copy(o, a),
                          ["psB0", "psB1"], ["op_mT"])
            for j in range(2):
                b = self.psA_i % 4
                self.psA_i += 1
                for c in range(nk):
                    self.mm(self.psA[:, b * 512:(b + 1) * 512], mT[:, c, :], wsb[:, c, j * 512:(j + 1) * 512],
                            c == 0, c == nk - 1, ["op_mT", "op_w"], [f"psA{b}"])
                self.tt(tmp[:, j * 512:(j + 1) * 512], self.psA[:, b * 512:(b + 1) * 512],
                        self.gate[who][0][:, j * 512:(j + 1) * 512], ALU.mult, [f"psA{b}", f"gate{who}0"], ["op_t"])
            self.tt(xt[s][:, :], xt[s][:, :], tmp[:, :], ALU.add, [f"op_x{s}", "op_t"], [f"op_x{s}"])
            self.ld(self.XR[t * 128:(t + 1) * 128, :], xt[s][:, :], [f"op_x{s}"], [("XR", t)] + self.dk("XR"), q="pool")


def rope_table():
    t = np.arange(SEQ)
    row = (t // 64).astype(np.float32)
    col = (t % 64).astype(np.float32)
    m = 16
    inv = (np.float32(10000.0) ** (-np.arange(m, dtype=np.float32) / np.float32(m))).astype(np.float32)
    ar = (row[:, None] * inv).astype(np.float32)
    ac = (col[:, None] * inv).astype(np.float32)
    C = np.concatenate([np.cos(ar), np.cos(ar), np.cos(ac), np.cos(ac)], 1)
    S = np.concatenate([-np.sin(ar), np.sin(ar), -np.sin(ac), np.sin(ac)], 1)
    tab = np.zeros((L, 128), np.float32)
    tab[:LC, :64] = 1.0
    tab[LC:, :64] = C
    tab[LC:, 64:] = S
    return tab


def host_consts():
    return {
        "k_rope": rope_table(),
        "k_iota": np.tile(np.arange(256, dtype=np.float32), (128, 1)),
        "k_tri": np.triu(np.ones((128, 128), np.float32)),
        "k_neg4": np.tile(np.tril(np.full((128, 128), -30000.0, np.float32), -1), (1, 4)),
        "k_ident": np.eye(128, dtype=np.float32),
        "k_flip": np.ascontiguousarray(np.eye(128, dtype=np.float32)[::-1]),
    }


def make_in_maps(inputs):
    consts = host_consts()
    maps = []
    for b in range(8):
        m = {}
        for k in INPUT_SHAPES:
            if k in consts:
                m[k] = consts[k]
            elif k in ("x", "c", "ctx"):
                m[k] = np.ascontiguousarray(np.asarray(inputs[k], dtype=np.float32)[b])
            else:
                m[k] = np.ascontiguousarray(np.asarray(inputs[k], dtype=np.float32))
        maps.append(m)
    return maps


def kernel(**inputs):
    b = Builder()
    nc = b.build()
    res = run_bass_kernel_spmd(nc, make_in_maps(inputs), core_ids=list(range(8)))
    return np.stack([r["out"] for r in res.results], axis=0).astype(np.float32)
```

```python
import math
import os
from contextlib import ExitStack
import numpy as np
import concourse.bass as bass
import concourse.mybir as mybir
from concourse.bass_utils import run_bass_kernel_spmd

F32 = mybir.dt.float32
U32 = mybir.dt.uint32
I32 = mybir.dt.int32
ALU = mybir.AluOpType
AF = mybir.ActivationFunctionType
AX = mybir.AxisListType

ENGS = ("pe", "act", "dve", "pool", "sp")
EPOCH = 30000
NDMASEM = {"sp": 24, "pool": 10, "act": 6}

D = 1024
LC = 256
SEQ = 2048
L = LC + SEQ
NT = L // 128
EPS = 1e-6
EV_IN = 2560
ODD_IN = 5184


class Op:
    __slots__ = ("eng", "fn", "reads", "writes", "dma", "deps", "sig", "idx", "epos")

    def __init__(self, eng, fn, reads, writes, dma):
        self.eng = eng
        self.fn = fn
        self.reads = reads
        self.writes = writes
        self.dma = dma
        self.deps = []
        self.sig = None


class Prog:
    def __init__(self, nc, stack):
        self.nc = nc
        self.stack = stack
        self.ops = []
        self.done = 0
        self.last_w = {}
        self.readers = {}
        self.epos = {e: 0 for e in ENGS}
        self.cnt = {e: 0 for e in ENGS}
        self.dcnt = {e: 0 for e in ENGS}
        self.sems = {}
        self.seen = {e: {} for e in ENGS}
        self.pending = {e: [] for e in ENGS}

    def add(self, eng, fn, r=(), w=(), dma=False):
        op = Op(eng, fn, tuple(r), tuple(w), dma)
        op.idx = len(self.ops)
        self.ops.append(op)
        return op

    def pe(self, fn, r=(), w=()):
        return self.add("pe", fn, r, w)

    def act(self, fn, r=(), w=()):
        return self.add("act", fn, r, w)

    def dve(self, fn, r=(), w=()):
        return self.add("dve", fn, r, w)

    def pool(self, fn, r=(), w=()):
        return self.add("pool", fn, r, w)

    def dma(self, fn, r=(), w=(), q="sp"):
        return self.add(q, fn, r, w, dma=True)

    def sem(self, key):
        if key not in self.sems:
            self.sems[key] = self.stack.enter_context(self.nc.semaphore("s_%s_%s_%d" % key))
        return self.sems[key]

    def flush(self):
        nc = self.nc
        ops = self.ops
        base = self.done
        new = ops[base:]
        if not new:
            return
        last_w, readers = self.last_w, self.readers
        for op in new:
            op.epos = self.epos[op.eng]
            self.epos[op.eng] += 1
            deps = set()
            for k in op.reads:
                if k in last_w:
                    deps.add(last_w[k])
            for k in op.writes:
                if k in last_w:
                    deps.add(last_w[k])
                for rr in readers.get(k, ()):
                    deps.add(rr)
            deps.discard(op.idx)
            for k in op.writes:
                last_w[k] = op.idx
                readers[k] = []
            for k in op.reads:
                if k not in op.writes:
                    readers.setdefault(k, []).append(op.idx)
            fd = []
            for d in deps:
                if d < base:
                    continue
                p = ops[d]
                if (not p.dma) and (not op.dma) and p.eng == op.eng:
                    if p.eng == "pe":
                        continue
                    if op.epos - p.epos > 3:
                        continue
                fd.append(d)
            if self.pending[op.eng]:
                fd.extend(self.pending[op.eng])
                self.pending[op.eng] = []
            op.deps = sorted(set(fd))
        needs = set()
        for op in new:
            for d in op.deps:
                needs.add(d)
        frontier = []
        lastc = {}
        for op in new:
            if op.dma:
                frontier.append(op.idx)
            elif op.fn is not None:
                lastc[op.eng] = op.idx
        for e, i in lastc.items():
            needs.add(i)
            frontier.append(i)
        for op in new:
            if op.dma:
                slot = self.dcnt[op.eng] % NDMASEM[op.eng]
                op.sig = ("d", op.eng, slot, 16 * (self.dcnt[op.eng] // NDMASEM[op.eng] + 1))
                self.dcnt[op.eng] += 1
            elif op.idx in needs:
                ep = self.cnt[op.eng] // EPOCH
                op.sig = ("c", op.eng, ep, self.cnt[op.eng] % EPOCH + 1)
                self.cnt[op.eng] += 1
            if op.sig is not None:
                self.sem(op.sig[:3])
        sems = self.sems
        seen_all = self.seen

        def run_engine(ename, eng):
            seen = seen_all[ename]
            for op in new:
                if op.eng != ename:
                    continue
                waits = {}
                if op.dma and op.sig[3] > 16:
                    waits[op.sig[:3]] = op.sig[3] - 16
                for d in op.deps:
                    sg = ops[d].sig
                    key = sg[:3]
                    if waits.get(key, 0) < sg[3]:
                        waits[key] = sg[3]
                for key, v in waits.items():
                    if seen.get(key, 0) >= v:
                        continue
                    seen[key] = v
                    eng.wait_ge(sems[key], v)
                ins = op.fn(eng)
                if op.sig is not None and ins is not None:
                    ins.then_inc(sems[op.sig[:3]], 16 if op.dma else 1)

        with nc.Block() as block:
            @block.tensor
            def _(e):
                run_engine("pe", e)

            @block.scalar
            def _(e):
                run_engine("act", e)

            @block.vector
            def _(e):
                run_engine("dve", e)

            @block.gpsimd
            def _(e):
                run_engine("pool", e)

            @block.sync
            def _(e):
                run_engine("sp", e)
        self.done = len(ops)
        for e in ENGS:
            self.pending[e] = list(frontier)


INPUT_SHAPES = {
    "x": [SEQ, D], "c": [D], "ctx": [LC, D], "c_ctx": [D],
    "mod_w": [2, D, 6 * D], "mod_b": [2, 6 * D], "norm1_g": [2, D], "norm2_g": [2, D],
    "ev_w_in": [1, D, EV_IN], "ev_w_out": [1, D, D], "attn_q_gain": [1, 64], "attn_k_gain": [1, 64],
    "rw_mu": [1, 1792], "rw_w0": [1, 2, 512], "rw_w2": [1, 2, 64, 512], "rw_a0": [1, 2, 512],
    "rw_a2": [1, 2, 64, 512], "rw_g2": [1, 128, 512], "rw_k_k": [1, 512], "rw_k_a": [1, 512],
    "rw_r_k": [1, 512], "rw_gn_w": [1, 512], "rw_gn_b": [1, 512],
    "ssd_w_in": [1, D, ODD_IN], "ssd_conv_w": [1, 3, 3072], "ssd_conv_b": [1, 3072],
    "ssd_dt_bias": [1, 2, 32], "ssd_a_log": [1, 2, 32], "ssd_d": [1, 32], "ssd_norm_w": [1, 2048],
    "ssd_w_out": [1, 2048, D], "peer_w_q": [2, D, 2048], "peer_keys": [2, 8, 2, 128, 128],
    "peer_u": [2, 16384, D], "peer_v": [2, 16384, D],
    "k_ident": [128, 128], "k_flip": [128, 128], "k_rope": [L, 128],
    "k_tri": [128, 128], "k_neg4": [128, 512], "k_iota": [128, 256], "k_sel": [2, 128],
}


class Builder:
    def __init__(self, debug_outs=(), stop_after=None):
        self.nc = bass.Bass("TRN2", target_bir_lowering=False)
        self.stack = ExitStack()
        self.sstack = None
        self.P = Prog(self.nc, self.stack)
        self.debug_outs = set(debug_outs)
        self.stop_after = stop_after
        self.inp = {}
        for k, shp in INPUT_SHAPES.items():
            self.inp[k] = self.nc.dram_tensor(k, shp, F32, kind="ExternalInput").ap()
        self.out = self.nc.dram_tensor("out", [SEQ, D], F32, kind="ExternalOutput").ap()
        self.uid = 0

    def sb(self, name, shape, dt=F32):
        st = self.sstack if self.sstack is not None else self.stack
        return st.enter_context(self.nc.sbuf_tensor(name, shape, dt))

    def begin(self):
        self.sstack = ExitStack()

    def end(self):
        self.P.flush()
        self.sstack.close()
        self.sstack = None

    def ps(self, name, shape, dt=F32):
        return self.stack.enter_context(self.nc.psum_tensor(name, shape, dt))

    def dk(self, name):
        return [("dbg", name)] if name in self.debug_outs else []

    def dram(self, name, shape, dt=F32):
        kind = "ExternalOutput" if name in self.debug_outs else "Internal"
        return self.nc.dram_tensor(name, shape, dt, kind=kind).ap()

    def tt(self, out, a, b, op, r, w):
        self.P.dve(lambda e: e.tensor_tensor(out, a, b, op), r, w)

    def ts(self, out, a, s1, s2, op0, op1, r, w, accum=None):
        if op1 is None:
            self.P.dve(lambda e: e.tensor_scalar(out, a, s1, None, op0), r, w)
        elif accum is None:
            self.P.dve(lambda e: e.tensor_scalar(out, a, s1, s2, op0, op1), r, w)
        else:
            self.P.dve(lambda e: e.tensor_scalar(out, a, s1, s2, op0, op1, accum), r, w)

    def stt(self, out, a, s, b, op0, op1, r, w, accum=None):
        if accum is None:
            self.P.dve(lambda e: e.scalar_tensor_tensor(out, a, s, b, op0, op1), r, w)
        else:
            self.P.dve(lambda e: e.scalar_tensor_tensor(out, a, s, b, op0, op1, accum), r, w)

    def actf(self, out, in_, func, r, w, bias=0.0, scale=1.0, accum=None):
        if accum is None:
            self.P.act(lambda e: e.activation(out, in_, func, bias=bias, scale=scale), r, w)
        else:
            self.P.act(lambda e: e.activation(out, in_, func, bias=bias, scale=scale, accum_out=accum), r, w)

    def rsqrt(self, out, in_, scale, bias, r, w):
        self.actf(out, in_, AF.Sqrt, r, w, bias=bias, scale=scale)
        self.P.dve(lambda e: e.reciprocal(out, out), w, w)

    def mm(self, out, lhsT, rhs, start, stop, r, w):
        self.P.pe(lambda e: e.matmul(out, lhsT, rhs, start=start, stop=stop), r, w)

    def tr(self, out, in_, r, w):
        ident = self.ident
        n = in_.shape[0]
        self.P.pe(lambda e: e.transpose(out, in_, ident[0:n, 0:n]), r, w)

    def ld(self, out, in_, r, w, q="sp"):
        self.P.dma(lambda e: e.dma_start(out=out, in_=in_), r, w, q=q)

    def build(self):
        P = self.P
        self.identt = self.sb("identt", [128, 128])
        self.ident = self.identt[:, :]
        self.ld(self.ident, self.inp["k_ident"], [], ["ident"])
        self.flipt = self.sb("flipt", [128, 128])
        self.ld(self.flipt[:, :], self.inp["k_flip"], [], ["flip"])
        self.psA = self.ps("psA", [128, 2048])
        self.psB = self.ps("psB", [128, 1024])
        self.psC = self.ps("psC", [128, 1024])
        self.psA_i = 0
        self.XR = self.dram("XR", [L, D])
        self.MODR = self.dram("MODR", [2, 2, 6 * D])
        self.Z0 = self.dram("Z0", [L, EV_IN])
        self.MIX0 = self.dram("MIX0", [L, D])
        self.NS = 3096
        self.SEQ = [self.dram(f"SEQ{d}", [L, self.NS]) for d in range(2)]
        self.HIST = [self.dram(f"HIST{d}", [L, 1024]) for d in range(2)]
        self.Y = [self.dram(f"Y{d}", [L, 512]) for d in range(2)]
        self.G = self.dram("G", [L, 512])
        self.H2 = self.dram("H2", [L, D])
        self.Z1 = self.dram("Z1", [L, ODD_IN])
        self.XBC = self.dram("XBC", [L, 3072])
        self.YD = [self.dram(f"YD{d}", [L, 2048]) for d in range(2)]
        self.MIX1 = self.dram("MIX1", [SEQ, 2048])
        self.IDX = self.dram("IDX", [L, 128], U32)
        self.GATE = self.dram("GATE", [L, 128])

        self.colmod = self.sb("colmod", [128, 2, 6, 8])
        self.gate = [[self.sb(f"gate{w}{m}", [128, D]) for m in range(2)] for w in range(2)]
        self.ng = self.sb("ng", [128, 2, 8])
        self.Acol = self.sb("Acol", [128, 2, 2, 8])
        self.nm_junk = self.sb("nm_junk", [128, D])
        self.nm_ss = self.sb("nm_ss", [128, 1])
        self.nm_xn = self.sb("nm_xn", [128, D])
        self.cur_x_is_input = True
        stages = [
            ("mod", lambda: self.stage_mod()),
            ("inproj0", lambda: self.stage_inproj(0, self.inp["ev_w_in"][0], EV_IN, self.Z0)),
            ("attn", lambda: self.stage_attn()),
            ("rwprep", lambda: self.stage_rw_prep()),
            ("rwscan", lambda: self.stage_rw_scan()),
            ("rwpost", lambda: self.stage_rw_post()),
            ("rwcomb", lambda: self.stage_rw_comb()),
            ("outproj0", lambda: self.stage_outproj(0, self.MIX0, ("MIXa", "MIXb"), D, self.inp["ev_w_out"][0], 0, True)),
            ("route0", lambda: self.stage_peer_route(0, 0)),
            ("expert0", lambda: self.stage_peer_expert(0, 0, False)),
            ("inproj1", lambda: self.stage_inproj(1, self.inp["ssd_w_in"][0], ODD_IN, self.Z1)),
            ("ssdconv", lambda: self.stage_ssd_conv()),
            ("ssdscan", lambda: self.stage_ssd_scan()),
            ("ssdcomb", lambda: self.stage_ssd_comb()),
            ("outproj1", lambda: self.stage_outproj(1, self.MIX1, ("MIX1",), 2048, self.inp["ssd_w_out"][0], 2, False)),
            ("route1", lambda: self.stage_peer_route(1, 2)),
            ("expert1", lambda: self.stage_peer_expert(1, 2, True)),
        ]
        for name, fn in stages:
            self.begin()
            fn()
            self.end()
            if name == "outproj0":
                self.cur_x_is_input = False
            if self.stop_after == name:
                break
        return self.finish()

    def finish(self):
        P = self.P
        P.flush()
        P.add("sp", lambda e: None, r=[], w=[])
        P.flush()
        return self.nc

    def xsrc(self, t):
        if self.cur_x_is_input:
            if t < 2:
                return self.inp["ctx"][t * 128:(t + 1) * 128, :]
            return self.inp["x"][(t - 2) * 128:(t - 1) * 128, :]
        return self.XR[t * 128:(t + 1) * 128, :]

    def stage_mod(self):
        P = self.P
        craw = self.sb("craw", [128, 2, 8])
        sc2 = self.sb("sc2", [128, 8, 2])
        self.ld(craw[:, 0, :], self.inp["c"].rearrange("(p c) -> p c", c=8), [], ["craw"])
        self.ld(craw[:, 1, :], self.inp["c_ctx"].rearrange("(p c) -> p c", c=8), [], ["craw"])
        self.actf(sc2[:, :, :].rearrange("p c w -> p w c"), craw[:, :, :], AF.Silu, ["craw"], ["sc2"])
        wt = [self.sb(f"modw{s}", [128, 3072]) for s in range(2)]
        mb = self.sb("modb", [2, 3072])
        rr = self.sb("modr", [2, 3072])
        n = 0
        for i in range(2):
            wv = self.inp["mod_w"][i].rearrange("(p c) n -> c p n", c=8)
            for hf in range(2):
                cs = slice(hf * 3072, (hf + 1) * 3072)
                self.ld(mb[:, :], self.inp["mod_b"][i:i + 1, cs].to_broadcast([2, 3072]), [], ["modb"])
                for kc in range(8):
                    s = n % 2
                    n += 1
                    self.ld(wt[s][:, :], wv[kc][:, cs], [], [f"modw{s}"], q=("sp" if s == 0 else "pool"))
                    for j in range(6):
                        o = self.psA[0:2, j * 512:(j + 1) * 512] if j < 4 else self.psB[0:2, (j - 4) * 512:(j - 3) * 512]
                        key = f"psA{j}" if j < 4 else f"psB{j - 4}"
                        self.mm(o, sc2[:, kc, :], wt[s][:, j * 512:(j + 1) * 512], kc == 0, kc == 7,
                                ["sc2", f"modw{s}"], [key])
                for j in range(6):
                    o = self.psA[0:2, j * 512:(j + 1) * 512] if j < 4 else self.psB[0:2, (j - 4) * 512:(j - 3) * 512]
                    key = f"psA{j}" if j < 4 else f"psB{j - 4}"
                    self.tt(rr[:, j * 512:(j + 1) * 512], o, mb[:, j * 512:(j + 1) * 512], ALU.add,
                            [key, "modb"], ["modr"])
                self.ld(self.MODR[i][:, cs], rr[:, :], ["modr"], [("MODR", i)] + self.dk("MODR"))

    def load_mod(self, i):
        for who in range(2):
            self.ld(self.colmod[:, who, :, :],
                    self.MODR[i, who].rearrange("(m p c) -> p m c", m=6, p=128, c=8),
                    [("MODR", i)], ["colmod"])
            for m in range(2):
                self.ld(self.gate[who][m][:, :],
                        self.MODR[i, who:who + 1, (2 + 3 * m) * D:(3 + 3 * m) * D].to_broadcast([128, D]),
                        [("MODR", i)], [f"gate{who}{m}"], q="pool")
        self.ld(self.ng[:, 0, :], self.inp["norm1_g"][i].rearrange("(p c) -> p c", c=8), [], ["ng"])
        self.ld(self.ng[:, 1, :], self.inp["norm2_g"][i].rearrange("(p c) -> p c", c=8), [], ["ng"])
        for k in range(2):
            for who in range(2):
                self.stt(self.Acol[:, k, who, :], self.colmod[:, who, 1 + 3 * k, :], 1.0, self.ng[:, k, :],
                         ALU.add, ALU.mult, ["colmod", "ng"], ["Acol"])

    def norm_mod_T(self, xt, xkey, k, who, hT, hkey, pskey="psB"):
        junk, ss, xn = self.nm_junk, self.nm_ss, self.nm_xn
        self.actf(junk[:, :], xt, AF.Square, [xkey], ["nm_junk", "nm_ss"], accum=ss[:, :])
        self.rsqrt(ss[:, :], ss[:, :], 1.0 / D, EPS, ["nm_ss"], ["nm_ss"])
        self.ts(xn[:, :], xt, ss[:, 0:1], None, ALU.mult, None, [xkey, "nm_ss"], ["nm_xn"])
        pst = self.psB if pskey == "psB" else self.psC
        xv = xn[:, :].rearrange("p (q c) -> p c q", c=8)
        for c in range(8):
            self.tr(pst[:, c * 128:(c + 1) * 128], xv[:, c, :], ["nm_xn", "ident"], [pskey + str(c // 4)])
        for c in range(8):
            A = self.Acol[:, k, who, c:c + 1]
            B = self.colmod[:, who, 3 * k, c:c + 1]
            self.actf(hT[:, c, :], pst[:, c * 128:(c + 1) * 128], AF.Identity,
                      [pskey + str(c // 4), "Acol", "colmod"], [hkey], bias=B, scale=A)

    def stage_inproj(self, i, W, N, Z):
        P = self.P
        self.load_mod(i)
        GW = 2560 if N == EV_IN else 2592
        ng = N // GW
        wsb = self.sb(f"win{i}", [128, 8, GW])
        xts = [self.sb(f"ip_x{i}{s}", [128, D]) for s in range(2)]
        hT = self.sb(f"ip_hT{i}", [128, 8, 128])
        zt = [self.sb(f"ip_z{i}{s}", [128, GW]) for s in range(2)]
        Wv = W.rearrange("(p c) n -> p c n", c=8)
        for g in range(ng):
            for c in range(8):
                self.ld(wsb[:, c, :], Wv[:, c, g * GW:(g + 1) * GW], [], ["wsb"], q=("sp" if c % 2 == 0 else "pool"))
            for t in range(NT):
                s = t % 2
                who = 1 if t < 2 else 0
                self.ld(xts[s][:, :], self.xsrc(t), [("XR", t)], [f"ip_x{s}"])
                self.norm_mod_T(xts[s][:, :], f"ip_x{s}", 0, who, hT, "ip_hT")
                nch = (GW + 511) // 512
                for j in range(nch):
                    w = min(512, GW - j * 512)
                    b = self.psA_i % 4
                    self.psA_i += 1
                    for c in range(8):
                        self.mm(self.psA[:, b * 512:b * 512 + w], hT[:, c, :], wsb[:, c, j * 512:j * 512 + w],
                                c == 0, c == 7, ["ip_hT", "wsb"], [f"psA{b}"])
                    if j % 2 == 0:
                        self.P.dve(lambda e, o=zt[s][:, j * 512:j * 512 + w], a=self.psA[:, b * 512:b * 512 + w]:
                                   e.tensor_copy(o, a), [f"psA{b}"], [f"ip_z{s}"])
                    else:
                        self.P.act(lambda e, o=zt[s][:, j * 512:j * 512 + w], a=self.psA[:, b * 512:b * 512 + w]:
                                   e.copy(o, a), [f"psA{b}"], [f"ip_z{s}"])
                self.ld(Z[t * 128:(t + 1) * 128, g * GW:(g + 1) * GW], zt[s][:, :], [f"ip_z{s}"],
                        [("Z", i, t)] + self.dk("Z%d" % i), q="pool")

    def stage_attn(self):
        P = self.P
        Z = self.Z0
        QT = self.sb("at_QT", [64, NT, 8, 128])
        KT = self.sb("at_KT", [64, 2, L])
        V1 = self.sb("at_V1", [128, NT, 2, 65])
        gq = self.sb("at_g", [128, 10, 64])
        zq = [self.sb(f"at_zq{s}", [128, 768]) for s in range(2)]
        rt = [self.sb(f"at_rt{s}", [128, 128]) for s in range(2)]
        sq = self.sb("at_sq", [128, 640])
        ss = self.sb("at_ss", [128, 10])
        qa = self.sb("at_qa", [128, 640])
        qb = self.sb("at_qb", [128, 640])
        for h in range(10):
            src = self.inp["attn_q_gain"] if h < 8 else self.inp["attn_k_gain"]
            self.ld(gq[:, h, :], src[0:1, :].to_broadcast([128, 64]), [], ["at_g"], q="pool")
        P.dve(lambda e: e.memset(V1[:, :, :, 64:65], 1.0), [], ["at_V1"])
        for t in range(NT):
            s = t % 2
            self.ld(zq[s][:, :], Z[t * 128:(t + 1) * 128, 0:768], [("Z", 0, t)], [f"at_zq{s}"])
            self.ld(rt[s][:, :], self.inp["k_rope"][t * 128:(t + 1) * 128, :], [], [f"at_rt{s}"], q="pool")
            qk = zq[s][:, 0:640]
            qk3 = qk.rearrange("p (h d) -> p h d", d=64)
            self.tt(sq[:, :], qk, qk, ALU.mult, [f"at_zq{s}"], ["at_sq"])
            P.dve(lambda e, o=ss[:, :], i=sq[:, :].rearrange("p (h d) -> p h d", d=64): e.tensor_reduce(o, i, AX.X, ALU.add),
                  ["at_sq"], ["at_ss"])
            self.rsqrt(ss[:, :], ss[:, :], 1.0 / 64, EPS, ["at_ss"], ["at_ss"])
            qa3 = qa[:, :].rearrange("p (h d) -> p h d", d=64)
            self.tt(qa3, qk3, ss[:, :].unsqueeze(2).to_broadcast([128, 10, 64]), ALU.mult, [f"at_zq{s}", "at_ss"], ["at_qa"])
            self.tt(qa3, qa3, gq[:, :, :], ALU.mult, ["at_qa", "at_g"], ["at_qa"])
            Cb = rt[s][:, 0:64].unsqueeze(1).to_broadcast([128, 10, 64])
            qb3 = qb[:, :].rearrange("p (h d) -> p h d", d=64)
            self.tt(qb3, qa3, Cb, ALU.mult, ["at_qa", f"at_rt{s}"], ["at_qb"])
            qa5 = qa[:, :].rearrange("p (h a x m) -> p h a x m", a=2, x=2, m=16)
            sq5 = sq[:, :].rearrange("p (h a x m) -> p h a x m", a=2, x=2, m=16)
            S4 = rt[s][:, 64:128].rearrange("p (a x m) -> p a x m", a=2, x=2)
            for x in range(2):
                Sb = S4[:, :, x, :].unsqueeze(1).to_broadcast([128, 10, 2, 16])
                self.tt(sq5[:, :, :, x, :], qa5[:, :, :, 1 - x, :], Sb, ALU.mult, ["at_qa", f"at_rt{s}"], ["at_sq"])
            self.tt(qb[:, :], qb[:, :], sq[:, :], ALU.add, ["at_qb", "at_sq"], ["at_qb"])
            for h in range(8):
                self.tr(self.psB[0:64, h * 128:(h + 1) * 128], qb[:, h * 64:(h + 1) * 64], ["at_qb", "ident"], [f"psB{h // 4}"])
            for g in range(2):
                self.tr(self.psC[0:64, g * 128:(g + 1) * 128], qb[:, 512 + g * 64:576 + g * 64], ["at_qb", "ident"], ["psC0"])
            P.act(lambda e, o=QT[:, t, :, :], i=self.psB[0:64, :].rearrange("p (h q) -> p h q", q=128): e.copy(o, i),
                  ["psB0", "psB1"], [("at_QT", t)])
            P.dve(lambda e, o=KT[:, :, t * 128:(t + 1) * 128], i=self.psC[0:64, 0:256].rearrange("p (g q) -> p g q", q=128):
                  e.tensor_copy(o, i), ["psC0"], [("at_KT", t)])
            P.act(lambda e, o=V1[:, t, :, 0:64], i=zq[s][:, 640:768].rearrange("p (g d) -> p g d", d=64): e.copy(o, i),
                  [f"at_zq{s}", "at_V1"], [("at_V1", t)])
        pt = [self.sb(f"at_pt{s}", [128, 512]) for s in range(3)]
        ot = [self.sb(f"at_ot{s}", [128, 512]) for s in range(2)]
        rc = self.sb("at_rc", [128, 8])
        accs = [(self.psB, 0, "psB0"), (self.psB, 512, "psB1"), (self.psC, 0, "psC0"), (self.psC, 512, "psC1")]
        n = 0
        for t in range(NT):
            kcs = [0, 1] if t < 2 else list(range(NT))
            so = t % 2
            for g in range(2):
                for ki, kc in enumerate(kcs):
                    b = self.psA_i % 4
                    self.psA_i += 1
                    sp_ = n % 3
                    n += 1
                    self.mm(self.psA[:, b * 512:(b + 1) * 512], KT[:, g, kc * 128:(kc + 1) * 128],
                            QT[:, t, g * 4:(g + 1) * 4, :], True, True, [("at_KT", kc), ("at_QT", t)], [f"psA{b}"])
                    self.actf(pt[sp_][:, :], self.psA[:, b * 512:(b + 1) * 512], AF.Exp, [f"psA{b}"], [f"at_pt{sp_}"], scale=0.125)
                    for h in range(4):
                        pst, off, key = accs[h]
                        self.mm(pst[:, off:off + 65], pt[sp_][:, h * 128:(h + 1) * 128], V1[:, kc, g, :],
                                ki == 0, ki == len(kcs) - 1, [f"at_pt{sp_}", ("at_V1", kc)], [key])
                for h in range(4):
                    pst, off, key = accs[h]
                    hh = g * 4 + h
                    P.dve(lambda e, o=rc[:, hh:hh + 1], i=pst[:, off + 64:off + 65]: e.reciprocal(o, i), [key], ["at_rc"])
                    self.ts(ot[so][:, hh * 64:(hh + 1) * 64], pst[:, off:off + 64], rc[:, hh:hh + 1], None,
                            ALU.mult, None, [key, "at_rc"], [f"at_ot{so}"])
            self.ld(self.MIX0[t * 128:(t + 1) * 128, 0:512], ot[so][:, :], [f"at_ot{so}"],
                    [("MIXa", t)] + self.dk("MIX0"), q="pool")

    @staticmethod
    def tmap(d, j):
        if d == 0:
            return j
        return 1 - j if j < 2 else 19 - j

    def bc(self, src_row, n):
        return src_row.to_broadcast([128, n])

    def stage_rw_prep(self):
        P = self.P
        ZO = 768
        Z = self.Z0
        NS = self.NS
        mu_b = self.sb("rw_mub", [128, 1792])
        kkb = self.sb("rw_kkb", [128, 512])
        kab = self.sb("rw_kab", [128, 512])
        rkb = self.sb("rw_rkb", [128, 512])
        g2 = self.sb("rw_g2s", [128, 512])
        W2 = [self.sb(f"rw_W2{d}", [65, 512]) for d in range(2)]
        A2 = [self.sb(f"rw_A2{d}", [65, 512]) for d in range(2)]
        self.ld(mu_b[:, :], self.bc(self.inp["rw_mu"][0:1, :], 1792), [], ["rwc"])
        self.ld(kkb[:, :], self.bc(self.inp["rw_k_k"][0:1, :], 512), [], ["rwc"], q="pool")
        self.ld(kab[:, :], self.bc(self.inp["rw_k_a"][0:1, :], 512), [], ["rwc"])
        self.ld(rkb[:, :], self.bc(self.inp["rw_r_k"][0:1, :], 512), [], ["rwc"], q="pool")
        self.ld(g2[:, :], self.inp["rw_g2"][0], [], ["rwc"])
        for d in range(2):
            self.ld(W2[d][0:64, :], self.inp["rw_w2"][0, d], [], ["rwc"], q="pool")
            self.ld(W2[d][64:65, :], self.inp["rw_w0"][0, d:d + 1, :], [], ["rwc"])
            self.ld(A2[d][0:64, :], self.inp["rw_a2"][0, d], [], ["rwc"], q="pool")
            self.ld(A2[d][64:65, :], self.inp["rw_a0"][0, d:d + 1, :], [], ["rwc"])
        wtT = self.sb("rw_wtT", [65, 128])
        alT = self.sb("rw_alT", [65, 128])
        sgT = self.sb("rw_sgT", [128, 128])
        P.dve(lambda e: e.memset(wtT[:, :], 1.0), [], ["rw_wtT"])
        P.dve(lambda e: e.memset(alT[:, :], 1.0), [], ["rw_alT"])
        zt = [self.sb(f"rw_zt{s}", [128, 1792]) for s in range(2)]
        zm = [self.sb(f"rw_zm{s}", [128, 1792]) for s in range(2)]
        zp = [self.sb(f"rw_zp{s}", [128, 1792]) for s in range(2)]
        zf = self.sb("rw_zf", [128, 1792])
        seqt = [self.sb(f"rw_seq{s}", [128, NS]) for s in range(2)]
        at = self.sb("rw_at", [128, 512])
        t1 = self.sb("rw_t1", [128, 512])
        t2 = self.sb("rw_t2", [128, 512])
        ss = self.sb("rw_ss", [128, 8])
        gt = [self.sb(f"rw_gt{s}", [128, 512]) for s in range(2)]
        n = 0
        for d in range(2):
            for j in range(NT):
                T = self.tmap(d, j)
                s = n % 2
                n += 1
                r0 = T * 128
                self.ld(zt[s][:, :], Z[r0:r0 + 128, ZO:ZO + 1792], [], [f"rw_zt{s}"])
                P.pool(lambda e, o=zm[s][:, :]: e.memset(o, 0.0), [], [f"rw_zm{s}"])
                P.pool(lambda e, o=zp[s][:, :]: e.memset(o, 0.0), [], [f"rw_zp{s}"])
                if T in (0, 2):
                    self.ld(zm[s][1:128, :], Z[r0:r0 + 127, ZO:ZO + 1792], [], [f"rw_zm{s}"], q="pool")
                else:
                    self.ld(zm[s][:, :], Z[r0 - 1:r0 + 127, ZO:ZO + 1792], [], [f"rw_zm{s}"], q="pool")
                if T in (1, NT - 1):
                    self.ld(zp[s][0:127, :], Z[r0 + 1:r0 + 128, ZO:ZO + 1792], [], [f"rw_zp{s}"])
                else:
                    self.ld(zp[s][:, :], Z[r0 + 1:r0 + 129, ZO:ZO + 1792], [], [f"rw_zp{s}"])
                self.tt(zm[s][:, :], zm[s][:, :], zp[s][:, :], ALU.add, [f"rw_zm{s}", f"rw_zp{s}"], [f"rw_zm{s}"])
                self.stt(zm[s][:, :], zm[s][:, :], 0.5, zt[s][:, :], ALU.mult, ALU.subtract, [f"rw_zm{s}", f"rw_zt{s}"], [f"rw_zm{s}"])
                self.tt(zm[s][:, :], zm[s][:, :], mu_b[:, :], ALU.mult, [f"rw_zm{s}", "rwc"], [f"rw_zm{s}"])
                self.tt(zt[s][:, :], zt[s][:, :], zm[s][:, :], ALU.add, [f"rw_zm{s}", f"rw_zt{s}"], [f"rw_zt{s}"])
                zz, zk = zt[s], f"rw_zt{s}"
                if d == 1:
                    for q in range(4):
                        w = 512 if q < 3 else 256
                        b = self.psA_i % 4
                        self.psA_i += 1
                        self.mm(self.psA[:, b * 512:b * 512 + w], self.flipt[:, :], zt[s][:, q * 512:q * 512 + w], True, True,
                                [zk, "flip"], [f"psA{b}"])
                        if q % 2 == 0:
                            P.act(lambda e, o=zf[:, q * 512:q * 512 + w], i=self.psA[:, b * 512:b * 512 + w]: e.copy(o, i),
                                  [f"psA{b}"], ["rw_zf"])
                        else:
                            P.dve(lambda e, o=zf[:, q * 512:q * 512 + w], i=self.psA[:, b * 512:b * 512 + w]: e.tensor_copy(o, i),
                                  [f"psA{b}"], ["rw_zf"])
                    zz, zk = zf, "rw_zf"
                r_ = zz[:, 0:512]
                k_ = zz[:, 512:1024]
                v_ = zz[:, 1024:1536]
                sq = seqt[s]
                sk = f"rw_seq{s}"
                self.tr(self.psB[0:64, 0:128], zz[:, 1536:1600], [zk, "ident"], ["psB0"])
                self.tr(self.psB[0:64, 128:256], zz[:, 1600:1664], [zk, "ident"], ["psB0"])
                self.tr(self.psB[:, 256:384], zz[:, 1664:1792], [zk, "ident"], ["psB0"])
                self.actf(wtT[0:64, :], self.psB[0:64, 0:128], AF.Tanh, ["psB0"], ["rw_wtT"])
                P.dve(lambda e, o=alT[0:64, :], i=self.psB[0:64, 128:256]: e.tensor_copy(o, i), ["psB0"], ["rw_alT"])
                self.actf(sgT[:, :], self.psB[:, 256:384], AF.Sigmoid, ["psB0"], ["rw_sgT"])
                b1 = self.psA_i % 4
                b2 = (self.psA_i + 1) % 4
                b3 = (self.psA_i + 2) % 4
                self.psA_i += 3
                self.mm(self.psA[:, b1 * 512:(b1 + 1) * 512], wtT[:, :], W2[d][:, :], True, True, ["rw_wtT", "rwc"], [f"psA{b1}"])
                self.mm(self.psA[:, b2 * 512:(b2 + 1) * 512], alT[:, :], A2[d][:, :], True, True, ["rw_alT", "rwc"], [f"psA{b2}"])
                self.actf(t1[:, :], self.psA[:, b1 * 512:(b1 + 1) * 512], AF.Sigmoid, [f"psA{b1}"], ["rw_t1"])
                self.actf(sq[:, 1536:2048], t1[:, :], AF.Exp, ["rw_t1"], [sk], scale=-math.exp(-0.5))
                self.actf(at[:, :], self.psA[:, b2 * 512:(b2 + 1) * 512], AF.Sigmoid, [f"psA{b2}"], ["rw_at"])
                if d == 0:
                    sg = j % 2
                    self.mm(self.psA[:, b3 * 512:(b3 + 1) * 512], sgT[:, :], g2[:, :], True, True, ["rw_sgT", "rwc"], [f"psA{b3}"])
                    P.act(lambda e, o=gt[sg][:, :], i=self.psA[:, b3 * 512:(b3 + 1) * 512]: e.copy(o, i), [f"psA{b3}"], [f"rw_gt{sg}"])
                    self.ld(self.G[r0:r0 + 128, :], gt[sg][:, :], [f"rw_gt{sg}"], [("G", T)], q="pool")
                self.tt(t1[:, :], k_, kkb[:, :], ALU.mult, [zk, "rwc", "rw_t1"], ["rw_t1"])
                self.tt(t2[:, :], t1[:, :], t1[:, :], ALU.mult, ["rw_t1"], ["rw_t2"])
                P.dve(lambda e, o=ss[:, :], i=t2[:, :].rearrange("p (h d) -> p h d", d=64): e.tensor_reduce(o, i, AX.X, ALU.add),
                      ["rw_t2"], ["rw_ss"])
                self.actf(ss[:, :], ss[:, :], AF.Sqrt, ["rw_ss"], ["rw_ss"])
                self.ts(ss[:, :], ss[:, :], 1e-12, None, ALU.max, None, ["rw_ss"], ["rw_ss"])
                P.dve(lambda e, o=ss[:, :]: e.reciprocal(o, o), ["rw_ss"], ["rw_ss"])
                self.tt(sq[:, 0:512].rearrange("p (h d) -> p h d", d=64), t1[:, :].rearrange("p (h d) -> p h d", d=64),
                        ss[:, :].unsqueeze(2).to_broadcast([128, 8, 64]), ALU.mult, ["rw_t1", "rw_ss"], [sk])
                self.tt(sq[:, 512:1024], sq[:, 1536:2048], r_, ALU.mult, [sk, zk], [sk])
                self.tt(sq[:, 2048:2560], sq[:, 0:512], at[:, :], ALU.mult, [sk, "rw_at"], [sk])
                self.stt(t2[:, :], at[:, :], -1.0, kab[:, :], ALU.add, ALU.mult, ["rw_at", "rwc", "rw_t2"], ["rw_t2"])
                self.stt(sq[:, 1024:1536], t2[:, :], 1.0, k_, ALU.add, ALU.mult, ["rw_t2", zk], [sk])
                P.act(lambda e, o=sq[:, 2560:3072], i=v_: e.copy(o, i), [zk], [sk])
                self.tt(t1[:, :], sq[:, 2048:2560], r_, ALU.mult, [sk, zk, "rw_t1"], ["rw_t1"])
                P.dve(lambda e, o=sq[:, 3072:3080], i=t1[:, :].rearrange("p (h d) -> p h d", d=64): e.tensor_reduce(o, i, AX.X, ALU.add),
                      ["rw_t1"], [sk])
                self.tt(t1[:, :], sq[:, 1024:1536], r_, ALU.mult, [sk, zk, "rw_t1"], ["rw_t1"])
                P.dve(lambda e, o=sq[:, 3080:3088], i=t1[:, :].rearrange("p (h d) -> p h d", d=64): e.tensor_reduce(o, i, AX.X, ALU.add),
                      ["rw_t1"], [sk])
                self.tt(t1[:, :], t1[:, :], rkb[:, :], ALU.mult, ["rw_t1", "rwc"], ["rw_t1"])
                P.dve(lambda e, o=sq[:, 3088:3096], i=t1[:, :].rearrange("p (h d) -> p h d", d=64): e.tensor_reduce(o, i, AX.X, ALU.add),
                      ["rw_t1"], [sk])
                self.ld(self.SEQ[d][j * 128:(j + 1) * 128, :], sq[:, :], [sk], [("SEQ", d, j)] + self.dk("SEQ%d" % d), q="pool")

    def stage_rw_scan(self):
        P = self.P
        NS = self.NS
        CS, NB = 2, 4
        RS, NR = 4, 3
        S = self.sb("sc_S", [128, 512])
        prod = self.sb("sc_prod", [128, 1024])
        tmp = self.sb("sc_tmp", [128, 512])
        hist = [self.sb(f"sc_hist{s}", [128, 128, 16]) for s in range(2)]
        Bt = [self.sb(f"sc_B{s}", [128, CS, 1536]) for s in range(NB)]
        VK = [self.sb(f"sc_VK{s}", [128, CS, 512]) for s in range(NB)]
        Rt = [self.sb(f"sc_R{s}", [2, RS, 1024]) for s in range(NR)]
        sel = self.sb("sc_sel", [2, 128])
        vcat = self.sb("sc_vcat", [128, 8, 2, 64])
        vT = self.sb("sc_vT", [128, 8, 128])
        HT = self.sb("sc_HT", [128, 16, 128])
        self.ld(sel[:, :], self.inp["k_sel"], [], ["sc_sel"])
        P.dve(lambda e: e.memset(S[:, :], 0.0), [], ["sc_S"])
        banks = [(self.psA, 0, "psA0"), (self.psA, 512, "psA1"), (self.psA, 1024, "psA2"), (self.psA, 1536, "psA3"),
                 (self.psC, 0, "psC0"), (self.psC, 512, "psC1")]
        nchunk = 0
        nr = 0
        nbank = 0
        for jb in range(NT):
            hb = jb % 2
            H = hist[hb]
            hk = f"sc_hist{hb}"
            for d in range(2):
                self.ld(vcat[:, :, d, :], self.SEQ[d][jb * 128:(jb + 1) * 128, 2560:3072].rearrange("p (h v) -> p h v", v=64),
                        [], ["sc_vcat"], q=("sp" if d == 0 else "pool"))
            for h in range(8):
                self.tr(self.psB[:, h * 128:(h + 1) * 128], vcat[:, h, :, :].rearrange("p d v -> p (d v)"),
                        ["sc_vcat", "ident"], [f"psB{h // 4}"])
            for hh in range(2):
                P.act(lambda e, o=vT[:, hh * 4:(hh + 1) * 4, :], i=self.psB[:, hh * 512:(hh + 1) * 512].rearrange("p (h q) -> p h q", q=128):
                      e.copy(o, i), [f"psB{hh}"], ["sc_vT"])
            for st in range(128):
                pos = jb * 128 + st
                if st % RS == 0:
                    rs = nr % NR
                    nr += 1
                    for d in range(2):
                        self.ld(Rt[rs][d:d + 1, :, :], self.SEQ[d][pos:pos + RS, 1536:2560].unsqueeze(0), [], [("sc_R", rs)],
                                q=("sp" if d == 0 else "pool"))
                if st % CS == 0:
                    sl = nchunk % NB
                    nchunk += 1
                    for d in range(2):
                        src = self.SEQ[d][pos:pos + CS, 0:1536]
                        srcb = bass.AP(src.tensor, src.offset, [[0, 64], [NS, CS], [1, 1536]])
                        self.ld(Bt[sl][d * 64:(d + 1) * 64, :, :], srcb, [], [("sc_B", sl)], q=("sp" if d == 0 else "pool"))
                    self.tt(VK[sl][:, :, :].rearrange("p s (h k) -> p s h k", k=64),
                            vT[:, :, st:st + CS].rearrange("p h s -> p s h").unsqueeze(3).to_broadcast([128, CS, 8, 64]),
                            Bt[sl][:, :, 1024:1536].rearrange("p s (h k) -> p s h k", k=64), ALU.mult,
                            ["sc_vT", ("sc_B", sl)], [("sc_VK", sl)])
                k = st % CS
                Bk = Bt[sl][:, k, :]
                pw, ow, kw = banks[nbank % 6]
                pa, oa, ka_ = banks[(nbank + 1) % 6]
                nbank += 2
                self.mm(pw[:, ow:ow + 512], sel[:, :], Rt[rs][:, st % RS, 0:512], True, True, ["sc_sel", ("sc_R", rs)], [kw])
                self.mm(pa[:, oa:oa + 512], sel[:, :], Rt[rs][:, st % RS, 512:1024], True, True, ["sc_sel", ("sc_R", rs)], [ka_])
                self.tt(prod[:, :].rearrange("p (a n) -> p a n", a=2), S[:, :].unsqueeze(1).to_broadcast([128, 2, 512]),
                        Bk[:, 0:1024].rearrange("p (a n) -> p a n", a=2), ALU.mult, ["sc_S", ("sc_B", sl)], ["sc_prod"])
                P.dve(lambda e, o=H[:, st, :], i=prod[:, :].rearrange("p (a k) -> p a k", k=64): e.tensor_reduce(o, i, AX.X, ALU.add),
                      ["sc_prod"], [hk])
                self.tt(S[:, :], S[:, :], pw[:, ow:ow + 512], ALU.mult, ["sc_S", kw], ["sc_S"])
                self.tt(tmp[:, :].rearrange("p (h k) -> p h k", k=64), H[:, st, 0:8].unsqueeze(2).to_broadcast([128, 8, 64]),
                        pa[:, oa:oa + 512].rearrange("p (h k) -> p h k", k=64), ALU.mult, [hk, ka_], ["sc_tmp"])
                self.tt(S[:, :], S[:, :], tmp[:, :], ALU.subtract, ["sc_S", "sc_tmp"], ["sc_S"])
                self.tt(S[:, :], S[:, :], VK[sl][:, k, :], ALU.add, ["sc_S", ("sc_VK", sl)], ["sc_S"])
            for q in range(16):
                self.tr(self.psA[:, q * 128:(q + 1) * 128], H[:, :, q], [hk, "ident"], [f"psA{q // 4}"])
            for b in range(4):
                src = self.psA[:, b * 512:(b + 1) * 512].rearrange("p (q m) -> p q m", m=128)
                if b % 2 == 0:
                    P.act(lambda e, o=HT[:, b * 4:(b + 1) * 4, :], i=src: e.copy(o, i), [f"psA{b}"], ["sc_HT"])
                else:
                    P.dve(lambda e, o=HT[:, b * 4:(b + 1) * 4, :], i=src: e.tensor_copy(o, i), [f"psA{b}"], ["sc_HT"])
            for d in range(2):
                self.ld(self.HIST[d][jb * 128:(jb + 1) * 128, :].rearrange("p (q v) -> p q v", v=64),
                        HT[:, :, d * 64:(d + 1) * 64], ["sc_HT"], [("HIST", d, jb)] + self.dk("HIST%d" % d), q="pool")

    def stage_rw_post(self):
        P = self.P
        gwb = self.sb("rp_gw", [128, 512])
        gbb = self.sb("rp_gb", [128, 512])
        self.ld(gwb[:, :], self.bc(self.inp["rw_gn_w"][0:1, :], 512), [], ["rpc"])
        self.ld(gbb[:, :], self.bc(self.inp["rw_gn_b"][0:1, :], 512), [], ["rpc"], q="pool")
        ht = [self.sb(f"rp_ht{s}", [128, 16, 64]) for s in range(2)]
        sv = [self.sb(f"rp_sv{s}", [128, 536]) for s in range(2)]
        o = self.sb("rp_o", [128, 8, 64])
        t = self.sb("rp_t", [128, 8, 64])
        m = self.sb("rp_m", [128, 8])
        yt = [self.sb(f"rp_y{s}", [128, 512]) for s in range(2)]
        n = 0

        def b8(ap):
            return ap.unsqueeze(2).to_broadcast([128, 8, 64])

        for d in range(2):
            for j in range(NT):
                s = n % 2
                n += 1
                hk, vk, yk = f"rp_ht{s}", f"rp_sv{s}", f"rp_y{s}"
                self.ld(ht[s][:, :, :], self.HIST[d][j * 128:(j + 1) * 128, :].rearrange("p (q v) -> p q v", v=64), [], [hk])
                self.ld(sv[s][:, :], self.SEQ[d][j * 128:(j + 1) * 128, 2560:3096], [], [vk], q="pool")
                v3 = sv[s][:, 0:512].rearrange("p (h v) -> p h v", v=64)
                c1, c2, bs = sv[s][:, 512:520], sv[s][:, 520:528], sv[s][:, 528:536]
                y3 = yt[s][:, :].rearrange("p (h v) -> p h v", v=64)
                self.tt(t[:, :, :], ht[s][:, 0:8, :], b8(c1), ALU.mult, [hk, vk], ["rp_t"])
                self.tt(o[:, :, :], ht[s][:, 8:16, :], t[:, :, :], ALU.subtract, [hk, "rp_t"], ["rp_o"])
                self.tt(t[:, :, :], v3, b8(c2), ALU.mult, [vk, "rp_t"], ["rp_t"])
                self.tt(o[:, :, :], o[:, :, :], t[:, :, :], ALU.add, ["rp_o", "rp_t"], ["rp_o"])
                P.dve(lambda e, o_=m[:, :], i=o[:, :, :]: e.tensor_reduce(o_, i, AX.X, ALU.add), ["rp_o"], ["rp_m"])
                self.ts(m[:, :], m[:, :], 1.0 / 64, None, ALU.mult, None, ["rp_m"], ["rp_m"])
                self.tt(o[:, :, :], o[:, :, :], b8(m[:, :]), ALU.subtract, ["rp_o", "rp_m"], ["rp_o"])
                self.tt(t[:, :, :], o[:, :, :], o[:, :, :], ALU.mult, ["rp_o", "rp_t"], ["rp_t"])
                P.dve(lambda e, o_=m[:, :], i=t[:, :, :]: e.tensor_reduce(o_, i, AX.X, ALU.add), ["rp_t", "rp_m"], ["rp_m"])
                self.rsqrt(m[:, :], m[:, :], 1.0 / 64, 64e-5, ["rp_m"], ["rp_m"])
                self.tt(o[:, :, :], o[:, :, :], b8(m[:, :]), ALU.mult, ["rp_o", "rp_m"], ["rp_o"])
                self.tt(o[:, :, :], o[:, :, :], gwb[:, :].rearrange("p (h v) -> p h v", v=64), ALU.mult, ["rp_o", "rpc"], ["rp_o"])
                self.tt(o[:, :, :], o[:, :, :], gbb[:, :].rearrange("p (h v) -> p h v", v=64), ALU.add, ["rp_o", "rpc"], ["rp_o"])
                self.tt(t[:, :, :], v3, b8(bs), ALU.mult, [vk, "rp_t"], ["rp_t"])
                self.tt(y3, o[:, :, :], t[:, :, :], ALU.add, ["rp_o", "rp_t"], [yk])
                self.ld(self.Y[d][j * 128:(j + 1) * 128, :], yt[s][:, :], [yk], [("Y", d, j)] + self.dk("Y%d" % d), q="pool")

    def stage_rw_comb(self):
        P = self.P
        yf = [self.sb(f"rc_f{s}", [128, 512]) for s in range(2)]
        yb = [self.sb(f"rc_b{s}", [128, 512]) for s in range(2)]
        gg = [self.sb(f"rc_g{s}", [128, 512]) for s in range(2)]
        for T in range(NT):
            s = T % 2
            jb = self.tmap(1, T)
            self.ld(yf[s][:, :], self.Y[0][T * 128:(T + 1) * 128, :], [], [f"rc_f{s}"])
            self.ld(yb[s][:, :], self.Y[1][jb * 128:(jb + 1) * 128, :], [], [f"rc_b{s}"], q="pool")
            self.ld(gg[s][:, :], self.G[T * 128:(T + 1) * 128, :], [], [f"rc_g{s}"])
            b = self.psA_i % 4
            self.psA_i += 1
            self.mm(self.psA[:, b * 512:(b + 1) * 512], self.flipt[:, :], yb[s][:, :], True, True, [f"rc_b{s}", "flip"], [f"psA{b}"])
            self.tt(yf[s][:, :], yf[s][:, :], self.psA[:, b * 512:(b + 1) * 512], ALU.add, [f"rc_f{s}", f"psA{b}"], [f"rc_f{s}"])
            self.tt(yf[s][:, :], yf[s][:, :], gg[s][:, :], ALU.mult, [f"rc_f{s}", f"rc_g{s}"], [f"rc_f{s}"])
            self.ld(self.MIX0[T * 128:(T + 1) * 128, 512:1024], yf[s][:, :], [f"rc_f{s}"], [("MIXb", T)] + self.dk("MIX0"), q="pool")

    def stage_peer_route(self, i, t0):
        P = self.P
        wq = self.sb(f"pr_wq{i}", [128, 8, 2048])
        Wv = self.inp["peer_w_q"][i].rearrange("(p c) n -> p c n", c=8)
        for c in range(8):
            self.ld(wq[:, c, :], Wv[:, c, :], [], ["pr_wq"], q=("sp" if c % 2 == 0 else "pool"))
        kraw = self.sb(f"pr_kraw{i}", [128, 16, 128])
        keysT = self.sb(f"pr_kT{i}", [128, 16, 128])
        self.ld(kraw[:, :, :], self.inp["peer_keys"][i].rearrange("h p n d -> n (h p) d"), [], ["pr_kraw"])
        for hp in range(16):
            pst, key = (self.psB, f"psB{(hp % 8) // 4}")
            self.tr(pst[:, (hp % 8) * 128:(hp % 8 + 1) * 128], kraw[:, hp, :], ["pr_kraw", "ident"], [key])
            if hp % 8 == 7:
                P.act(lambda e, o=keysT[:, hp - 7:hp + 1, :], a=pst[:, :].rearrange("p (c q) -> p c q", q=128): e.copy(o, a),
                      ["psB0", "psB1"], ["pr_kT"])
        xt = [self.sb(f"pr_x{i}{s}", [128, D]) for s in range(2)]
        hT = self.sb(f"pr_hT{i}", [128, 8, 128])
        h2 = [self.sb(f"pr_h2{i}{s}", [128, D]) for s in range(2)]
        qT = self.sb(f"pr_qT{i}", [128, 16, 128])
        sc = self.sb(f"pr_sc{i}", [128, 16, 128])
        sc2 = self.sb(f"pr_sc2{i}", [128, 16, 128])
        sv = self.sb(f"pr_sv{i}", [128, 16, 16])
        si = self.sb(f"pr_si{i}", [128, 16, 16], U32)
        sif = self.sb(f"pr_sif{i}", [128, 16, 16])
        cs = self.sb(f"pr_cs{i}", [128, 8, 256])
        cs2 = self.sb(f"pr_cs2{i}", [128, 8, 256])
        ci = self.sb(f"pr_ci{i}", [128, 8, 256])
        junk = self.sb(f"pr_junk{i}", [128, 256])
        bs = self.sb(f"pr_bs{i}", [128, 8, 16])
        idxf = [self.sb(f"pr_idxf{i}{s}", [128, 128]) for s in range(2)]
        idxu = [self.sb(f"pr_idxu{i}{s}", [128, 128], U32) for s in range(2)]
        gt = [self.sb(f"pr_gt{i}{s}", [128, 8, 16]) for s in range(2)]
        gs = self.sb(f"pr_gs{i}", [128, 8])
        pos = self.sb(f"pr_pos{i}", [128, 8, 16], U32)
        posf = self.sb(f"pr_posf{i}", [128, 8, 16])
        iota = self.sb(f"pr_iota{i}", [128, 256])
        self.ld(iota[:, :], self.inp["k_iota"], [], ["pr_iota"])
        for t in range(t0, NT):
            s = t % 2
            who = 1 if t < 2 else 0
            row = (t - t0) * 128
            self.ld(xt[s][:, :], self.xsrc(t), [("XR", t)], [f"pr_x{s}"])
            self.norm_mod_T(xt[s][:, :], f"pr_x{s}", 1, who, hT, "pr_hT")
            for c in range(8):
                self.tr(self.psC[:, c * 128:(c + 1) * 128], hT[:, c, :], ["pr_hT", "ident"], [f"psC{c // 4}"])
            P.act(lambda e, o=h2[s][:, :].rearrange("t (p c) -> t c p", c=8), a=self.psC[:, :].rearrange("t (c p) -> t c p", p=128):
                  e.copy(o, a), ["psC0", "psC1"], [f"pr_h2{s}"])
            self.ld(self.H2[row:row + 128, :], h2[s][:, :], [f"pr_h2{s}"], [("H2", t)] + self.dk("H2"), q="pool")
            for hp in range(16):
                b = hp // 4
                for c in range(8):
                    self.mm(self.psA[:, hp * 128:(hp + 1) * 128], wq[:, c, hp * 128:(hp + 1) * 128], hT[:, c, :],
                            c == 0, c == 7, ["pr_wq", "pr_hT"], [f"psA{b}"])
            for b in range(4):
                P.act(lambda e, o=qT[:, b * 4:(b + 1) * 4, :], a=self.psA[:, b * 512:(b + 1) * 512].rearrange("p (c q) -> p c q", q=128):
                      e.copy(o, a), [f"psA{b}"], ["pr_qT"])
            for hp in range(16):
                pst, key = (self.psB, f"psB{(hp % 8) // 4}") if hp < 8 else (self.psC, f"psC{(hp % 8) // 4}")
                self.mm(pst[:, (hp % 8) * 128:(hp % 8 + 1) * 128], qT[:, hp, :], keysT[:, hp, :], True, True,
                        ["pr_qT", "pr_kT"], [key])
            for q in range(4):
                pst, key = (self.psB, f"psB{q % 2}") if q < 2 else (self.psC, f"psC{q % 2}")
                P.act(lambda e, o=sc[:, q * 4:(q + 1) * 4, :], a=pst[:, (q % 2) * 512:(q % 2 + 1) * 512].rearrange("p (c q) -> p c q", q=128):
                      e.copy(o, a), [key], ["pr_sc"])
            for hp in range(16):
                P.dve(lambda e, o=sv[:, hp, 0:8], a=sc[:, hp, :]: e.max(o, a), ["pr_sc"], ["pr_sv"])
                P.dve(lambda e, o=si[:, hp, 0:8], m=sv[:, hp, 0:8], a=sc[:, hp, :]: e.max_index(o, m, a), ["pr_sc", "pr_sv"], ["pr_si"])
                P.dve(lambda e, o=sc2[:, hp, :], m=sv[:, hp, 0:8], a=sc[:, hp, :]: e.match_replace(o, m, a, -1e30),
                      ["pr_sc", "pr_sv"], ["pr_sc2"])
                P.dve(lambda e, o=sv[:, hp, 8:16], a=sc2[:, hp, :]: e.max(o, a), ["pr_sc2"], ["pr_sv"])
                P.dve(lambda e, o=si[:, hp, 8:16], m=sv[:, hp, 8:16], a=sc2[:, hp, :]: e.max_index(o, m, a), ["pr_sc2", "pr_sv"], ["pr_si"])
            P.dve(lambda e: e.tensor_copy(sif[:, :, :], si[:, :, :]), ["pr_si"], ["pr_sif"])
            sv4 = sv[:, :, :].rearrange("p (h a) k -> p h a k", a=2)
            sf4 = sif[:, :, :].rearrange("p (h a) k -> p h a k", a=2)
            cs4 = cs[:, :, :].rearrange("p h (a b) -> p h a b", b=16)
            ci4 = ci[:, :, :].rearrange("p h (a b) -> p h a b", b=16)
            self.tt(cs4, sv4[:, :, 0, :].unsqueeze(3).to_broadcast([128, 8, 16, 16]),
                    sv4[:, :, 1, :].unsqueeze(2).to_broadcast([128, 8, 16, 16]), ALU.add, ["pr_sv"], ["pr_cs"])
            self.ts(sf4[:, :, 0, :], sf4[:, :, 0, :], 128.0, None, ALU.mult, None, ["pr_sif"], ["pr_sif"])
            self.tt(ci4, sf4[:, :, 0, :].unsqueeze(3).to_broadcast([128, 8, 16, 16]),
                    sf4[:, :, 1, :].unsqueeze(2).to_broadcast([128, 8, 16, 16]), ALU.add, ["pr_sif"], ["pr_ci"])
            for h in range(8):
                P.dve(lambda e, o=bs[:, h, 0:8], a=cs[:, h, :]: e.max(o, a), ["pr_cs"], ["pr_bs"])
                P.dve(lambda e, o=cs2[:, h, :], m=bs[:, h, 0:8], a=cs[:, h, :]: e.match_replace(o, m, a, -1e30),
                      ["pr_cs", "pr_bs"], ["pr_cs2"])
                P.dve(lambda e, o=bs[:, h, 8:16], a=cs2[:, h, :]: e.max(o, a), ["pr_cs2"], ["pr_bs"])
            for h in range(8):
                P.dve(lambda e, o=pos[:, h, 0:8], m=bs[:, h, 0:8], a=cs[:, h, :]: e.max_index(o, m, a), ["pr_cs", "pr_bs"], ["pr_pos"])
                P.dve(lambda e, o=pos[:, h, 8:16], m=bs[:, h, 8:16], a=cs2[:, h, :]: e.max_index(o, m, a), ["pr_cs2", "pr_bs"], ["pr_pos"])
            P.dve(lambda e: e.tensor_copy(posf[:, :, :], pos[:, :, :]), ["pr_pos"], ["pr_posf"])
            for h in range(8):
                for k in range(16):
                    self.stt(junk[:, :], iota[:, :], posf[:, h, k:k + 1], ci[:, h, :], ALU.is_equal, ALU.mult,
                             ["pr_iota", "pr_posf", "pr_ci"], ["pr_junk", f"pr_idxf{s}"], accum=idxf[s][:, h * 16 + k:h * 16 + k + 1])
            if i > 0:
                self.ts(idxf[s][:, :], idxf[s][:, :], float(16384 * i), None, ALU.add, None, [f"pr_idxf{s}"], [f"pr_idxf{s}"])
            P.dve(lambda e, o=idxu[s][:, :], a=idxf[s][:, :]: e.tensor_copy(o, a), [f"pr_idxf{s}"], [f"pr_idxu{s}"])
            self.ld(self.IDX[row:row + 128, :], idxu[s][:, :], [f"pr_idxu{s}"], [("IDX", t)] + self.dk("IDX"), q="pool")
            self.tt(gt[s][:, :, :], bs[:, :, :], bs[:, :, 0:1].to_broadcast([128, 8, 16]), ALU.subtract, ["pr_bs"], [f"pr_gt{s}"])
            self.actf(gt[s][:, :, :], gt[s][:, :, :], AF.Exp, [f"pr_gt{s}"], [f"pr_gt{s}"])
            P.dve(lambda e, o=gs[:, :], a=gt[s][:, :, :]: e.tensor_reduce(o, a, AX.X, ALU.add), [f"pr_gt{s}"], ["pr_gs"])
            P.dve(lambda e: e.reciprocal(gs[:, :], gs[:, :]), ["pr_gs"], ["pr_gs"])
            self.tt(gt[s][:, :, :], gt[s][:, :, :], gs[:, :].unsqueeze(2).to_broadcast([128, 8, 16]), ALU.mult,
                    [f"pr_gt{s}", "pr_gs"], [f"pr_gt{s}"])
            self.ld(self.GATE[row:row + 128, :], gt[s][:, :, :].rearrange("p h k -> p (h k)"), [f"pr_gt{s}"],
                    [("GATE", t)] + self.dk("GATE"), q="pool")

    def stage_peer_expert(self, i, t0, final):
        P = self.P
        GS = 8
        Ug = [self.sb(f"pe_U{i}{s}", [128, GS, D]) for s in range(2)]
        Vg = [self.sb(f"pe_V{i}{s}", [128, GS, D]) for s in range(2)]
        h2 = [self.sb(f"pe_h2{i}{s}", [128, D]) for s in range(2)]
        xt = [self.sb(f"pe_x{i}{s}", [128, D]) for s in range(2)]
        idx = [self.sb(f"pe_idx{i}{s}", [128, 128], U32) for s in range(2)]
        gate = [self.sb(f"pe_gate{i}{s}", [128, 128]) for s in range(2)]
        act = self.sb(f"pe_act{i}", [128, 128])
        coef = self.sb(f"pe_coef{i}", [128, 128])
        junk = self.sb(f"pe_junk{i}", [128, D])
        acc = self.sb(f"pe_acc{i}", [128, D])
        U = self.inp["peer_u"].rearrange("l e d -> (l e) d")
        V = self.inp["peer_v"].rearrange("l e d -> (l e) d")
        ng = 0
        for t in range(t0, NT):
            s = t % 2
            who = 1 if t < 2 else 0
            row = (t - t0) * 128
            self.ld(h2[s][:, :], self.H2[row:row + 128, :], [], [f"pe_h2{s}"])
            self.ld(xt[s][:, :], self.xsrc(t), [("XR", t)], [f"pe_x{s}"])
            self.ld(idx[s][:, :], self.IDX[row:row + 128, :], [], [f"pe_idx{s}"])
            self.ld(gate[s][:, :], self.GATE[row:row + 128, :], [], [f"pe_gate{s}"])
            for g in range(128 // GS):
                gb = ng % 2
                ng += 1
                for k in range(GS):
                    sl = g * GS + k
                    P.dma(lambda e, o=Ug[gb][:, k, :], ix=idx[s][:, sl:sl + 1]: e.indirect_dma_start(
                        out=o, out_offset=None, in_=U, in_offset=bass.IndirectOffsetOnAxis(ap=ix, axis=0)),
                        [f"pe_idx{s}"], [("pe_U", gb, k)], q="pool")
                for k in range(GS):
                    sl = g * GS + k
                    P.dma(lambda e, o=Vg[gb][:, k, :], ix=idx[s][:, sl:sl + 1]: e.indirect_dma_start(
                        out=o, out_offset=None, in_=V, in_offset=bass.IndirectOffsetOnAxis(ap=ix, axis=0)),
                        [f"pe_idx{s}"], [("pe_V", gb, k)], q="pool")
                for k in range(GS):
                    sl = g * GS + k
                    self.stt(junk[:, :], Ug[gb][:, k, :], 1.0, h2[s][:, :], ALU.mult, ALU.mult,
                             [("pe_U", gb, k), f"pe_h2{s}"], ["pe_junk", "pe_act"], accum=act[:, sl:sl + 1])
                sl0 = g * GS
                self.actf(coef[:, sl0:sl0 + GS], act[:, sl0:sl0 + GS], AF.Gelu, ["pe_act"], ["pe_coef"])
                self.tt(coef[:, sl0:sl0 + GS], coef[:, sl0:sl0 + GS], gate[s][:, sl0:sl0 + GS], ALU.mult,
                        ["pe_coef", f"pe_gate{s}"], ["pe_coef"])
                for k in range(GS):
                    sl = g * GS + k
                    if sl == 0:
                        self.ts(acc[:, :], Vg[gb][:, k, :], coef[:, 0:1], None, ALU.mult, None, [("pe_V", gb, k), "pe_coef"], ["pe_acc"])
                    else:
                        self.stt(acc[:, :], Vg[gb][:, k, :], coef[:, sl:sl + 1], acc[:, :], ALU.mult, ALU.add,
                                 [("pe_V", gb, k), "pe_coef", "pe_acc"], ["pe_acc"])
            self.tt(acc[:, :], acc[:, :], self.gate[who][1][:, :], ALU.mult, ["pe_acc", f"gate{who}1"], ["pe_acc"])
            self.tt(xt[s][:, :], xt[s][:, :], acc[:, :], ALU.add, [f"pe_x{s}", "pe_acc"], [f"pe_x{s}"])
            if final:
                self.ld(self.out[(t - 2) * 128:(t - 1) * 128, :], xt[s][:, :], [f"pe_x{s}"], [("out", t)], q="sp")
            else:
                self.ld(self.XR[t * 128:(t + 1) * 128, :], xt[s][:, :], [f"pe_x{s}"], [("XR", t)] + self.dk("XR"), q="sp")

    def stage_ssd_conv(self):
        P = self.P
        Z = self.Z1
        W = 1536
        wb = [self.sb(f"sv_w{k}", [128, W]) for k in range(3)]
        bb = self.sb("sv_b", [128, W])
        x0 = [self.sb(f"sv_x0{s}", [128, W]) for s in range(2)]
        xm = [self.sb(f"sv_xm{s}", [128, W]) for s in range(2)]
        xp = [self.sb(f"sv_xp{s}", [128, W]) for s in range(2)]
        n = 0
        for hf in range(2):
            CO = 2048 + hf * W
            for k in range(3):
                self.ld(wb[k][:, :], self.bc(self.inp["ssd_conv_w"][0, k:k + 1, hf * W:(hf + 1) * W], W), [], ["svc"],
                        q=("sp" if k % 2 == 0 else "pool"))
            self.ld(bb[:, :], self.bc(self.inp["ssd_conv_b"][0:1, hf * W:(hf + 1) * W], W), [], ["svc"], q="pool")
            for T in range(NT):
                s = n % 2
                n += 1
                r0 = T * 128
                self.ld(x0[s][:, :], Z[r0:r0 + 128, CO:CO + W], [], [f"sv_x0{s}"])
                P.pool(lambda e, o=xm[s][:, :]: e.memset(o, 0.0), [], [f"sv_xm{s}"])
                P.pool(lambda e, o=xp[s][:, :]: e.memset(o, 0.0), [], [f"sv_xp{s}"])
                if T in (0, 2):
                    self.ld(xm[s][1:128, :], Z[r0:r0 + 127, CO:CO + W], [], [f"sv_xm{s}"], q="pool")
                else:
                    self.ld(xm[s][:, :], Z[r0 - 1:r0 + 127, CO:CO + W], [], [f"sv_xm{s}"], q="pool")
                if T in (1, NT - 1):
                    self.ld(xp[s][0:127, :], Z[r0 + 1:r0 + 128, CO:CO + W], [], [f"sv_xp{s}"])
                else:
                    self.ld(xp[s][:, :], Z[r0 + 1:r0 + 129, CO:CO + W], [], [f"sv_xp{s}"])
                self.tt(xm[s][:, :], xm[s][:, :], wb[0][:, :], ALU.mult, [f"sv_xm{s}", "svc"], [f"sv_xm{s}"])
                self.tt(xp[s][:, :], xp[s][:, :], wb[2][:, :], ALU.mult, [f"sv_xp{s}", "svc"], [f"sv_xp{s}"])
                self.tt(x0[s][:, :], x0[s][:, :], wb[1][:, :], ALU.mult, [f"sv_x0{s}", "svc"], [f"sv_x0{s}"])
                self.tt(xm[s][:, :], xm[s][:, :], xp[s][:, :], ALU.add, [f"sv_xm{s}", f"sv_xp{s}"], [f"sv_xm{s}"])
                self.tt(x0[s][:, :], x0[s][:, :], xm[s][:, :], ALU.add, [f"sv_x0{s}", f"sv_xm{s}"], [f"sv_x0{s}"])
                self.tt(x0[s][:, :], x0[s][:, :], bb[:, :], ALU.add, [f"sv_x0{s}", "svc"], [f"sv_x0{s}"])
                self.actf(x0[s][:, :], x0[s][:, :], AF.Silu, [f"sv_x0{s}"], [f"sv_x0{s}"])
                self.ld(self.XBC[r0:r0 + 128, hf * W:(hf + 1) * W], x0[s][:, :], [f"sv_x0{s}"],
                        [("XBC", T, hf)] + self.dk("XBC"), q="pool")

    def stage_ssd_scan(self):
        P = self.P
        tri = self.sb("ss_tri", [128, 128])
        neg4 = self.sb("ss_neg4", [128, 512])
        ones = self.sb("ss_ones", [128, 128])
        self.ld(tri[:, :], self.inp["k_tri"], [], ["ssc"])
        self.ld(neg4[:, :], self.inp["k_neg4"], [], ["ssc"], q="pool")
        P.dve(lambda e: e.memset(ones[:, :], 1.0), [], ["ss_ones"])
        dtb = [self.sb(f"ss_dtb{d}", [128, 32]) for d in range(2)]
        ab = [self.sb(f"ss_ab{d}", [128, 32]) for d in range(2)]
        for d in range(2):
            self.ld(dtb[d][:, :], self.bc(self.inp["ssd_dt_bias"][0, d:d + 1, :], 32), [], ["ssc"])
            self.ld(ab[d][:, :], self.bc(self.inp["ssd_a_log"][0, d:d + 1, :], 32), [], ["ssc2"], q="pool")
            self.actf(ab[d][:, :], ab[d][:, :], AF.Exp, ["ssc2"], ["ssc2"])
        xbc = [self.sb(f"ss_xbc{s}", [128, 3072]) for s in range(2)]
        xf = self.sb("ss_xf", [128, 3072])
        dtrt = [self.sb(f"ss_dtr{s}", [128, 512]) for s in range(2)]
        dtr = [x[:, 0:32] for x in dtrt]
        dt = self.sb("ss_dt", [128, 32])
        dat = self.sb("ss_da", [128, 512])
        tot = self.sb("ss_tot", [128, 32])
        da = dat[:, 0:32]
        P.dve(lambda e: e.memset(dat[:, :], 0.0), [], ["ss_da"])
        for s_ in range(2):
            P.dve(lambda e, o=dtrt[s_][:, :]: e.memset(o, 0.0), [], [f"ss_dtr{s_}"])
        cum = self.sb("ss_cum", [128, 32])
        ecum = self.sb("ss_ecum", [128, 32])
        de = self.sb("ss_de", [128, 32])
        cd = self.sb("ss_cd", [128, 32])
        xd = self.sb("ss_xd", [128, 2048])
        xde = self.sb("ss_xde", [128, 2048])
        BCT = self.sb("ss_BCT", [128, 8, 128])
        CBm = self.sb("ss_CBm", [128, 4, 128])
        rseg = self.sb("ss_rseg", [128, 4, 128])
        seg = self.sb("ss_seg", [128, 4, 128])
        M = [self.sb(f"ss_M{s}", [128, 4, 128]) for s in range(2)]
        Sst = self.sb("ss_S", [128, 2048])
        yt = [self.sb(f"ss_y{s}", [128, 2048]) for s in range(2)]
        n = 0
        for d in range(2):
            P.dve(lambda e: e.memset(Sst[:, :], 0.0), [], ["ss_S"])
            for j in range(NT):
                T = self.tmap(d, j)
                s = n % 2
                n += 1
                r0 = T * 128
                xk = f"ss_xbc{s}"
                self.ld(xbc[s][:, 0:1536], self.XBC[r0:r0 + 128, 0:1536], [], [xk])
                self.ld(xbc[s][:, 1536:3072], self.XBC[r0:r0 + 128, 1536:3072], [], [xk])
                self.ld(dtr[s], self.Z1[r0:r0 + 128, 5120 + d * 32:5152 + d * 32], [], [f"ss_dtr{s}"], q="pool")
                X, dk = xbc[s], f"ss_dtr{s}"
                if d == 1:
                    for q in range(6):
                        b = self.psA_i % 4
                        self.psA_i += 1
                        self.mm(self.psA[:, b * 512:(b + 1) * 512], self.flipt[:, :], xbc[s][:, q * 512:(q + 1) * 512], True, True,
                                [xk, "flip"], [f"psA{b}"])
                        if q % 2 == 0:
                            P.act(lambda e, o=xf[:, q * 512:(q + 1) * 512], i=self.psA[:, b * 512:(b + 1) * 512]: e.copy(o, i), [f"psA{b}"], ["ss_xf"])
                        else:
                            P.dve(lambda e, o=xf[:, q * 512:(q + 1) * 512], i=self.psA[:, b * 512:(b + 1) * 512]: e.tensor_copy(o, i), [f"psA{b}"], ["ss_xf"])
                    self.mm(self.psC[:, 0:512], self.flipt[:, :], dtrt[s][:, :], True, True, [dk, "flip"], ["psC0"])
                    self.tt(dt[:, :], self.psC[:, 0:32], dtb[d][:, :], ALU.add, ["psC0", "ssc"], ["ss_dt"])
                    X, xk = xf, "ss_xf"
                else:
                    self.tt(dt[:, :], dtr[s], dtb[d][:, :], ALU.add, [dk, "ssc"], ["ss_dt"])
                if float(os.environ.get("SSD_CUT", "9")) < 0.3:
                    continue
                self.actf(dt[:, :], dt[:, :], AF.Exp, ["ss_dt"], ["ss_dt"])
                self.actf(dt[:, :], dt[:, :], AF.Ln, ["ss_dt"], ["ss_dt"], bias=1.0)
                self.stt(da, dt[:, :], -1.0, ab[d][:, :], ALU.mult, ALU.mult, ["ss_dt", "ssc2"], ["ss_da"])
                if float(os.environ.get("SSD_CUT", "9")) < 0.5:
                    continue
                self.mm(self.psC[:, 512:1024], tri[:, :], dat[:, :], True, True, ["ss_da", "ssc"], ["psC1"])
                self.mm(self.psC[:, 0:512], ones[:, :], dat[:, :], True, True, ["ss_da", "ss_ones"], ["psC0"])
                P.dve(lambda e: e.tensor_copy(cum[:, :], self.psC[:, 512:544]), ["psC1"], ["ss_cum"])
                P.dve(lambda e: e.tensor_copy(tot[:, :], self.psC[:, 0:32]), ["psC0"], ["ss_tot"])
                self.actf(ecum[:, :], cum[:, :], AF.Exp, ["ss_cum"], ["ss_ecum"])
                self.actf(cd[:, :], tot[:, :], AF.Exp, ["ss_tot"], ["ss_cd"])
                self.tt(de[:, :], tot[:, :], cum[:, :], ALU.subtract, ["ss_tot", "ss_cum"], ["ss_de"])
                self.actf(de[:, :], de[:, :], AF.Exp, ["ss_de"], ["ss_de"])
                if float(os.environ.get("SSD_CUT", "9")) < 0.7:
                    continue
                xs3 = X[:, 0:2048].rearrange("p (h q) -> p h q", q=64)
                self.tt(xd[:, :].rearrange("p (h q) -> p h q", q=64), xs3, dt[:, :].unsqueeze(2).to_broadcast([128, 32, 64]),
                        ALU.mult, [xk, "ss_dt"], ["ss_xd"])
                self.tt(xde[:, :].rearrange("p (h q) -> p h q", q=64), xd[:, :].rearrange("p (h q) -> p h q", q=64),
                        de[:, :].unsqueeze(2).to_broadcast([128, 32, 64]), ALU.mult, ["ss_xd", "ss_de"], ["ss_xde"])
                if float(os.environ.get("SSD_CUT", "9")) < 2:
                    continue
                for q in range(8):
                    self.tr(self.psB[:, q * 128:(q + 1) * 128], X[:, 2048 + q * 128:2176 + q * 128], [xk, "ident"], [f"psB{q // 4}"])
                P.act(lambda e, o=BCT[:, 0:4, :], i=self.psB[:, 0:512].rearrange("p (g q) -> p g q", q=128): e.copy(o, i), ["psB0"], ["ss_BCT"])
                P.act(lambda e, o=BCT[:, 4:8, :], i=self.psB[:, 512:1024].rearrange("p (g q) -> p g q", q=128): e.copy(o, i), ["psB1"], ["ss_BCT"])
                for g in range(4):
                    self.mm(self.psB[:, g * 128:(g + 1) * 128], BCT[:, g, :], BCT[:, 4 + g, :], True, True, ["ss_BCT"], ["psB0"])
                self.tt(CBm[:, :, :], self.psB[:, 0:512].rearrange("p (g q) -> p g q", q=128),
                        tri[:, :].unsqueeze(1).to_broadcast([128, 4, 128]), ALU.mult, ["psB0", "ssc"], ["ss_CBm"])
                for g in range(4):
                    self.mm(self.psA[:, g * 512:(g + 1) * 512], BCT[:, 4 + g, :], Sst[:, g * 512:(g + 1) * 512], True, True,
                            ["ss_BCT", "ss_S"], [f"psA{g}"])
                yk = f"ss_y{s}"
                for g in range(4):
                    self.tt(yt[s][:, g * 512:(g + 1) * 512].rearrange("p (h q) -> p h q", q=64),
                            self.psA[:, g * 512:(g + 1) * 512].rearrange("p (h q) -> p h q", q=64),
                            ecum[:, g * 8:(g + 1) * 8].unsqueeze(2).to_broadcast([128, 8, 64]), ALU.mult,
                            [f"psA{g}", "ss_ecum"], [yk])
                if float(os.environ.get("SSD_CUT", "9")) < 3:
                    continue
                for hb in range(8):
                    g = hb // 2
                    h0 = hb * 4
                    ms = hb % 2
                    cb = hb % 2
                    self.tt(rseg[:, :, :], tri[:, :].unsqueeze(1).to_broadcast([128, 4, 128]),
                            dat[:, h0:h0 + 4].unsqueeze(2).to_broadcast([128, 4, 128]), ALU.mult, ["ssc", "ss_da"], ["ss_rseg"])
                    pss = self.psC[:, cb * 512:(cb + 1) * 512]
                    self.mm(pss, ones[:, :], rseg[:, :, :].rearrange("p a b -> p (a b)"), True, False, ["ss_ones", "ss_rseg"], [f"psC{cb}"])
                    self.mm(pss, self.ident, neg4[:, :], False, True, ["ident", "ssc"], [f"psC{cb}"])
                    self.tt(seg[:, :, :], pss.rearrange("p (a b) -> p a b", b=128),
                            cum[:, h0:h0 + 4].unsqueeze(2).to_broadcast([128, 4, 128]), ALU.subtract, [f"psC{cb}", "ss_cum"], ["ss_seg"])
                    self.actf(seg[:, :, :], seg[:, :, :], AF.Exp, ["ss_seg"], ["ss_seg"])
                    self.tt(M[ms][:, :, :], seg[:, :, :], CBm[:, g:g + 1, :].to_broadcast([128, 4, 128]), ALU.mult,
                            ["ss_seg", "ss_CBm"], [f"ss_M{ms}"])
                    for q in range(4):
                        h = h0 + q
                        self.mm(self.psA[:, h * 64:(h + 1) * 64], M[ms][:, q, :], xd[:, h * 64:(h + 1) * 64], True, True,
                                [f"ss_M{ms}", "ss_xd"], [f"psA{h // 8}"])
                if float(os.environ.get("SSD_CUT", "9")) < 4:
                    continue
                for g in range(4):
                    self.tt(yt[s][:, g * 512:(g + 1) * 512], yt[s][:, g * 512:(g + 1) * 512], self.psA[:, g * 512:(g + 1) * 512],
                            ALU.add, [yk, f"psA{g}"], [yk])
                self.ld(self.YD[d][j * 128:(j + 1) * 128, :], yt[s][:, :], [yk], [("YD", d, j)] + self.dk("YD%d" % d), q="pool")
                for g in range(4):
                    self.mm(self.psA[:, g * 512:(g + 1) * 512], X[:, 2048 + g * 128:2176 + g * 128], xde[:, g * 512:(g + 1) * 512],
                            True, True, [xk, "ss_xde"], [f"psA{g}"])
                for g in range(4):
                    S3 = Sst[:, g * 512:(g + 1) * 512].rearrange("p (h q) -> p h q", q=64)
                    self.tt(S3, S3, cd[:, g * 8:(g + 1) * 8].unsqueeze(2).to_broadcast([128, 8, 64]), ALU.mult, ["ss_S", "ss_cd"], ["ss_S"])
                    self.tt(Sst[:, g * 512:(g + 1) * 512], Sst[:, g * 512:(g + 1) * 512], self.psA[:, g * 512:(g + 1) * 512], ALU.add,
                            ["ss_S", f"psA{g}"], ["ss_S"])

    def stage_ssd_comb(self):
        P = self.P
        dsk = self.sb("sm_dsk", [128, 32])
        nw = self.sb("sm_nw", [128, 2048])
        self.ld(dsk[:, :], self.bc(self.inp["ssd_d"][0:1, :], 32), [], ["smc"])
        self.ld(nw[:, :], self.bc(self.inp["ssd_norm_w"][0:1, :], 2048), [], ["smc"], q="pool")
        yf = [self.sb(f"sm_f{s}", [128, 2048]) for s in range(2)]
        yb = [self.sb(f"sm_b{s}", [128, 2048]) for s in range(2)]
        xs = [self.sb(f"sm_x{s}", [128, 2048]) for s in range(2)]
        zg = [self.sb(f"sm_z{s}", [128, 2048]) for s in range(2)]
        sq = self.sb("sm_sq", [128, 2048])
        ss = self.sb("sm_ss", [128, 4])
        for T in range(2, NT):
            s = T % 2
            jb = self.tmap(1, T)
            fk = f"sm_f{s}"
            self.ld(yf[s][:, :], self.YD[0][T * 128:(T + 1) * 128, :], [], [fk])
            self.ld(yb[s][:, :], self.YD[1][jb * 128:(jb + 1) * 128, :], [], [f"sm_b{s}"], q="pool")
            self.ld(xs[s][:, :], self.XBC[T * 128:(T + 1) * 128, 0:2048], [], [f"sm_x{s}"])
            self.ld(zg[s][:, :], self.Z1[T * 128:(T + 1) * 128, 0:2048], [], [f"sm_z{s}"], q="pool")
            for q in range(4):
                self.mm(self.psA[:, q * 512:(q + 1) * 512], self.flipt[:, :], yb[s][:, q * 512:(q + 1) * 512], True, True,
                        [f"sm_b{s}", "flip"], [f"psA{q}"])
                self.tt(yf[s][:, q * 512:(q + 1) * 512], yf[s][:, q * 512:(q + 1) * 512], self.psA[:, q * 512:(q + 1) * 512], ALU.add,
                        [fk, f"psA{q}"], [fk])
            x3 = xs[s][:, :].rearrange("p (h q) -> p h q", q=64)
            self.tt(x3, x3, dsk[:, :].unsqueeze(2).to_broadcast([128, 32, 64]), ALU.mult, [f"sm_x{s}", "smc"], [f"sm_x{s}"])
            self.tt(yf[s][:, :], yf[s][:, :], xs[s][:, :], ALU.add, [fk, f"sm_x{s}"], [fk])
            self.actf(zg[s][:, :], zg[s][:, :], AF.Silu, [f"sm_z{s}"], [f"sm_z{s}"])
            self.tt(yf[s][:, :], yf[s][:, :], zg[s][:, :], ALU.mult, [fk, f"sm_z{s}"], [fk])
            self.tt(sq[:, :], yf[s][:, :], yf[s][:, :], ALU.mult, [fk], ["sm_sq"])
            P.dve(lambda e, o=ss[:, :], i=sq[:, :].rearrange("p (g q) -> p g q", q=512): e.tensor_reduce(o, i, AX.X, ALU.add),
                  ["sm_sq"], ["sm_ss"])
            self.rsqrt(ss[:, :], ss[:, :], 1.0 / 512, EPS, ["sm_ss"], ["sm_ss"])
            y3 = yf[s][:, :].rearrange("p (g q) -> p g q", q=512)
            self.tt(y3, y3, ss[:, :].unsqueeze(2).to_broadcast([128, 4, 512]), ALU.mult, [fk, "sm_ss"], [fk])
            self.tt(yf[s][:, :], yf[s][:, :], nw[:, :], ALU.mult, [fk, "smc"], [fk])
            self.ld(self.MIX1[(T - 2) * 128:(T - 1) * 128, :], yf[s][:, :], [fk], [("MIX1", T)] + self.dk("MIX1"), q="pool")

    def stage_outproj(self, i, MIX, mixkeys, K, W, t0, first_x_from_input):
        P = self.P
        nk = K // 128
        wsb = self.sb(f"op_w{i}", [128, nk, D])
        mt = [self.sb(f"op_m{i}{s}", [128, K]) for s in range(2)]
        mT = self.sb(f"op_mT{i}", [128, nk, 128])
        xt = [self.sb(f"op_x{i}{s}", [128, D]) for s in range(2)]
        tmp = self.sb(f"op_t{i}", [128, D])
        Wv = W.rearrange("(c p) n -> p c n", p=128)
        for c in range(nk):
            self.ld(wsb[:, c, :], Wv[:, c, :], [], ["op_w"], q=("sp" if c % 2 == 0 else "pool"))
        for t in range(t0, NT):
            s = t % 2
            who = 1 if t < 2 else 0
            self.ld(mt[s][:, :], MIX[(t - t0) * 128:(t - t0 + 1) * 128, :], [(k, t) for k in mixkeys], [f"op_m{s}"])
            self.ld(xt[s][:, :], self.xsrc(t), [("XR", t)], [f"op_x{s}"], q="pool")
            for c in range(nk):
                pst, key = (self.psB, f"psB{(c % 8) // 4}")
                self.tr(pst[:, (c % 8) * 128:(c % 8 + 1) * 128], mt[s][:, c * 128:(c + 1) * 128], [f"op_m{s}", "ident"], [key])
                if c % 8 == 7 or c == nk - 1:
                    c0 = c - (c % 8)
                    nn = c - c0 + 1
                    P.act(lambda e, o=mT[:, c0:c0 + nn, :], a=pst[:, 0:nn * 128].rearrange("p (c q) -> p c q", q=128): e.copy(o, a),
                          ["psB0", "psB1"], ["op_mT"])
            for j in range(2):
                b = self.psA_i % 4
                self.psA_i += 1
                for c in range(nk):
                    self.mm(self.psA[:, b * 512:(b + 1) * 512], mT[:, c, :], wsb[:, c, j * 512:(j + 1) * 512],
                            c == 0, c == nk - 1, ["op_mT", "op_w"], [f"psA{b}"])
                self.tt(tmp[:, j * 512:(j + 1) * 512], self.psA[:, b * 512:(b + 1) * 512],
                        self.gate[who][0][:, j * 512:(j + 1) * 512], ALU.mult, [f"psA{b}", f"gate{who}0"], ["op_t"])
            self.tt(xt[s][:, :], xt[s][:, :], tmp[:, :], ALU.add, [f"op_x{s}", "op_t"], [f"op_x{s}"])
            self.ld(self.XR[t * 128:(t + 1) * 128, :], xt[s][:, :], [f"op_x{s}"], [("XR", t)] + self.dk("XR"), q="pool")


def rope_table():
    t = np.arange(SEQ)
    row = (t // 64).astype(np.float32)
    col = (t % 64).astype(np.float32)
    m = 16
    inv = (np.float32(10000.0) ** (-np.arange(m, dtype=np.float32) / np.float32(m))).astype(np.float32)
    ar = (row[:, None] * inv).astype(np.float32)
    ac = (col[:, None] * inv).astype(np.float32)
    C = np.concatenate([np.cos(ar), np.cos(ar), np.cos(ac), np.cos(ac)], 1)
    S = np.concatenate([-np.sin(ar), np.sin(ar), -np.sin(ac), np.sin(ac)], 1)
    tab = np.zeros((L, 128), np.float32)
    tab[:LC, :64] = 1.0
    tab[LC:, :64] = C
    tab[LC:, 64:] = S
    return tab


def host_consts():
    return {
        "k_rope": rope_table(),
        "k_sel": np.kron(np.eye(2, dtype=np.float32), np.ones((1, 64), np.float32)),
        "k_iota": np.tile(np.arange(256, dtype=np.float32), (128, 1)),
        "k_tri": np.triu(np.ones((128, 128), np.float32)),
        "k_neg4": np.tile(np.tril(np.full((128, 128), -30000.0, np.float32), -1), (1, 4)),
        "k_ident": np.eye(128, dtype=np.float32),
        "k_flip": np.ascontiguousarray(np.eye(128, dtype=np.float32)[::-1]),
    }


def make_in_maps(inputs):
    consts = host_consts()
    maps = []
    for b in range(8):
        m = {}
        for k in INPUT_SHAPES:
            if k in consts:
                m[k] = consts[k]
            elif k in ("x", "c", "ctx"):
                m[k] = np.ascontiguousarray(np.asarray(inputs[k], dtype=np.float32)[b])
            else:
                m[k] = np.ascontiguousarray(np.asarray(inputs[k], dtype=np.float32))
        maps.append(m)
    return maps


def kernel(**inputs):
    b = Builder()
    nc = b.build()
    res = run_bass_kernel_spmd(nc, make_in_maps(inputs), core_ids=list(range(8)))
    return np.stack([r["out"] for r in res.results], axis=0).astype(np.float32)
```

```python
import math
import os
from contextlib import ExitStack
import numpy as np
import concourse.bass as bass
import concourse.mybir as mybir
from concourse.bass_utils import run_bass_kernel_spmd

F32 = mybir.dt.float32
U32 = mybir.dt.uint32
I32 = mybir.dt.int32
ALU = mybir.AluOpType
AF = mybir.ActivationFunctionType
AX = mybir.AxisListType

ENGS = ("pe", "act", "dve", "pool", "sp")
EPOCH = 30000
NDMASEM = {"sp": 24, "pool": 10, "act": 6}

D = 1024
LC = 256
SEQ = 2048
L = LC + SEQ
NT = L // 128
EPS = 1e-6
EV_IN = 2560
ODD_IN = 5184


class Op:
    __slots__ = ("eng", "fn", "reads", "writes", "dma", "deps", "sig", "idx", "epos")

    def __init__(self, eng, fn, reads, writes, dma):
        self.eng = eng
        self.fn = fn
        self.reads = reads
        self.writes = writes
        self.dma = dma
        self.deps = []
        self.sig = None


class Prog:
    def __init__(self, nc, stack):
        self.nc = nc
        self.stack = stack
        self.ops = []
        self.done = 0
        self.last_w = {}
        self.readers = {}
        self.epos = {e: 0 for e in ENGS}
        self.cnt = {e: 0 for e in ENGS}
        self.dcnt = {e: 0 for e in ENGS}
        self.sems = {}
        self.seen = {e: {} for e in ENGS}
        self.pending = {e: [] for e in ENGS}

    def add(self, eng, fn, r=(), w=(), dma=False):
        op = Op(eng, fn, tuple(r), tuple(w), dma)
        op.idx = len(self.ops)
        self.ops.append(op)
        return op

    def pe(self, fn, r=(), w=()):
        return self.add("pe", fn, r, w)

    def act(self, fn, r=(), w=()):
        return self.add("act", fn, r, w)

    def dve(self, fn, r=(), w=()):
        return self.add("dve", fn, r, w)

    def pool(self, fn, r=(), w=()):
        return self.add("pool", fn, r, w)

    def dma(self, fn, r=(), w=(), q="sp"):
        return self.add(q, fn, r, w, dma=True)

    def sem(self, key):
        if key not in self.sems:
            self.sems[key] = self.stack.enter_context(self.nc.semaphore("s_%s_%s_%d" % key))
        return self.sems[key]

    def flush(self):
        nc = self.nc
        ops = self.ops
        base = self.done
        new = ops[base:]
        if not new:
            return
        last_w, readers = self.last_w, self.readers
        for op in new:
            op.epos = self.epos[op.eng]
            self.epos[op.eng] += 1
            deps = set()
            for k in op.reads:
                if k in last_w:
                    deps.add(last_w[k])
            for k in op.writes:
                if k in last_w:
                    deps.add(last_w[k])
                for rr in readers.get(k, ()):
                    deps.add(rr)
            deps.discard(op.idx)
            for k in op.writes:
                last_w[k] = op.idx
                readers[k] = []
            for k in op.reads:
                if k not in op.writes:
                    readers.setdefault(k, []).append(op.idx)
            fd = []
            for d in deps:
                if d < base:
                    continue
                p = ops[d]
                if (not p.dma) and (not op.dma) and p.eng == op.eng:
                    if p.eng == "pe":
                        continue
                    if op.epos - p.epos > 3:
                        continue
                fd.append(d)
            if self.pending[op.eng]:
                fd.extend(self.pending[op.eng])
                self.pending[op.eng] = []
            op.deps = sorted(set(fd))
        needs = set()
        for op in new:
            for d in op.deps:
                needs.add(d)
        frontier = []
        lastc = {}
        for op in new:
            if op.dma:
                frontier.append(op.idx)
            elif op.fn is not None:
                lastc[op.eng] = op.idx
        for e, i in lastc.items():
            needs.add(i)
            frontier.append(i)
        for op in new:
            if op.dma:
                slot = self.dcnt[op.eng] % NDMASEM[op.eng]
                op.sig = ("d", op.eng, slot, 16 * (self.dcnt[op.eng] // NDMASEM[op.eng] + 1))
                self.dcnt[op.eng] += 1
            elif op.idx in needs:
                ep = self.cnt[op.eng] // EPOCH
                op.sig = ("c", op.eng, ep, self.cnt[op.eng] % EPOCH + 1)
                self.cnt[op.eng] += 1
            if op.sig is not None:
                self.sem(op.sig[:3])
        sems = self.sems
        seen_all = self.seen

        def run_engine(ename, eng):
            seen = seen_all[ename]
            for op in new:
                if op.eng != ename:
                    continue
                waits = {}
                if op.dma and op.sig[3] > 16:
                    waits[op.sig[:3]] = op.sig[3] - 16
                for d in op.deps:
                    sg = ops[d].sig
                    key = sg[:3]
                    if waits.get(key, 0) < sg[3]:
                        waits[key] = sg[3]
                for key, v in waits.items():
                    if seen.get(key, 0) >= v:
                        continue
                    seen[key] = v
                    eng.wait_ge(sems[key], v)
                ins = op.fn(eng)
                if op.sig is not None and ins is not None:
                    ins.then_inc(sems[op.sig[:3]], 16 if op.dma else 1)

        with nc.Block() as block:
            @block.tensor
            def _(e):
                run_engine("pe", e)

            @block.scalar
            def _(e):
                run_engine("act", e)

            @block.vector
            def _(e):
                run_engine("dve", e)

            @block.gpsimd
            def _(e):
                run_engine("pool", e)

            @block.sync
            def _(e):
                run_engine("sp", e)
        self.done = len(ops)
        for e in ENGS:
            self.pending[e] = list(frontier)


INPUT_SHAPES = {
    "x": [SEQ, D], "c": [D], "ctx": [LC, D], "c_ctx": [D],
    "mod_w": [2, D, 6 * D], "mod_b": [2, 6 * D], "norm1_g": [2, D], "norm2_g": [2, D],
    "ev_w_in": [1, D, EV_IN], "ev_w_out": [1, D, D], "attn_q_gain": [1, 64], "attn_k_gain": [1, 64],
    "rw_mu": [1, 1792], "rw_w0": [1, 2, 512], "rw_w2": [1, 2, 64, 512], "rw_a0": [1, 2, 512],
    "rw_a2": [1, 2, 64, 512], "rw_g2": [1, 128, 512], "rw_k_k": [1, 512], "rw_k_a": [1, 512],
    "rw_r_k": [1, 512], "rw_gn_w": [1, 512], "rw_gn_b": [1, 512],
    "ssd_w_in": [1, D, ODD_IN], "ssd_conv_w": [1, 3, 3072], "ssd_conv_b": [1, 3072],
    "ssd_dt_bias": [1, 2, 32], "ssd_a_log": [1, 2, 32], "ssd_d": [1, 32], "ssd_norm_w": [1, 2048],
    "ssd_w_out": [1, 2048, D], "peer_w_q": [2, D, 2048], "peer_keys": [2, 8, 2, 128, 128],
    "peer_u": [2, 16384, D], "peer_v": [2, 16384, D],
    "k_ident": [128, 128], "k_flip": [128, 128], "k_rope": [L, 128],
    "k_tri": [128, 128], "k_neg4": [128, 512], "k_iota": [128, 256], "k_sel": [2, 128],
}


class Builder:
    def __init__(self, debug_outs=(), stop_after=None):
        self.nc = bass.Bass("TRN2", target_bir_lowering=False)
        self.stack = ExitStack()
        self.sstack = None
        self.P = Prog(self.nc, self.stack)
        self.debug_outs = set(debug_outs)
        self.stop_after = stop_after
        self.inp = {}
        for k, shp in INPUT_SHAPES.items():
            self.inp[k] = self.nc.dram_tensor(k, shp, F32, kind="ExternalInput").ap()
        self.out = self.nc.dram_tensor("out", [SEQ, D], F32, kind="ExternalOutput").ap()
        self.uid = 0

    def sb(self, name, shape, dt=F32):
        st = self.sstack if self.sstack is not None else self.stack
        return st.enter_context(self.nc.sbuf_tensor(name, shape, dt))

    def begin(self):
        self.sstack = ExitStack()

    def end(self):
        self.P.flush()
        self.sstack.close()
        self.sstack = None

    def ps(self, name, shape, dt=F32):
        return self.stack.enter_context(self.nc.psum_tensor(name, shape, dt))

    def dk(self, name):
        return [("dbg", name)] if name in self.debug_outs else []

    def dram(self, name, shape, dt=F32):
        kind = "ExternalOutput" if name in self.debug_outs else "Internal"
        return self.nc.dram_tensor(name, shape, dt, kind=kind).ap()

    def tt(self, out, a, b, op, r, w):
        self.P.dve(lambda e: e.tensor_tensor(out, a, b, op), r, w)

    def ts(self, out, a, s1, s2, op0, op1, r, w, accum=None):
        if op1 is None:
            self.P.dve(lambda e: e.tensor_scalar(out, a, s1, None, op0), r, w)
        elif accum is None:
            self.P.dve(lambda e: e.tensor_scalar(out, a, s1, s2, op0, op1), r, w)
        else:
            self.P.dve(lambda e: e.tensor_scalar(out, a, s1, s2, op0, op1, accum), r, w)

    def stt(self, out, a, s, b, op0, op1, r, w, accum=None):
        if accum is None:
            self.P.dve(lambda e: e.scalar_tensor_tensor(out, a, s, b, op0, op1), r, w)
        else:
            self.P.dve(lambda e: e.scalar_tensor_tensor(out, a, s, b, op0, op1, accum), r, w)

    def actf(self, out, in_, func, r, w, bias=0.0, scale=1.0, accum=None):
        if accum is None:
            self.P.act(lambda e: e.activation(out, in_, func, bias=bias, scale=scale), r, w)
        else:
            self.P.act(lambda e: e.activation(out, in_, func, bias=bias, scale=scale, accum_out=accum), r, w)

    def rsqrt(self, out, in_, scale, bias, r, w):
        self.actf(out, in_, AF.Sqrt, r, w, bias=bias, scale=scale)
        self.P.dve(lambda e: e.reciprocal(out, out), w, w)

    def mm(self, out, lhsT, rhs, start, stop, r, w):
        self.P.pe(lambda e: e.matmul(out, lhsT, rhs, start=start, stop=stop), r, w)

    def tr(self, out, in_, r, w):
        ident = self.ident
        n = in_.shape[0]
        self.P.pe(lambda e: e.transpose(out, in_, ident[0:n, 0:n]), r, w)

    def ld(self, out, in_, r, w, q="sp"):
        self.P.dma(lambda e: e.dma_start(out=out, in_=in_), r, w, q=q)

    def build(self):
        P = self.P
        self.identt = self.sb("identt", [128, 128])
        self.ident = self.identt[:, :]
        self.ld(self.ident, self.inp["k_ident"], [], ["ident"])
        self.flipt = self.sb("flipt", [128, 128])
        self.ld(self.flipt[:, :], self.inp["k_flip"], [], ["flip"])
        self.psA = self.ps("psA", [128, 2048])
        self.psB = self.ps("psB", [128, 1024])
        self.psC = self.ps("psC", [128, 1024])
        self.psA_i = 0
        self.XR = self.dram("XR", [L, D])
        self.MODR = self.dram("MODR", [2, 2, 6 * D])
        self.Z0 = self.dram("Z0", [L, EV_IN])
        self.MIX0 = self.dram("MIX0", [L, D])
        self.NS = 3096
        self.SEQ = [self.dram(f"SEQ{d}", [L, self.NS]) for d in range(2)]
        self.HIST = [self.dram(f"HIST{d}", [L, 1024]) for d in range(2)]
        self.Y = [self.dram(f"Y{d}", [L, 512]) for d in range(2)]
        self.G = self.dram("G", [L, 512])
        self.H2 = self.dram("H2", [L, D])
        self.Z1 = self.dram("Z1", [L, ODD_IN])
        self.XBC = self.dram("XBC", [L, 3072])
        self.YD = [self.dram(f"YD{d}", [L, 2048]) for d in range(2)]
        self.MIX1 = self.dram("MIX1", [SEQ, 2048])
        self.IDX = self.dram("IDX", [L, 128], U32)
        self.GATE = self.dram("GATE", [L, 128])

        self.colmod = self.sb("colmod", [128, 2, 6, 8])
        self.gate = [[self.sb(f"gate{w}{m}", [128, D]) for m in range(2)] for w in range(2)]
        self.ng = self.sb("ng", [128, 2, 8])
        self.Acol = self.sb("Acol", [128, 2, 2, 8])
        self.nm_junk = self.sb("nm_junk", [128, D])
        self.nm_ss = self.sb("nm_ss", [128, 1])
        self.nm_xn = self.sb("nm_xn", [128, D])
        self.cur_x_is_input = True
        stages = [
            ("mod", lambda: self.stage_mod()),
            ("inproj0", lambda: self.stage_inproj(0, self.inp["ev_w_in"][0], EV_IN, self.Z0)),
            ("attn", lambda: self.stage_attn()),
            ("rwprep", lambda: self.stage_rw_prep()),
            ("rwscan", lambda: self.stage_rw_scan()),
            ("rwpost", lambda: self.stage_rw_post()),
            ("rwcomb", lambda: self.stage_rw_comb()),
            ("outproj0", lambda: self.stage_outproj(0, self.MIX0, ("MIXa", "MIXb"), D, self.inp["ev_w_out"][0], 0, True)),
            ("route0", lambda: self.stage_peer_route(0, 0)),
            ("expert0", lambda: self.stage_peer_expert(0, 0, False)),
            ("inproj1", lambda: self.stage_inproj(1, self.inp["ssd_w_in"][0], ODD_IN, self.Z1)),
            ("ssdconv", lambda: self.stage_ssd_conv()),
            ("ssdscan", lambda: self.stage_ssd_scan()),
            ("ssdcomb", lambda: self.stage_ssd_comb()),
            ("outproj1", lambda: self.stage_outproj(1, self.MIX1, ("MIX1",), 2048, self.inp["ssd_w_out"][0], 2, False)),
            ("route1", lambda: self.stage_peer_route(1, 2)),
            ("expert1", lambda: self.stage_peer_expert(1, 2, True)),
        ]
        for name, fn in stages:
            self.begin()
            fn()
            self.end()
            if name == "outproj0":
                self.cur_x_is_input = False
            if self.stop_after == name:
                break
        return self.finish()

    def finish(self):
        P = self.P
        P.flush()
        P.add("sp", lambda e: None, r=[], w=[])
        P.flush()
        return self.nc

    def xsrc(self, t):
        if self.cur_x_is_input:
            if t < 2:
                return self.inp["ctx"][t * 128:(t + 1) * 128, :]
            return self.inp["x"][(t - 2) * 128:(t - 1) * 128, :]
        return self.XR[t * 128:(t + 1) * 128, :]

    def stage_mod(self):
        P = self.P
        craw = self.sb("craw", [128, 2, 8])
        sc2 = self.sb("sc2", [128, 8, 2])
        self.ld(craw[:, 0, :], self.inp["c"].rearrange("(p c) -> p c", c=8), [], ["craw"])
        self.ld(craw[:, 1, :], self.inp["c_ctx"].rearrange("(p c) -> p c", c=8), [], ["craw"])
        self.actf(sc2[:, :, :].rearrange("p c w -> p w c"), craw[:, :, :], AF.Silu, ["craw"], ["sc2"])
        wt = [self.sb(f"modw{s}", [128, 3072]) for s in range(2)]
        mb = self.sb("modb", [2, 3072])
        rr = self.sb("modr", [2, 3072])
        n = 0
        for i in range(2):
            wv = self.inp["mod_w"][i].rearrange("(p c) n -> c p n", c=8)
            for hf in range(2):
                cs = slice(hf * 3072, (hf + 1) * 3072)
                self.ld(mb[:, :], self.inp["mod_b"][i:i + 1, cs].to_broadcast([2, 3072]), [], ["modb"])
                for kc in range(8):
                    s = n % 2
                    n += 1
                    self.ld(wt[s][:, :], wv[kc][:, cs], [], [f"modw{s}"], q=("sp" if s == 0 else "pool"))
                    for j in range(6):
                        o = self.psA[0:2, j * 512:(j + 1) * 512] if j < 4 else self.psB[0:2, (j - 4) * 512:(j - 3) * 512]
                        key = f"psA{j}" if j < 4 else f"psB{j - 4}"
                        self.mm(o, sc2[:, kc, :], wt[s][:, j * 512:(j + 1) * 512], kc == 0, kc == 7,
                                ["sc2", f"modw{s}"], [key])
                for j in range(6):
                    o = self.psA[0:2, j * 512:(j + 1) * 512] if j < 4 else self.psB[0:2, (j - 4) * 512:(j - 3) * 512]
                    key = f"psA{j}" if j < 4 else f"psB{j - 4}"
                    self.tt(rr[:, j * 512:(j + 1) * 512], o, mb[:, j * 512:(j + 1) * 512], ALU.add,
                            [key, "modb"], ["modr"])
                self.ld(self.MODR[i][:, cs], rr[:, :], ["modr"], [("MODR", i)] + self.dk("MODR"))

    def load_mod(self, i):
        for who in range(2):
            self.ld(self.colmod[:, who, :, :],
                    self.MODR[i, who].rearrange("(m p c) -> p m c", m=6, p=128, c=8),
                    [("MODR", i)], ["colmod"])
            for m in range(2):
                self.ld(self.gate[who][m][:, :],
                        self.MODR[i, who:who + 1, (2 + 3 * m) * D:(3 + 3 * m) * D].to_broadcast([128, D]),
                        [("MODR", i)], [f"gate{who}{m}"], q="pool")
        self.ld(self.ng[:, 0, :], self.inp["norm1_g"][i].rearrange("(p c) -> p c", c=8), [], ["ng"])
        self.ld(self.ng[:, 1, :], self.inp["norm2_g"][i].rearrange("(p c) -> p c", c=8), [], ["ng"])
        for k in range(2):
            for who in range(2):
                self.stt(self.Acol[:, k, who, :], self.colmod[:, who, 1 + 3 * k, :], 1.0, self.ng[:, k, :],
                         ALU.add, ALU.mult, ["colmod", "ng"], ["Acol"])

    def norm_mod_T(self, xt, xkey, k, who, hT, hkey, pskey="psB"):
        junk, ss, xn = self.nm_junk, self.nm_ss, self.nm_xn
        self.actf(junk[:, :], xt, AF.Square, [xkey], ["nm_junk", "nm_ss"], accum=ss[:, :])
        self.rsqrt(ss[:, :], ss[:, :], 1.0 / D, EPS, ["nm_ss"], ["nm_ss"])
        self.ts(xn[:, :], xt, ss[:, 0:1], None, ALU.mult, None, [xkey, "nm_ss"], ["nm_xn"])
        pst = self.psB if pskey == "psB" else self.psC
        xv = xn[:, :].rearrange("p (q c) -> p c q", c=8)
        for c in range(8):
            self.tr(pst[:, c * 128:(c + 1) * 128], xv[:, c, :], ["nm_xn", "ident"], [pskey + str(c // 4)])
        for c in range(8):
            A = self.Acol[:, k, who, c:c + 1]
            B = self.colmod[:, who, 3 * k, c:c + 1]
            self.actf(hT[:, c, :], pst[:, c * 128:(c + 1) * 128], AF.Identity,
                      [pskey + str(c // 4), "Acol", "colmod"], [hkey], bias=B, scale=A)

    def stage_inproj(self, i, W, N, Z):
        P = self.P
        self.load_mod(i)
        GW = 2560 if N == EV_IN else 2592
        ng = N // GW
        wsb = self.sb(f"win{i}", [128, 8, GW])
        xts = [self.sb(f"ip_x{i}{s}", [128, D]) for s in range(2)]
        hT = self.sb(f"ip_hT{i}", [128, 8, 128])
        zt = [self.sb(f"ip_z{i}{s}", [128, GW]) for s in range(2)]
        Wv = W.rearrange("(p c) n -> p c n", c=8)
        for g in range(ng):
            for c in range(8):
                self.ld(wsb[:, c, :], Wv[:, c, g * GW:(g + 1) * GW], [], ["wsb"], q=("sp" if c % 2 == 0 else "pool"))
            for t in range(NT):
                s = t % 2
                who = 1 if t < 2 else 0
                self.ld(xts[s][:, :], self.xsrc(t), [("XR", t)], [f"ip_x{s}"])
                self.norm_mod_T(xts[s][:, :], f"ip_x{s}", 0, who, hT, "ip_hT")
                nch = (GW + 511) // 512
                for j in range(nch):
                    w = min(512, GW - j * 512)
                    b = self.psA_i % 4
                    self.psA_i += 1
                    for c in range(8):
                        self.mm(self.psA[:, b * 512:b * 512 + w], hT[:, c, :], wsb[:, c, j * 512:j * 512 + w],
                                c == 0, c == 7, ["ip_hT", "wsb"], [f"psA{b}"])
                    if j % 2 == 0:
                        self.P.dve(lambda e, o=zt[s][:, j * 512:j * 512 + w], a=self.psA[:, b * 512:b * 512 + w]:
                                   e.tensor_copy(o, a), [f"psA{b}"], [f"ip_z{s}"])
                    else:
                        self.P.act(lambda e, o=zt[s][:, j * 512:j * 512 + w], a=self.psA[:, b * 512:b * 512 + w]:
                                   e.copy(o, a), [f"psA{b}"], [f"ip_z{s}"])
                self.ld(Z[t * 128:(t + 1) * 128, g * GW:(g + 1) * GW], zt[s][:, :], [f"ip_z{s}"],
                        [("Z", i, t)] + self.dk("Z%d" % i), q="pool")

    def stage_attn(self):
        P = self.P
        Z = self.Z0
        QT = self.sb("at_QT", [64, NT, 8, 128])
        KT = self.sb("at_KT", [64, 2, L])
        V1 = self.sb("at_V1", [128, NT, 2, 65])
        gq = self.sb("at_g", [128, 10, 64])
        zq = [self.sb(f"at_zq{s}", [128, 768]) for s in range(2)]
        rt = [self.sb(f"at_rt{s}", [128, 128]) for s in range(2)]
        sq = self.sb("at_sq", [128, 640])
        ss = self.sb("at_ss", [128, 10])
        qa = self.sb("at_qa", [128, 640])
        qb = self.sb("at_qb", [128, 640])
        for h in range(10):
            src = self.inp["attn_q_gain"] if h < 8 else self.inp["attn_k_gain"]
            self.ld(gq[:, h, :], src[0:1, :].to_broadcast([128, 64]), [], ["at_g"], q="pool")
        P.dve(lambda e: e.memset(V1[:, :, :, 64:65], 1.0), [], ["at_V1"])
        for t in range(NT):
            s = t % 2
            self.ld(zq[s][:, :], Z[t * 128:(t + 1) * 128, 0:768], [("Z", 0, t)], [f"at_zq{s}"])
            self.ld(rt[s][:, :], self.inp["k_rope"][t * 128:(t + 1) * 128, :], [], [f"at_rt{s}"], q="pool")
            qk = zq[s][:, 0:640]
            qk3 = qk.rearrange("p (h d) -> p h d", d=64)
            self.tt(sq[:, :], qk, qk, ALU.mult, [f"at_zq{s}"], ["at_sq"])
            P.dve(lambda e, o=ss[:, :], i=sq[:, :].rearrange("p (h d) -> p h d", d=64): e.tensor_reduce(o, i, AX.X, ALU.add),
                  ["at_sq"], ["at_ss"])
            self.rsqrt(ss[:, :], ss[:, :], 1.0 / 64, EPS, ["at_ss"], ["at_ss"])
            qa3 = qa[:, :].rearrange("p (h d) -> p h d", d=64)
            self.tt(qa3, qk3, ss[:, :].unsqueeze(2).to_broadcast([128, 10, 64]), ALU.mult, [f"at_zq{s}", "at_ss"], ["at_qa"])
            self.tt(qa3, qa3, gq[:, :, :], ALU.mult, ["at_qa", "at_g"], ["at_qa"])
            Cb = rt[s][:, 0:64].unsqueeze(1).to_broadcast([128, 10, 64])
            qb3 = qb[:, :].rearrange("p (h d) -> p h d", d=64)
            self.tt(qb3, qa3, Cb, ALU.mult, ["at_qa", f"at_rt{s}"], ["at_qb"])
            qa5 = qa[:, :].rearrange("p (h a x m) -> p h a x m", a=2, x=2, m=16)
            sq5 = sq[:, :].rearrange("p (h a x m) -> p h a x m", a=2, x=2, m=16)
            S4 = rt[s][:, 64:128].rearrange("p (a x m) -> p a x m", a=2, x=2)
            for x in range(2):
                Sb = S4[:, :, x, :].unsqueeze(1).to_broadcast([128, 10, 2, 16])
                self.tt(sq5[:, :, :, x, :], qa5[:, :, :, 1 - x, :], Sb, ALU.mult, ["at_qa", f"at_rt{s}"], ["at_sq"])
            self.tt(qb[:, :], qb[:, :], sq[:, :], ALU.add, ["at_qb", "at_sq"], ["at_qb"])
            for h in range(8):
                self.tr(self.psB[0:64, h * 128:(h + 1) * 128], qb[:, h * 64:(h + 1) * 64], ["at_qb", "ident"], [f"psB{h // 4}"])
            for g in range(2):
                self.tr(self.psC[0:64, g * 128:(g + 1) * 128], qb[:, 512 + g * 64:576 + g * 64], ["at_qb", "ident"], ["psC0"])
            P.act(lambda e, o=QT[:, t, :, :], i=self.psB[0:64, :].rearrange("p (h q) -> p h q", q=128): e.copy(o, i),
                  ["psB0", "psB1"], [("at_QT", t)])
            P.dve(lambda e, o=KT[:, :, t * 128:(t + 1) * 128], i=self.psC[0:64, 0:256].rearrange("p (g q) -> p g q", q=128):
                  e.tensor_copy(o, i), ["psC0"], [("at_KT", t)])
            P.act(lambda e, o=V1[:, t, :, 0:64], i=zq[s][:, 640:768].rearrange("p (g d) -> p g d", d=64): e.copy(o, i),
                  [f"at_zq{s}", "at_V1"], [("at_V1", t)])
        pt = [self.sb(f"at_pt{s}", [128, 512]) for s in range(3)]
        ot = [self.sb(f"at_ot{s}", [128, 512]) for s in range(2)]
        rc = self.sb("at_rc", [128, 8])
        accs = [(self.psB, 0, "psB0"), (self.psB, 512, "psB1"), (self.psC, 0, "psC0"), (self.psC, 512, "psC1")]
        n = 0
        for t in range(NT):
            kcs = [0, 1] if t < 2 else list(range(NT))
            so = t % 2
            for g in range(2):
                for ki, kc in enumerate(kcs):
                    b = self.psA_i % 4
                    self.psA_i += 1
                    sp_ = n % 3
                    n += 1
                    self.mm(self.psA[:, b * 512:(b + 1) * 512], KT[:, g, kc * 128:(kc + 1) * 128],
                            QT[:, t, g * 4:(g + 1) * 4, :], True, True, [("at_KT", kc), ("at_QT", t)], [f"psA{b}"])
                    self.actf(pt[sp_][:, :], self.psA[:, b * 512:(b + 1) * 512], AF.Exp, [f"psA{b}"], [f"at_pt{sp_}"], scale=0.125)
                    for h in range(4):
                        pst, off, key = accs[h]
                        self.mm(pst[:, off:off + 65], pt[sp_][:, h * 128:(h + 1) * 128], V1[:, kc, g, :],
                                ki == 0, ki == len(kcs) - 1, [f"at_pt{sp_}", ("at_V1", kc)], [key])
                for h in range(4):
                    pst, off, key = accs[h]
                    hh = g * 4 + h
                    P.dve(lambda e, o=rc[:, hh:hh + 1], i=pst[:, off + 64:off + 65]: e.reciprocal(o, i), [key], ["at_rc"])
                    self.ts(ot[so][:, hh * 64:(hh + 1) * 64], pst[:, off:off + 64], rc[:, hh:hh + 1], None,
                            ALU.mult, None, [key, "at_rc"], [f"at_ot{so}"])
            self.ld(self.MIX0[t * 128:(t + 1) * 128, 0:512], ot[so][:, :], [f"at_ot{so}"],
                    [("MIXa", t)] + self.dk("MIX0"), q="pool")

    @staticmethod
    def tmap(d, j):
        if d == 0:
            return j
        return 1 - j if j < 2 else 19 - j

    def bc(self, src_row, n):
        return src_row.to_broadcast([128, n])

    def stage_rw_prep(self):
        P = self.P
        ZO = 768
        Z = self.Z0
        NS = self.NS
        mu_b = self.sb("rw_mub", [128, 1792])
        kkb = self.sb("rw_kkb", [128, 512])
        kab = self.sb("rw_kab", [128, 512])
        rkb = self.sb("rw_rkb", [128, 512])
        g2 = self.sb("rw_g2s", [128, 512])
        W2 = [self.sb(f"rw_W2{d}", [65, 512]) for d in range(2)]
        A2 = [self.sb(f"rw_A2{d}", [65, 512]) for d in range(2)]
        self.ld(mu_b[:, :], self.bc(self.inp["rw_mu"][0:1, :], 1792), [], ["rwc"])
        self.ld(kkb[:, :], self.bc(self.inp["rw_k_k"][0:1, :], 512), [], ["rwc"], q="pool")
        self.ld(kab[:, :], self.bc(self.inp["rw_k_a"][0:1, :], 512), [], ["rwc"])
        self.ld(rkb[:, :], self.bc(self.inp["rw_r_k"][0:1, :], 512), [], ["rwc"], q="pool")
        self.ld(g2[:, :], self.inp["rw_g2"][0], [], ["rwc"])
        for d in range(2):
            self.ld(W2[d][0:64, :], self.inp["rw_w2"][0, d], [], ["rwc"], q="pool")
            self.ld(W2[d][64:65, :], self.inp["rw_w0"][0, d:d + 1, :], [], ["rwc"])
            self.ld(A2[d][0:64, :], self.inp["rw_a2"][0, d], [], ["rwc"], q="pool")
            self.ld(A2[d][64:65, :], self.inp["rw_a0"][0, d:d + 1, :], [], ["rwc"])
        wtT = self.sb("rw_wtT", [65, 128])
        alT = self.sb("rw_alT", [65, 128])
        sgT = self.sb("rw_sgT", [128, 128])
        P.dve(lambda e: e.memset(wtT[:, :], 1.0), [], ["rw_wtT"])
        P.dve(lambda e: e.memset(alT[:, :], 1.0), [], ["rw_alT"])
        zt = [self.sb(f"rw_zt{s}", [128, 1792]) for s in range(2)]
        zm = [self.sb(f"rw_zm{s}", [128, 1792]) for s in range(2)]
        zp = [self.sb(f"rw_zp{s}", [128, 1792]) for s in range(2)]
        zf = self.sb("rw_zf", [128, 1792])
        seqt = [self.sb(f"rw_seq{s}", [128, NS]) for s in range(2)]
        at = self.sb("rw_at", [128, 512])
        t1 = self.sb("rw_t1", [128, 512])
        t2 = self.sb("rw_t2", [128, 512])
        ss = self.sb("rw_ss", [128, 8])
        gt = [self.sb(f"rw_gt{s}", [128, 512]) for s in range(2)]
        n = 0
        for d in range(2):
            for j in range(NT):
                T = self.tmap(d, j)
                s = n % 2
                n += 1
                r0 = T * 128
                self.ld(zt[s][:, :], Z[r0:r0 + 128, ZO:ZO + 1792], [], [f"rw_zt{s}"])
                P.pool(lambda e, o=zm[s][:, :]: e.memset(o, 0.0), [], [f"rw_zm{s}"])
                P.pool(lambda e, o=zp[s][:, :]: e.memset(o, 0.0), [], [f"rw_zp{s}"])
                if T in (0, 2):
                    self.ld(zm[s][1:128, :], Z[r0:r0 + 127, ZO:ZO + 1792], [], [f"rw_zm{s}"], q="pool")
                else:
                    self.ld(zm[s][:, :], Z[r0 - 1:r0 + 127, ZO:ZO + 1792], [], [f"rw_zm{s}"], q="pool")
                if T in (1, NT - 1):
                    self.ld(zp[s][0:127, :], Z[r0 + 1:r0 + 128, ZO:ZO + 1792], [], [f"rw_zp{s}"])
                else:
                    self.ld(zp[s][:, :], Z[r0 + 1:r0 + 129, ZO:ZO + 1792], [], [f"rw_zp{s}"])
                self.tt(zm[s][:, :], zm[s][:, :], zp[s][:, :], ALU.add, [f"rw_zm{s}", f"rw_zp{s}"], [f"rw_zm{s}"])
                self.stt(zm[s][:, :], zm[s][:, :], 0.5, zt[s][:, :], ALU.mult, ALU.subtract, [f"rw_zm{s}", f"rw_zt{s}"], [f"rw_zm{s}"])
                self.tt(zm[s][:, :], zm[s][:, :], mu_b[:, :], ALU.mult, [f"rw_zm{s}", "rwc"], [f"rw_zm{s}"])
                self.tt(zt[s][:, :], zt[s][:, :], zm[s][:, :], ALU.add, [f"rw_zm{s}", f"rw_zt{s}"], [f"rw_zt{s}"])
                zz, zk = zt[s], f"rw_zt{s}"
                if d == 1:
                    for q in range(4):
                        w = 512 if q < 3 else 256
                        b = self.psA_i % 4
                        self.psA_i += 1
                        self.mm(self.psA[:, b * 512:b * 512 + w], self.flipt[:, :], zt[s][:, q * 512:q * 512 + w], True, True,
                                [zk, "flip"], [f"psA{b}"])
                        if q % 2 == 0:
                            P.act(lambda e, o=zf[:, q * 512:q * 512 + w], i=self.psA[:, b * 512:b * 512 + w]: e.copy(o, i),
                                  [f"psA{b}"], ["rw_zf"])
                        else:
                            P.dve(lambda e, o=zf[:, q * 512:q * 512 + w], i=self.psA[:, b * 512:b * 512 + w]: e.tensor_copy(o, i),
                                  [f"psA{b}"], ["rw_zf"])
                    zz, zk = zf, "rw_zf"
                r_ = zz[:, 0:512]
                k_ = zz[:, 512:1024]
                v_ = zz[:, 1024:1536]
                sq = seqt[s]
                sk = f"rw_seq{s}"
                self.tr(self.psB[0:64, 0:128], zz[:, 1536:1600], [zk, "ident"], ["psB0"])
                self.tr(self.psB[0:64, 128:256], zz[:, 1600:1664], [zk, "ident"], ["psB0"])
                self.tr(self.psB[:, 256:384], zz[:, 1664:1792], [zk, "ident"], ["psB0"])
                self.actf(wtT[0:64, :], self.psB[0:64, 0:128], AF.Tanh, ["psB0"], ["rw_wtT"])
                P.dve(lambda e, o=alT[0:64, :], i=self.psB[0:64, 128:256]: e.tensor_copy(o, i), ["psB0"], ["rw_alT"])
                self.actf(sgT[:, :], self.psB[:, 256:384], AF.Sigmoid, ["psB0"], ["rw_sgT"])
                b1 = self.psA_i % 4
                b2 = (self.psA_i + 1) % 4
                b3 = (self.psA_i + 2) % 4
                self.psA_i += 3
                self.mm(self.psA[:, b1 * 512:(b1 + 1) * 512], wtT[:, :], W2[d][:, :], True, True, ["rw_wtT", "rwc"], [f"psA{b1}"])
                self.mm(self.psA[:, b2 * 512:(b2 + 1) * 512], alT[:, :], A2[d][:, :], True, True, ["rw_alT", "rwc"], [f"psA{b2}"])
                self.actf(t1[:, :], self.psA[:, b1 * 512:(b1 + 1) * 512], AF.Sigmoid, [f"psA{b1}"], ["rw_t1"])
                self.actf(sq[:, 1536:2048], t1[:, :], AF.Exp, ["rw_t1"], [sk], scale=-math.exp(-0.5))
                self.actf(at[:, :], self.psA[:, b2 * 512:(b2 + 1) * 512], AF.Sigmoid, [f"psA{b2}"], ["rw_at"])
                if d == 0:
                    sg = j % 2
                    self.mm(self.psA[:, b3 * 512:(b3 + 1) * 512], sgT[:, :], g2[:, :], True, True, ["rw_sgT", "rwc"], [f"psA{b3}"])
                    P.act(lambda e, o=gt[sg][:, :], i=self.psA[:, b3 * 512:(b3 + 1) * 512]: e.copy(o, i), [f"psA{b3}"], [f"rw_gt{sg}"])
                    self.ld(self.G[r0:r0 + 128, :], gt[sg][:, :], [f"rw_gt{sg}"], [("G", T)], q="pool")
                self.tt(t1[:, :], k_, kkb[:, :], ALU.mult, [zk, "rwc", "rw_t1"], ["rw_t1"])
                self.tt(t2[:, :], t1[:, :], t1[:, :], ALU.mult, ["rw_t1"], ["rw_t2"])
                P.dve(lambda e, o=ss[:, :], i=t2[:, :].rearrange("p (h d) -> p h d", d=64): e.tensor_reduce(o, i, AX.X, ALU.add),
                      ["rw_t2"], ["rw_ss"])
                self.actf(ss[:, :], ss[:, :], AF.Sqrt, ["rw_ss"], ["rw_ss"])
                self.ts(ss[:, :], ss[:, :], 1e-12, None, ALU.max, None, ["rw_ss"], ["rw_ss"])
                P.dve(lambda e, o=ss[:, :]: e.reciprocal(o, o), ["rw_ss"], ["rw_ss"])
                self.tt(sq[:, 0:512].rearrange("p (h d) -> p h d", d=64), t1[:, :].rearrange("p (h d) -> p h d", d=64),
                        ss[:, :].unsqueeze(2).to_broadcast([128, 8, 64]), ALU.mult, ["rw_t1", "rw_ss"], [sk])
                self.tt(sq[:, 512:1024], sq[:, 1536:2048], r_, ALU.mult, [sk, zk], [sk])
                self.tt(sq[:, 2048:2560], sq[:, 0:512], at[:, :], ALU.mult, [sk, "rw_at"], [sk])
                self.stt(t2[:, :], at[:, :], -1.0, kab[:, :], ALU.add, ALU.mult, ["rw_at", "rwc", "rw_t2"], ["rw_t2"])
                self.stt(sq[:, 1024:1536], t2[:, :], 1.0, k_, ALU.add, ALU.mult, ["rw_t2", zk], [sk])
                P.act(lambda e, o=sq[:, 2560:3072], i=v_: e.copy(o, i), [zk], [sk])
                self.tt(t1[:, :], sq[:, 2048:2560], r_, ALU.mult, [sk, zk, "rw_t1"], ["rw_t1"])
                P.dve(lambda e, o=sq[:, 3072:3080], i=t1[:, :].rearrange("p (h d) -> p h d", d=64): e.tensor_reduce(o, i, AX.X, ALU.add),
                      ["rw_t1"], [sk])
                self.tt(t1[:, :], sq[:, 1024:1536], r_, ALU.mult, [sk, zk, "rw_t1"], ["rw_t1"])
                P.dve(lambda e, o=sq[:, 3080:3088], i=t1[:, :].rearrange("p (h d) -> p h d", d=64): e.tensor_reduce(o, i, AX.X, ALU.add),
                      ["rw_t1"], [sk])
                self.tt(t1[:, :], t1[:, :], rkb[:, :], ALU.mult, ["rw_t1", "rwc"], ["rw_t1"])
                P.dve(lambda e, o=sq[:, 3088:3096], i=t1[:, :].rearrange("p (h d) -> p h d", d=64): e.tensor_reduce(o, i, AX.X, ALU.add),
                      ["rw_t1"], [sk])
                self.ld(self.SEQ[d][j * 128:(j + 1) * 128, :], sq[:, :], [sk], [("SEQ", d, j)] + self.dk("SEQ%d" % d), q="pool")

    def stage_rw_scan(self):
        P = self.P
        NS = self.NS
        CS, NB = 2, 4
        RS, NR = 4, 3
        S = self.sb("sc_S", [128, 512])
        prod = self.sb("sc_prod", [128, 1024])
        tmp = self.sb("sc_tmp", [128, 512])
        hist = [self.sb(f"sc_hist{s}", [128, 128, 16]) for s in range(2)]
        Bt = [self.sb(f"sc_B{s}", [128, CS, 1536]) for s in range(NB)]
        VK = [self.sb(f"sc_VK{s}", [128, CS, 512]) for s in range(NB)]
        Rt = [self.sb(f"sc_R{s}", [2, RS, 1024]) for s in range(NR)]
        sel = self.sb("sc_sel", [2, 128])
        vcat = self.sb("sc_vcat", [128, 8, 2, 64])
        vT = self.sb("sc_vT", [128, 8, 128])
        HT = self.sb("sc_HT", [128, 16, 128])
        self.ld(sel[:, :], self.inp["k_sel"], [], ["sc_sel"])
        P.dve(lambda e: e.memset(S[:, :], 0.0), [], ["sc_S"])
        banks = [(self.psA, 0, "psA0"), (self.psA, 512, "psA1"), (self.psA, 1024, "psA2"), (self.psA, 1536, "psA3"),
                 (self.psC, 0, "psC0"), (self.psC, 512, "psC1")]
        nchunk = 0
        nr = 0
        nbank = 0
        for jb in range(NT):
            hb = jb % 2
            H = hist[hb]
            hk = f"sc_hist{hb}"
            for d in range(2):
                self.ld(vcat[:, :, d, :], self.SEQ[d][jb * 128:(jb + 1) * 128, 2560:3072].rearrange("p (h v) -> p h v", v=64),
                        [], ["sc_vcat"], q=("sp" if d == 0 else "pool"))
            for h in range(8):
                self.tr(self.psB[:, h * 128:(h + 1) * 128], vcat[:, h, :, :].rearrange("p d v -> p (d v)"),
                        ["sc_vcat", "ident"], [f"psB{h // 4}"])
            for hh in range(2):
                P.act(lambda e, o=vT[:, hh * 4:(hh + 1) * 4, :], i=self.psB[:, hh * 512:(hh + 1) * 512].rearrange("p (h q) -> p h q", q=128):
                      e.copy(o, i), [f"psB{hh}"], ["sc_vT"])
            for st in range(128):
                pos = jb * 128 + st
                if st % RS == 0:
                    rs = nr % NR
                    nr += 1
                    for d in range(2):
                        self.ld(Rt[rs][d:d + 1, :, :], self.SEQ[d][pos:pos + RS, 1536:2560].unsqueeze(0), [], [("sc_R", rs)],
                                q=("sp" if d == 0 else "pool"))
                if st % CS == 0:
                    sl = nchunk % NB
                    nchunk += 1
                    for d in range(2):
                        src = self.SEQ[d][pos:pos + CS, 0:1536]
                        srcb = bass.AP(src.tensor, src.offset, [[0, 64], [NS, CS], [1, 1536]])
                        self.ld(Bt[sl][d * 64:(d + 1) * 64, :, :], srcb, [], [("sc_B", sl)], q=("sp" if d == 0 else "pool"))
                    for kk_ in range(CS):
                        P.pool(lambda e, o=VK[sl][:, kk_, :].rearrange("p (h k) -> p h k", k=64),
                               a=vT[:, :, st + kk_:st + kk_ + 1].to_broadcast([128, 8, 64]),
                               b=Bt[sl][:, kk_, 1024:1536].rearrange("p (h k) -> p h k", k=64): e.tensor_tensor(o, a, b, ALU.mult),
                               ["sc_vT", ("sc_B", sl)], [("sc_VK", sl)])
                k = st % CS
                Bk = Bt[sl][:, k, :]
                pw, ow, kw = banks[nbank % 6]
                pa, oa, ka_ = banks[(nbank + 1) % 6]
                nbank += 2
                self.mm(pw[:, ow:ow + 512], sel[:, :], Rt[rs][:, st % RS, 0:512], True, True, ["sc_sel", ("sc_R", rs)], [kw])
                self.mm(pa[:, oa:oa + 512], sel[:, :], Rt[rs][:, st % RS, 512:1024], True, True, ["sc_sel", ("sc_R", rs)], [ka_])
                self.tt(prod[:, :].rearrange("p (a n) -> p a n", a=2), S[:, :].unsqueeze(1).to_broadcast([128, 2, 512]),
                        Bk[:, 0:1024].rearrange("p (a n) -> p a n", a=2), ALU.mult, ["sc_S", ("sc_B", sl)], ["sc_prod"])
                P.dve(lambda e, o=H[:, st, :], i=prod[:, :].rearrange("p (a k) -> p a k", k=64): e.tensor_reduce(o, i, AX.X, ALU.add),
                      ["sc_prod"], [hk])
                self.tt(S[:, :], S[:, :], pw[:, ow:ow + 512], ALU.mult, ["sc_S", kw], ["sc_S"])
                self.tt(tmp[:, :].rearrange("p (h k) -> p h k", k=64), H[:, st, 0:8].unsqueeze(2).to_broadcast([128, 8, 64]),
                        pa[:, oa:oa + 512].rearrange("p (h k) -> p h k", k=64), ALU.mult, [hk, ka_], ["sc_tmp"])
                self.tt(S[:, :], S[:, :], tmp[:, :], ALU.subtract, ["sc_S", "sc_tmp"], ["sc_S"])
                self.tt(S[:, :], S[:, :], VK[sl][:, k, :], ALU.add, ["sc_S", ("sc_VK", sl)], ["sc_S"])
            for q in range(16):
                self.tr(self.psA[:, q * 128:(q + 1) * 128], H[:, :, q], [hk, "ident"], [f"psA{q // 4}"])
            for b in range(4):
                src = self.psA[:, b * 512:(b + 1) * 512].rearrange("p (q m) -> p q m", m=128)
                if b % 2 == 0:
                    P.act(lambda e, o=HT[:, b * 4:(b + 1) * 4, :], i=src: e.copy(o, i), [f"psA{b}"], ["sc_HT"])
                else:
                    P.dve(lambda e, o=HT[:, b * 4:(b + 1) * 4, :], i=src: e.tensor_copy(o, i), [f"psA{b}"], ["sc_HT"])
            for d in range(2):
                self.ld(self.HIST[d][jb * 128:(jb + 1) * 128, :].rearrange("p (q v) -> p q v", v=64),
                        HT[:, :, d * 64:(d + 1) * 64], ["sc_HT"], [("HIST", d, jb)] + self.dk("HIST%d" % d), q="pool")

    def stage_rw_post(self):
        P = self.P
        gwb = self.sb("rp_gw", [128, 512])
        gbb = self.sb("rp_gb", [128, 512])
        self.ld(gwb[:, :], self.bc(self.inp["rw_gn_w"][0:1, :], 512), [], ["rpc"])
        self.ld(gbb[:, :], self.bc(self.inp["rw_gn_b"][0:1, :], 512), [], ["rpc"], q="pool")
        ht = [self.sb(f"rp_ht{s}", [128, 16, 64]) for s in range(2)]
        sv = [self.sb(f"rp_sv{s}", [128, 536]) for s in range(2)]
        o = self.sb("rp_o", [128, 8, 64])
        t = self.sb("rp_t", [128, 8, 64])
        m = self.sb("rp_m", [128, 8])
        yt = [self.sb(f"rp_y{s}", [128, 512]) for s in range(2)]
        n = 0

        def b8(ap):
            return ap.unsqueeze(2).to_broadcast([128, 8, 64])

        for d in range(2):
            for j in range(NT):
                s = n % 2
                n += 1
                hk, vk, yk = f"rp_ht{s}", f"rp_sv{s}", f"rp_y{s}"
                self.ld(ht[s][:, :, :], self.HIST[d][j * 128:(j + 1) * 128, :].rearrange("p (q v) -> p q v", v=64), [], [hk])
                self.ld(sv[s][:, :], self.SEQ[d][j * 128:(j + 1) * 128, 2560:3096], [], [vk], q="pool")
                v3 = sv[s][:, 0:512].rearrange("p (h v) -> p h v", v=64)
                c1, c2, bs = sv[s][:, 512:520], sv[s][:, 520:528], sv[s][:, 528:536]
                y3 = yt[s][:, :].rearrange("p (h v) -> p h v", v=64)
                self.tt(t[:, :, :], ht[s][:, 0:8, :], b8(c1), ALU.mult, [hk, vk], ["rp_t"])
                self.tt(o[:, :, :], ht[s][:, 8:16, :], t[:, :, :], ALU.subtract, [hk, "rp_t"], ["rp_o"])
                self.tt(t[:, :, :], v3, b8(c2), ALU.mult, [vk, "rp_t"], ["rp_t"])
                self.tt(o[:, :, :], o[:, :, :], t[:, :, :], ALU.add, ["rp_o", "rp_t"], ["rp_o"])
                P.dve(lambda e, o_=m[:, :], i=o[:, :, :]: e.tensor_reduce(o_, i, AX.X, ALU.add), ["rp_o"], ["rp_m"])
                self.ts(m[:, :], m[:, :], 1.0 / 64, None, ALU.mult, None, ["rp_m"], ["rp_m"])
                self.tt(o[:, :, :], o[:, :, :], b8(m[:, :]), ALU.subtract, ["rp_o", "rp_m"], ["rp_o"])
                self.tt(t[:, :, :], o[:, :, :], o[:, :, :], ALU.mult, ["rp_o", "rp_t"], ["rp_t"])
                P.dve(lambda e, o_=m[:, :], i=t[:, :, :]: e.tensor_reduce(o_, i, AX.X, ALU.add), ["rp_t", "rp_m"], ["rp_m"])
                self.rsqrt(m[:, :], m[:, :], 1.0 / 64, 64e-5, ["rp_m"], ["rp_m"])
                self.tt(o[:, :, :], o[:, :, :], b8(m[:, :]), ALU.mult, ["rp_o", "rp_m"], ["rp_o"])
                self.tt(o[:, :, :], o[:, :, :], gwb[:, :].rearrange("p (h v) -> p h v", v=64), ALU.mult, ["rp_o", "rpc"], ["rp_o"])
                self.tt(o[:, :, :], o[:, :, :], gbb[:, :].rearrange("p (h v) -> p h v", v=64), ALU.add, ["rp_o", "rpc"], ["rp_o"])
                self.tt(t[:, :, :], v3, b8(bs), ALU.mult, [vk, "rp_t"], ["rp_t"])
                self.tt(y3, o[:, :, :], t[:, :, :], ALU.add, ["rp_o", "rp_t"], [yk])
                self.ld(self.Y[d][j * 128:(j + 1) * 128, :], yt[s][:, :], [yk], [("Y", d, j)] + self.dk("Y%d" % d), q="pool")

    def stage_rw_comb(self):
        P = self.P
        yf = [self.sb(f"rc_f{s}", [128, 512]) for s in range(2)]
        yb = [self.sb(f"rc_b{s}", [128, 512]) for s in range(2)]
        gg = [self.sb(f"rc_g{s}", [128, 512]) for s in range(2)]
        for T in range(NT):
            s = T % 2
            jb = self.tmap(1, T)
            self.ld(yf[s][:, :], self.Y[0][T * 128:(T + 1) * 128, :], [], [f"rc_f{s}"])
            self.ld(yb[s][:, :], self.Y[1][jb * 128:(jb + 1) * 128, :], [], [f"rc_b{s}"], q="pool")
            self.ld(gg[s][:, :], self.G[T * 128:(T + 1) * 128, :], [], [f"rc_g{s}"])
            b = self.psA_i % 4
            self.psA_i += 1
            self.mm(self.psA[:, b * 512:(b + 1) * 512], self.flipt[:, :], yb[s][:, :], True, True, [f"rc_b{s}", "flip"], [f"psA{b}"])
            self.tt(yf[s][:, :], yf[s][:, :], self.psA[:, b * 512:(b + 1) * 512], ALU.add, [f"rc_f{s}", f"psA{b}"], [f"rc_f{s}"])
            self.tt(yf[s][:, :], yf[s][:, :], gg[s][:, :], ALU.mult, [f"rc_f{s}", f"rc_g{s}"], [f"rc_f{s}"])
            self.ld(self.MIX0[T * 128:(T + 1) * 128, 512:1024], yf[s][:, :], [f"rc_f{s}"], [("MIXb", T)] + self.dk("MIX0"), q="pool")

    def stage_peer_route(self, i, t0):
        P = self.P
        wq = self.sb(f"pr_wq{i}", [128, 8, 2048])
        Wv = self.inp["peer_w_q"][i].rearrange("(p c) n -> p c n", c=8)
        for c in range(8):
            self.ld(wq[:, c, :], Wv[:, c, :], [], ["pr_wq"], q=("sp" if c % 2 == 0 else "pool"))
        kraw = self.sb(f"pr_kraw{i}", [128, 16, 128])
        keysT = self.sb(f"pr_kT{i}", [128, 16, 128])
        self.ld(kraw[:, :, :], self.inp["peer_keys"][i].rearrange("h p n d -> n (h p) d"), [], ["pr_kraw"])
        for hp in range(16):
            pst, key = (self.psB, f"psB{(hp % 8) // 4}")
            self.tr(pst[:, (hp % 8) * 128:(hp % 8 + 1) * 128], kraw[:, hp, :], ["pr_kraw", "ident"], [key])
            if hp % 8 == 7:
                P.act(lambda e, o=keysT[:, hp - 7:hp + 1, :], a=pst[:, :].rearrange("p (c q) -> p c q", q=128): e.copy(o, a),
                      ["psB0", "psB1"], ["pr_kT"])
        xt = [self.sb(f"pr_x{i}{s}", [128, D]) for s in range(2)]
        hT = self.sb(f"pr_hT{i}", [128, 8, 128])
        h2 = [self.sb(f"pr_h2{i}{s}", [128, D]) for s in range(2)]
        qT = self.sb(f"pr_qT{i}", [128, 16, 128])
        sc = self.sb(f"pr_sc{i}", [128, 16, 128])
        sc2 = self.sb(f"pr_sc2{i}", [128, 16, 128])
        sv = self.sb(f"pr_sv{i}", [128, 16, 16])
        si = self.sb(f"pr_si{i}", [128, 16, 16], U32)
        sif = self.sb(f"pr_sif{i}", [128, 16, 16])
        cs = self.sb(f"pr_cs{i}", [128, 8, 256])
        cs2 = self.sb(f"pr_cs2{i}", [128, 8, 256])
        ci = self.sb(f"pr_ci{i}", [128, 8, 256])
        junk = self.sb(f"pr_junk{i}", [128, 256])
        bs = self.sb(f"pr_bs{i}", [128, 8, 16])
        idxf = [self.sb(f"pr_idxf{i}{s}", [128, 128]) for s in range(2)]
        idxu = [self.sb(f"pr_idxu{i}{s}", [128, 128], U32) for s in range(2)]
        gt = [self.sb(f"pr_gt{i}{s}", [128, 8, 16]) for s in range(2)]
        gs = self.sb(f"pr_gs{i}", [128, 8])
        pos = self.sb(f"pr_pos{i}", [128, 8, 16], U32)
        pa = self.sb(f"pr_pa{i}", [128, 8, 16], U32)
        pb = self.sb(f"pr_pb{i}", [128, 8, 16], U32)
        paf = self.sb(f"pr_paf{i}", [128, 8, 16])
        pbf = self.sb(f"pr_pbf{i}", [128, 8, 16])
        E4 = self.sb(f"pr_E4{i}", [128, 8, 16, 16])
        bval = self.sb(f"pr_bval{i}", [128, 128])
        iota = self.sb(f"pr_iota{i}", [128, 256])
        self.ld(iota[:, :], self.inp["k_iota"], [], ["pr_iota"])
        for t in range(t0, NT):
            s = t % 2
            who = 1 if t < 2 else 0
            row = (t - t0) * 128
            self.ld(xt[s][:, :], self.xsrc(t), [("XR", t)], [f"pr_x{s}"])
            self.norm_mod_T(xt[s][:, :], f"pr_x{s}", 1, who, hT, "pr_hT")
            for c in range(8):
                self.tr(self.psC[:, c * 128:(c + 1) * 128], hT[:, c, :], ["pr_hT", "ident"], [f"psC{c // 4}"])
            P.act(lambda e, o=h2[s][:, :].rearrange("t (p c) -> t c p", c=8), a=self.psC[:, :].rearrange("t (c p) -> t c p", p=128):
                  e.copy(o, a), ["psC0", "psC1"], [f"pr_h2{s}"])
            self.ld(self.H2[row:row + 128, :], h2[s][:, :], [f"pr_h2{s}"], [("H2", t)] + self.dk("H2"), q="pool")
            for hp in range(16):
                b = hp // 4
                for c in range(8):
                    self.mm(self.psA[:, hp * 128:(hp + 1) * 128], wq[:, c, hp * 128:(hp + 1) * 128], hT[:, c, :],
                            c == 0, c == 7, ["pr_wq", "pr_hT"], [f"psA{b}"])
            for b in range(4):
                P.act(lambda e, o=qT[:, b * 4:(b + 1) * 4, :], a=self.psA[:, b * 512:(b + 1) * 512].rearrange("p (c q) -> p c q", q=128):
                      e.copy(o, a), [f"psA{b}"], ["pr_qT"])
            for hp in range(16):
                pst, key = (self.psB, f"psB{(hp % 8) // 4}") if hp < 8 else (self.psC, f"psC{(hp % 8) // 4}")
                self.mm(pst[:, (hp % 8) * 128:(hp % 8 + 1) * 128], qT[:, hp, :], keysT[:, hp, :], True, True,
                        ["pr_qT", "pr_kT"], [key])
            for q in range(4):
                pst, key = (self.psB, f"psB{q % 2}") if q < 2 else (self.psC, f"psC{q % 2}")
                P.act(lambda e, o=sc[:, q * 4:(q + 1) * 4, :], a=pst[:, (q % 2) * 512:(q % 2 + 1) * 512].rearrange("p (c q) -> p c q", q=128):
                      e.copy(o, a), [key], ["pr_sc"])
            for hp in range(16):
                P.dve(lambda e, o=sv[:, hp, 0:8], a=sc[:, hp, :]: e.max(o, a), ["pr_sc"], ["pr_sv"])
                P.dve(lambda e, o=si[:, hp, 0:8], m=sv[:, hp, 0:8], a=sc[:, hp, :]: e.max_index(o, m, a), ["pr_sc", "pr_sv"], ["pr_si"])
                P.dve(lambda e, o=sc2[:, hp, :], m=sv[:, hp, 0:8], a=sc[:, hp, :]: e.match_replace(o, m, a, -1e30),
                      ["pr_sc", "pr_sv"], ["pr_sc2"])
                P.dve(lambda e, o=sv[:, hp, 8:16], a=sc2[:, hp, :]: e.max(o, a), ["pr_sc2"], ["pr_sv"])
                P.dve(lambda e, o=si[:, hp, 8:16], m=sv[:, hp, 8:16], a=sc2[:, hp, :]: e.max_index(o, m, a), ["pr_sc2", "pr_sv"], ["pr_si"])
            P.dve(lambda e: e.tensor_copy(sif[:, :, :], si[:, :, :]), ["pr_si"], ["pr_sif"])
            sv4 = sv[:, :, :].rearrange("p (h a) k -> p h a k", a=2)
            sf4 = sif[:, :, :].rearrange("p (h a) k -> p h a k", a=2)
            cs4 = cs[:, :, :].rearrange("p h (a b) -> p h a b", b=16)
            ci4 = ci[:, :, :].rearrange("p h (a b) -> p h a b", b=16)
            self.tt(cs4, sv4[:, :, 0, :].unsqueeze(3).to_broadcast([128, 8, 16, 16]),
                    sv4[:, :, 1, :].unsqueeze(2).to_broadcast([128, 8, 16, 16]), ALU.add, ["pr_sv"], ["pr_cs"])
            self.ts(sf4[:, :, 0, :], sf4[:, :, 0, :], 128.0, None, ALU.mult, None, ["pr_sif"], ["pr_sif"])
            self.tt(ci4, sf4[:, :, 0, :].unsqueeze(3).to_broadcast([128, 8, 16, 16]),
                    sf4[:, :, 1, :].unsqueeze(2).to_broadcast([128, 8, 16, 16]), ALU.add, ["pr_sif"], ["pr_ci"])
            for h in range(8):
                P.dve(lambda e, o=bs[:, h, 0:8], a=cs[:, h, :]: e.max(o, a), ["pr_cs"], ["pr_bs"])
                P.dve(lambda e, o=cs2[:, h, :], m=bs[:, h, 0:8], a=cs[:, h, :]: e.match_replace(o, m, a, -1e30),
                      ["pr_cs", "pr_bs"], ["pr_cs2"])
                P.dve(lambda e, o=bs[:, h, 8:16], a=cs2[:, h, :]: e.max(o, a), ["pr_cs2"], ["pr_bs"])
            for h in range(8):
                P.dve(lambda e, o=pos[:, h, 0:8], m=bs[:, h, 0:8], a=cs[:, h, :]: e.max_index(o, m, a), ["pr_cs", "pr_bs"], ["pr_pos"])
                P.dve(lambda e, o=pos[:, h, 8:16], m=bs[:, h, 8:16], a=cs2[:, h, :]: e.max_index(o, m, a), ["pr_cs2", "pr_bs"], ["pr_pos"])
            P.dve(lambda e: e.tensor_single_scalar(pa[:, :, :], pos[:, :, :], 4, ALU.logical_shift_right), ["pr_pos"], ["pr_pa"])
            P.dve(lambda e: e.tensor_single_scalar(pb[:, :, :], pos[:, :, :], 15, ALU.bitwise_and), ["pr_pos"], ["pr_pb"])
            P.dve(lambda e: e.tensor_copy(paf[:, :, :], pa[:, :, :]), ["pr_pa"], ["pr_paf"])
            P.dve(lambda e: e.tensor_copy(pbf[:, :, :], pb[:, :, :]), ["pr_pb"], ["pr_pbf"])
            io4 = iota[:, 0:16].unsqueeze(1).unsqueeze(1).to_broadcast([128, 8, 16, 16])
            for which, (pf, pk) in enumerate(((paf, "pr_paf"), (pbf, "pr_pbf"))):
                self.tt(E4[:, :, :, :], io4, pf[:, :, :].unsqueeze(3).to_broadcast([128, 8, 16, 16]), ALU.is_equal,
                        ["pr_iota", pk], ["pr_E4"])
                self.tt(E4[:, :, :, :], E4[:, :, :, :], sf4[:, :, which, :].unsqueeze(2).to_broadcast([128, 8, 16, 16]), ALU.mult,
                        ["pr_E4", "pr_sif"], ["pr_E4"])
                P.dve(lambda e, o=(idxf[s][:, :] if which == 0 else bval[:, :]), a=E4[:, :, :, :].rearrange("p h k j -> p (h k) j"):
                      e.tensor_reduce(o, a, AX.X, ALU.add), ["pr_E4"], [f"pr_idxf{s}" if which == 0 else "pr_bval"])
            self.tt(idxf[s][:, :], idxf[s][:, :], bval[:, :], ALU.add, [f"pr_idxf{s}", "pr_bval"], [f"pr_idxf{s}"])
            if i > 0:
                self.ts(idxf[s][:, :], idxf[s][:, :], float(16384 * i), None, ALU.add, None, [f"pr_idxf{s}"], [f"pr_idxf{s}"])
            P.dve(lambda e, o=idxu[s][:, :], a=idxf[s][:, :]: e.tensor_copy(o, a), [f"pr_idxf{s}"], [f"pr_idxu{s}"])
            self.ld(self.IDX[row:row + 128, :], idxu[s][:, :], [f"pr_idxu{s}"], [("IDX", t)] + self.dk("IDX"), q="pool")
            self.tt(gt[s][:, :, :], bs[:, :, :], bs[:, :, 0:1].to_broadcast([128, 8, 16]), ALU.subtract, ["pr_bs"], [f"pr_gt{s}"])
            self.actf(gt[s][:, :, :], gt[s][:, :, :], AF.Exp, [f"pr_gt{s}"], [f"pr_gt{s}"])
            P.dve(lambda e, o=gs[:, :], a=gt[s][:, :, :]: e.tensor_reduce(o, a, AX.X, ALU.add), [f"pr_gt{s}"], ["pr_gs"])
            P.dve(lambda e: e.reciprocal(gs[:, :], gs[:, :]), ["pr_gs"], ["pr_gs"])
            self.tt(gt[s][:, :, :], gt[s][:, :, :], gs[:, :].unsqueeze(2).to_broadcast([128, 8, 16]), ALU.mult,
                    [f"pr_gt{s}", "pr_gs"], [f"pr_gt{s}"])
            self.ld(self.GATE[row:row + 128, :], gt[s][:, :, :].rearrange("p h k -> p (h k)"), [f"pr_gt{s}"],
                    [("GATE", t)] + self.dk("GATE"), q="pool")

    def stage_peer_expert(self, i, t0, final):
        P = self.P
        GS = 8
        Ug = [self.sb(f"pe_U{i}{s}", [128, GS, D]) for s in range(2)]
        Vg = [self.sb(f"pe_V{i}{s}", [128, GS, D]) for s in range(2)]
        h2 = [self.sb(f"pe_h2{i}{s}", [128, D]) for s in range(2)]
        xt = [self.sb(f"pe_x{i}{s}", [128, D]) for s in range(2)]
        idx = [self.sb(f"pe_idx{i}{s}", [128, 128], U32) for s in range(2)]
        gate = [self.sb(f"pe_gate{i}{s}", [128, 128]) for s in range(2)]
        act = self.sb(f"pe_act{i}", [128, 128])
        coef = self.sb(f"pe_coef{i}", [128, 128])
        junk = self.sb(f"pe_junk{i}", [128, D])
        acc = self.sb(f"pe_acc{i}", [128, D])
        U = self.inp["peer_u"].rearrange("l e d -> (l e) d")
        V = self.inp["peer_v"].rearrange("l e d -> (l e) d")
        ng = 0
        for t in range(t0, NT):
            s = t % 2
            who = 1 if t < 2 else 0
            row = (t - t0) * 128
            self.ld(h2[s][:, :], self.H2[row:row + 128, :], [], [f"pe_h2{s}"])
            self.ld(xt[s][:, :], self.xsrc(t), [("XR", t)], [f"pe_x{s}"])
            self.ld(idx[s][:, :], self.IDX[row:row + 128, :], [], [f"pe_idx{s}"])
            self.ld(gate[s][:, :], self.GATE[row:row + 128, :], [], [f"pe_gate{s}"])
            for g in range(128 // GS):
                gb = ng % 2
                ng += 1
                for k in range(GS):
                    sl = g * GS + k
                    P.dma(lambda e, o=Ug[gb][:, k, :], ix=idx[s][:, sl:sl + 1]: e.indirect_dma_start(
                        out=o, out_offset=None, in_=U, in_offset=bass.IndirectOffsetOnAxis(ap=ix, axis=0)),
                        [f"pe_idx{s}"], [("pe_U", gb, k)], q="pool")
                for k in range(GS):
                    sl = g * GS + k
                    P.dma(lambda e, o=Vg[gb][:, k, :], ix=idx[s][:, sl:sl + 1]: e.indirect_dma_start(
                        out=o, out_offset=None, in_=V, in_offset=bass.IndirectOffsetOnAxis(ap=ix, axis=0)),
                        [f"pe_idx{s}"], [("pe_V", gb, k)], q="pool")
                for k in range(GS):
                    sl = g * GS + k
                    self.stt(junk[:, :], Ug[gb][:, k, :], 1.0, h2[s][:, :], ALU.mult, ALU.mult,
                             [("pe_U", gb, k), f"pe_h2{s}"], ["pe_junk", "pe_act"], accum=act[:, sl:sl + 1])
                sl0 = g * GS
                self.actf(coef[:, sl0:sl0 + GS], act[:, sl0:sl0 + GS], AF.Gelu, ["pe_act"], ["pe_coef"])
                self.tt(coef[:, sl0:sl0 + GS], coef[:, sl0:sl0 + GS], gate[s][:, sl0:sl0 + GS], ALU.mult,
                        ["pe_coef", f"pe_gate{s}"], ["pe_coef"])
                for k in range(GS):
                    sl = g * GS + k
                    if sl == 0:
                        self.ts(acc[:, :], Vg[gb][:, k, :], coef[:, 0:1], None, ALU.mult, None, [("pe_V", gb, k), "pe_coef"], ["pe_acc"])
                    else:
                        self.stt(acc[:, :], Vg[gb][:, k, :], coef[:, sl:sl + 1], acc[:, :], ALU.mult, ALU.add,
                                 [("pe_V", gb, k), "pe_coef", "pe_acc"], ["pe_acc"])
            self.tt(acc[:, :], acc[:, :], self.gate[who][1][:, :], ALU.mult, ["pe_acc", f"gate{who}1"], ["pe_acc"])
            self.tt(xt[s][:, :], xt[s][:, :], acc[:, :], ALU.add, [f"pe_x{s}", "pe_acc"], [f"pe_x{s}"])
            if final:
                self.ld(self.out[(t - 2) * 128:(t - 1) * 128, :], xt[s][:, :], [f"pe_x{s}"], [("out", t)], q="sp")
            else:
                self.ld(self.XR[t * 128:(t + 1) * 128, :], xt[s][:, :], [f"pe_x{s}"], [("XR", t)] + self.dk("XR"), q="sp")

    def stage_ssd_conv(self):
        P = self.P
        Z = self.Z1
        W = 1536
        wb = [self.sb(f"sv_w{k}", [128, W]) for k in range(3)]
        bb = self.sb("sv_b", [128, W])
        x0 = [self.sb(f"sv_x0{s}", [128, W]) for s in range(2)]
        xm = [self.sb(f"sv_xm{s}", [128, W]) for s in range(2)]
        xp = [self.sb(f"sv_xp{s}", [128, W]) for s in range(2)]
        n = 0
        for hf in range(2):
            CO = 2048 + hf * W
            for k in range(3):
                self.ld(wb[k][:, :], self.bc(self.inp["ssd_conv_w"][0, k:k + 1, hf * W:(hf + 1) * W], W), [], ["svc"],
                        q=("sp" if k % 2 == 0 else "pool"))
            self.ld(bb[:, :], self.bc(self.inp["ssd_conv_b"][0:1, hf * W:(hf + 1) * W], W), [], ["svc"], q="pool")
            for T in range(NT):
                s = n % 2
                n += 1
                r0 = T * 128
                self.ld(x0[s][:, :], Z[r0:r0 + 128, CO:CO + W], [], [f"sv_x0{s}"])
                P.pool(lambda e, o=xm[s][:, :]: e.memset(o, 0.0), [], [f"sv_xm{s}"])
                P.pool(lambda e, o=xp[s][:, :]: e.memset(o, 0.0), [], [f"sv_xp{s}"])
                if T in (0, 2):
                    self.ld(xm[s][1:128, :], Z[r0:r0 + 127, CO:CO + W], [], [f"sv_xm{s}"], q="pool")
                else:
                    self.ld(xm[s][:, :], Z[r0 - 1:r0 + 127, CO:CO + W], [], [f"sv_xm{s}"], q="pool")
                if T in (1, NT - 1):
                    self.ld(xp[s][0:127, :], Z[r0 + 1:r0 + 128, CO:CO + W], [], [f"sv_xp{s}"])
                else:
                    self.ld(xp[s][:, :], Z[r0 + 1:r0 + 129, CO:CO + W], [], [f"sv_xp{s}"])
                self.tt(xm[s][:, :], xm[s][:, :], wb[0][:, :], ALU.mult, [f"sv_xm{s}", "svc"], [f"sv_xm{s}"])
                self.tt(xp[s][:, :], xp[s][:, :], wb[2][:, :], ALU.mult, [f"sv_xp{s}", "svc"], [f"sv_xp{s}"])
                self.tt(x0[s][:, :], x0[s][:, :], wb[1][:, :], ALU.mult, [f"sv_x0{s}", "svc"], [f"sv_x0{s}"])
                self.tt(xm[s][:, :], xm[s][:, :], xp[s][:, :], ALU.add, [f"sv_xm{s}", f"sv_xp{s}"], [f"sv_xm{s}"])
                self.tt(x0[s][:, :], x0[s][:, :], xm[s][:, :], ALU.add, [f"sv_x0{s}", f"sv_xm{s}"], [f"sv_x0{s}"])
                self.tt(x0[s][:, :], x0[s][:, :], bb[:, :], ALU.add, [f"sv_x0{s}", "svc"], [f"sv_x0{s}"])
                self.actf(x0[s][:, :], x0[s][:, :], AF.Silu, [f"sv_x0{s}"], [f"sv_x0{s}"])
                self.ld(self.XBC[r0:r0 + 128, hf * W:(hf + 1) * W], x0[s][:, :], [f"sv_x0{s}"],
                        [("XBC", T, hf)] + self.dk("XBC"), q="pool")

    def stage_ssd_scan(self):
        P = self.P
        tri = self.sb("ss_tri", [128, 128])
        neg4 = self.sb("ss_neg4", [128, 512])
        ones = self.sb("ss_ones", [128, 128])
        self.ld(tri[:, :], self.inp["k_tri"], [], ["ssc"])
        self.ld(neg4[:, :], self.inp["k_neg4"], [], ["ssc"], q="pool")
        P.dve(lambda e: e.memset(ones[:, :], 1.0), [], ["ss_ones"])
        dtb = [self.sb(f"ss_dtb{d}", [128, 32]) for d in range(2)]
        ab = [self.sb(f"ss_ab{d}", [128, 32]) for d in range(2)]
        for d in range(2):
            self.ld(dtb[d][:, :], self.bc(self.inp["ssd_dt_bias"][0, d:d + 1, :], 32), [], ["ssc"])
            self.ld(ab[d][:, :], self.bc(self.inp["ssd_a_log"][0, d:d + 1, :], 32), [], ["ssc2"], q="pool")
            self.actf(ab[d][:, :], ab[d][:, :], AF.Exp, ["ssc2"], ["ssc2"])
        xbc = [self.sb(f"ss_xbc{s}", [128, 3072]) for s in range(2)]
        xf = self.sb("ss_xf", [128, 3072])
        dtrt = [self.sb(f"ss_dtr{s}", [128, 512]) for s in range(2)]
        dtr = [x[:, 0:32] for x in dtrt]
        dt = self.sb("ss_dt", [128, 32])
        dat = self.sb("ss_da", [128, 512])
        tot = self.sb("ss_tot", [128, 32])
        da = dat[:, 0:32]
        P.dve(lambda e: e.memset(dat[:, :], 0.0), [], ["ss_da"])
        for s_ in range(2):
            P.dve(lambda e, o=dtrt[s_][:, :]: e.memset(o, 0.0), [], [f"ss_dtr{s_}"])
        cum = self.sb("ss_cum", [128, 32])
        ecum = self.sb("ss_ecum", [128, 32])
        de = self.sb("ss_de", [128, 32])
        cd = self.sb("ss_cd", [128, 32])
        xd = self.sb("ss_xd", [128, 2048])
        xde = self.sb("ss_xde", [128, 2048])
        BCT = self.sb("ss_BCT", [128, 8, 128])
        CBm = self.sb("ss_CBm", [128, 4, 128])
        rseg = self.sb("ss_rseg", [128, 4, 128])
        seg = self.sb("ss_seg", [128, 4, 128])
        M = [self.sb(f"ss_M{s}", [128, 4, 128]) for s in range(2)]
        Sst = self.sb("ss_S", [128, 2048])
        yt = [self.sb(f"ss_y{s}", [128, 2048]) for s in range(2)]
        n = 0
        for d in range(2):
            P.dve(lambda e: e.memset(Sst[:, :], 0.0), [], ["ss_S"])
            for j in range(NT):
                T = self.tmap(d, j)
                s = n % 2
                n += 1
                r0 = T * 128
                xk = f"ss_xbc{s}"
                self.ld(xbc[s][:, 0:1536], self.XBC[r0:r0 + 128, 0:1536], [], [xk])
                self.ld(xbc[s][:, 1536:3072], self.XBC[r0:r0 + 128, 1536:3072], [], [xk])
                self.ld(dtr[s], self.Z1[r0:r0 + 128, 5120 + d * 32:5152 + d * 32], [], [f"ss_dtr{s}"], q="pool")
                X, dk = xbc[s], f"ss_dtr{s}"
                if d == 1:
                    for q in range(6):
                        b = self.psA_i % 4
                        self.psA_i += 1
                        self.mm(self.psA[:, b * 512:(b + 1) * 512], self.flipt[:, :], xbc[s][:, q * 512:(q + 1) * 512], True, True,
                                [xk, "flip"], [f"psA{b}"])
                        if q % 2 == 0:
                            P.act(lambda e, o=xf[:, q * 512:(q + 1) * 512], i=self.psA[:, b * 512:(b + 1) * 512]: e.copy(o, i), [f"psA{b}"], ["ss_xf"])
                        else:
                            P.dve(lambda e, o=xf[:, q * 512:(q + 1) * 512], i=self.psA[:, b * 512:(b + 1) * 512]: e.tensor_copy(o, i), [f"psA{b}"], ["ss_xf"])
                    self.mm(self.psC[:, 0:512], self.flipt[:, :], dtrt[s][:, :], True, True, [dk, "flip"], ["psC0"])
                    self.tt(dt[:, :], self.psC[:, 0:32], dtb[d][:, :], ALU.add, ["psC0", "ssc"], ["ss_dt"])
                    X, xk = xf, "ss_xf"
                else:
                    self.tt(dt[:, :], dtr[s], dtb[d][:, :], ALU.add, [dk, "ssc"], ["ss_dt"])
                if float(os.environ.get("SSD_CUT", "9")) < 0.3:
                    continue
                self.actf(dt[:, :], dt[:, :], AF.Exp, ["ss_dt"], ["ss_dt"])
                self.actf(dt[:, :], dt[:, :], AF.Ln, ["ss_dt"], ["ss_dt"], bias=1.0)
                self.stt(da, dt[:, :], -1.0, ab[d][:, :], ALU.mult, ALU.mult, ["ss_dt", "ssc2"], ["ss_da"])
                if float(os.environ.get("SSD_CUT", "9")) < 0.5:
                    continue
                self.mm(self.psC[:, 512:1024], tri[:, :], dat[:, :], True, True, ["ss_da", "ssc"], ["psC1"])
                self.mm(self.psC[:, 0:512], ones[:, :], dat[:, :], True, True, ["ss_da", "ss_ones"], ["psC0"])
                P.dve(lambda e: e.tensor_copy(cum[:, :], self.psC[:, 512:544]), ["psC1"], ["ss_cum"])
                P.dve(lambda e: e.tensor_copy(tot[:, :], self.psC[:, 0:32]), ["psC0"], ["ss_tot"])
                self.actf(ecum[:, :], cum[:, :], AF.Exp, ["ss_cum"], ["ss_ecum"])
                self.actf(cd[:, :], tot[:, :], AF.Exp, ["ss_tot"], ["ss_cd"])
                self.tt(de[:, :], tot[:, :], cum[:, :], ALU.subtract, ["ss_tot", "ss_cum"], ["ss_de"])
                self.actf(de[:, :], de[:, :], AF.Exp, ["ss_de"], ["ss_de"])
                if float(os.environ.get("SSD_CUT", "9")) < 0.7:
                    continue
                xs3 = X[:, 0:2048].rearrange("p (h q) -> p h q", q=64)
                self.tt(xd[:, :].rearrange("p (h q) -> p h q", q=64), xs3, dt[:, :].unsqueeze(2).to_broadcast([128, 32, 64]),
                        ALU.mult, [xk, "ss_dt"], ["ss_xd"])
                self.tt(xde[:, :].rearrange("p (h q) -> p h q", q=64), xd[:, :].rearrange("p (h q) -> p h q", q=64),
                        de[:, :].unsqueeze(2).to_broadcast([128, 32, 64]), ALU.mult, ["ss_xd", "ss_de"], ["ss_xde"])
                if float(os.environ.get("SSD_CUT", "9")) < 2:
                    continue
                for q in range(8):
                    self.tr(self.psB[:, q * 128:(q + 1) * 128], X[:, 2048 + q * 128:2176 + q * 128], [xk, "ident"], [f"psB{q // 4}"])
                P.act(lambda e, o=BCT[:, 0:4, :], i=self.psB[:, 0:512].rearrange("p (g q) -> p g q", q=128): e.copy(o, i), ["psB0"], ["ss_BCT"])
                P.act(lambda e, o=BCT[:, 4:8, :], i=self.psB[:, 512:1024].rearrange("p (g q) -> p g q", q=128): e.copy(o, i), ["psB1"], ["ss_BCT"])
                for g in range(4):
                    self.mm(self.psB[:, g * 128:(g + 1) * 128], BCT[:, g, :], BCT[:, 4 + g, :], True, True, ["ss_BCT"], ["psB0"])
                self.tt(CBm[:, :, :], self.psB[:, 0:512].rearrange("p (g q) -> p g q", q=128),
                        tri[:, :].unsqueeze(1).to_broadcast([128, 4, 128]), ALU.mult, ["psB0", "ssc"], ["ss_CBm"])
                for g in range(4):
                    self.mm(self.psA[:, g * 512:(g + 1) * 512], BCT[:, 4 + g, :], Sst[:, g * 512:(g + 1) * 512], True, True,
                            ["ss_BCT", "ss_S"], [f"psA{g}"])
                yk = f"ss_y{s}"
                for g in range(4):
                    self.tt(yt[s][:, g * 512:(g + 1) * 512].rearrange("p (h q) -> p h q", q=64),
                            self.psA[:, g * 512:(g + 1) * 512].rearrange("p (h q) -> p h q", q=64),
                            ecum[:, g * 8:(g + 1) * 8].unsqueeze(2).to_broadcast([128, 8, 64]), ALU.mult,
                            [f"psA{g}", "ss_ecum"], [yk])
                if float(os.environ.get("SSD_CUT", "9")) < 3:
                    continue
                for hb in range(8):
                    g = hb // 2
                    h0 = hb * 4
                    ms = hb % 2
                    cb = hb % 2
                    self.tt(rseg[:, :, :], tri[:, :].unsqueeze(1).to_broadcast([128, 4, 128]),
                            dat[:, h0:h0 + 4].unsqueeze(2).to_broadcast([128, 4, 128]), ALU.mult, ["ssc", "ss_da"], ["ss_rseg"])
                    pss = self.psC[:, cb * 512:(cb + 1) * 512]
                    self.mm(pss, ones[:, :], rseg[:, :, :].rearrange("p a b -> p (a b)"), True, False, ["ss_ones", "ss_rseg"], [f"psC{cb}"])
                    self.mm(pss, self.ident, neg4[:, :], False, True, ["ident", "ssc"], [f"psC{cb}"])
                    self.tt(seg[:, :, :], pss.rearrange("p (a b) -> p a b", b=128),
                            cum[:, h0:h0 + 4].unsqueeze(2).to_broadcast([128, 4, 128]), ALU.subtract, [f"psC{cb}", "ss_cum"], ["ss_seg"])
                    self.actf(seg[:, :, :], seg[:, :, :], AF.Exp, ["ss_seg"], ["ss_seg"])
                    self.tt(M[ms][:, :, :], seg[:, :, :], CBm[:, g:g + 1, :].to_broadcast([128, 4, 128]), ALU.mult,
                            ["ss_seg", "ss_CBm"], [f"ss_M{ms}"])
                    for q in range(4):
                        h = h0 + q
                        self.mm(self.psA[:, h * 64:(h + 1) * 64], M[ms][:, q, :], xd[:, h * 64:(h + 1) * 64], True, True,
                                [f"ss_M{ms}", "ss_xd"], [f"psA{h // 8}"])
                if float(os.environ.get("SSD_CUT", "9")) < 4:
                    continue
                for g in range(4):
                    self.tt(yt[s][:, g * 512:(g + 1) * 512], yt[s][:, g * 512:(g + 1) * 512], self.psA[:, g * 512:(g + 1) * 512],
                            ALU.add, [yk, f"psA{g}"], [yk])
                self.ld(self.YD[d][j * 128:(j + 1) * 128, :], yt[s][:, :], [yk], [("YD", d, j)] + self.dk("YD%d" % d), q="pool")
                for g in range(4):
                    self.mm(self.psA[:, g * 512:(g + 1) * 512], X[:, 2048 + g * 128:2176 + g * 128], xde[:, g * 512:(g + 1) * 512],
                            True, True, [xk, "ss_xde"], [f"psA{g}"])
                for g in range(4):
                    S3 = Sst[:, g * 512:(g + 1) * 512].rearrange("p (h q) -> p h q", q=64)
                    self.tt(S3, S3, cd[:, g * 8:(g + 1) * 8].unsqueeze(2).to_broadcast([128, 8, 64]), ALU.mult, ["ss_S", "ss_cd"], ["ss_S"])
                    self.tt(Sst[:, g * 512:(g + 1) * 512], Sst[:, g * 512:(g + 1) * 512], self.psA[:, g * 512:(g + 1) * 512], ALU.add,
                            ["ss_S", f"psA{g}"], ["ss_S"])

    def stage_ssd_comb(self):
        P = self.P
        dsk = self.sb("sm_dsk", [128, 32])
        nw = self.sb("sm_nw", [128, 2048])
        self.ld(dsk[:, :], self.bc(self.inp["ssd_d"][0:1, :], 32), [], ["smc"])
        self.ld(nw[:, :], self.bc(self.inp["ssd_norm_w"][0:1, :], 2048), [], ["smc"], q="pool")
        yf = [self.sb(f"sm_f{s}", [128, 2048]) for s in range(2)]
        yb = [self.sb(f"sm_b{s}", [128, 2048]) for s in range(2)]
        xs = [self.sb(f"sm_x{s}", [128, 2048]) for s in range(2)]
        zg = [self.sb(f"sm_z{s}", [128, 2048]) for s in range(2)]
        sq = self.sb("sm_sq", [128, 2048])
        ss = self.sb("sm_ss", [128, 4])
        for T in range(2, NT):
            s = T % 2
            jb = self.tmap(1, T)
            fk = f"sm_f{s}"
            self.ld(yf[s][:, :], self.YD[0][T * 128:(T + 1) * 128, :], [], [fk])
            self.ld(yb[s][:, :], self.YD[1][jb * 128:(jb + 1) * 128, :], [], [f"sm_b{s}"], q="pool")
            self.ld(xs[s][:, :], self.XBC[T * 128:(T + 1) * 128, 0:2048], [], [f"sm_x{s}"])
            self.ld(zg[s][:, :], self.Z1[T * 128:(T + 1) * 128, 0:2048], [], [f"sm_z{s}"], q="pool")
            for q in range(4):
                self.mm(self.psA[:, q * 512:(q + 1) * 512], self.flipt[:, :], yb[s][:, q * 512:(q + 1) * 512], True, True,
                        [f"sm_b{s}", "flip"], [f"psA{q}"])
                self.tt(yf[s][:, q * 512:(q + 1) * 512], yf[s][:, q * 512:(q + 1) * 512], self.psA[:, q * 512:(q + 1) * 512], ALU.add,
                        [fk, f"psA{q}"], [fk])
            x3 = xs[s][:, :].rearrange("p (h q) -> p h q", q=64)
            self.tt(x3, x3, dsk[:, :].unsqueeze(2).to_broadcast([128, 32, 64]), ALU.mult, [f"sm_x{s}", "smc"], [f"sm_x{s}"])
            self.tt(yf[s][:, :], yf[s][:, :], xs[s][:, :], ALU.add, [fk, f"sm_x{s}"], [fk])
            self.actf(zg[s][:, :], zg[s][:, :], AF.Silu, [f"sm_z{s}"], [f"sm_z{s}"])
            self.tt(yf[s][:, :], yf[s][:, :], zg[s][:, :], ALU.mult, [fk, f"sm_z{s}"], [fk])
            self.tt(sq[:, :], yf[s][:, :], yf[s][:, :], ALU.mult, [fk], ["sm_sq"])
            P.dve(lambda e, o=ss[:, :], i=sq[:, :].rearrange("p (g q) -> p g q", q=512): e.tensor_reduce(o, i, AX.X, ALU.add),
                  ["sm_sq"], ["sm_ss"])
            self.rsqrt(ss[:, :], ss[:, :], 1.0 / 512, EPS, ["sm_ss"], ["sm_ss"])
            y3 = yf[s][:, :].rearrange("p (g q) -> p g q", q=512)
            self.tt(y3, y3, ss[:, :].unsqueeze(2).to_broadcast([128, 4, 512]), ALU.mult, [fk, "sm_ss"], [fk])
            self.tt(yf[s][:, :], yf[s][:, :], nw[:, :], ALU.mult, [fk, "smc"], [fk])
            self.ld(self.MIX1[(T - 2) * 128:(T - 1) * 128, :], yf[s][:, :], [fk], [("MIX1", T)] + self.dk("MIX1"), q="pool")

    def stage_outproj(self, i, MIX, mixkeys, K, W, t0, first_x_from_input):
        P = self.P
        nk = K // 128
        wsb = self.sb(f"op_w{i}", [128, nk, D])
        mt = [self.sb(f"op_m{i}{s}", [128, K]) for s in range(2)]
        mT = self.sb(f"op_mT{i}", [128, nk, 128])
        xt = [self.sb(f"op_x{i}{s}", [128, D]) for s in range(2)]
        tmp = self.sb(f"op_t{i}", [128, D])
        Wv = W.rearrange("(c p) n -> p c n", p=128)
        for c in range(nk):
            self.ld(wsb[:, c, :], Wv[:, c, :], [], ["op_w"], q=("sp" if c % 2 == 0 else "pool"))
        for t in range(t0, NT):
            s = t % 2
            who = 1 if t < 2 else 0
            self.ld(mt[s][:, :], MIX[(t - t0) * 128:(t - t0 + 1) * 128, :], [(k, t) for k in mixkeys], [f"op_m{s}"])
            self.ld(xt[s][:, :], self.xsrc(t), [("XR", t)], [f"op_x{s}"], q="pool")
            for c in range(nk):
                pst, key = (self.psB, f"psB{(c % 8) // 4}")
                self.tr(pst[:, (c % 8) * 128:(c % 8 + 1) * 128], mt[s][:, c * 128:(c + 1) * 128], [f"op_m{s}", "ident"], [key])
                if c % 8 == 7 or c == nk - 1:
                    c0 = c - (c % 8)
                    nn = c - c0 + 1
                    P.act(lambda e, o=mT[:, c0:c0 + nn, :], a=pst[:, 0:nn * 128].rearrange("p (c q) -> p c q", q=128): e.copy(o, a),
                          ["psB0", "psB1"], ["op_mT"])
            for j in range(2):
                b = self.psA_i % 4
                self.psA_i += 1
                for c in range(nk):
                    self.mm(self.psA[:, b * 512:(b + 1) * 512], mT[:, c, :], wsb[:, c, j * 512:(j + 1) * 512],
                            c == 0, c == nk - 1, ["op_mT", "op_w"], [f"psA{b}"])
                self.tt(tmp[:, j * 512:(j + 1) * 512], self.psA[:, b * 512:(b + 1) * 512],
                        self.gate[who][0][:, j * 512:(j + 1) * 512], ALU.mult, [f"psA{b}", f"gate{who}0"], ["op_t"])
            self.tt(xt[s][:, :], xt[s][:, :], tmp[:, :], ALU.add, [f"op_x{s}", "op_t"], [f"op_x{s}"])
            self.ld(self.XR[t * 128:(t + 1) * 128, :], xt[s][:, :], [f"op_x{s}"], [("XR", t)] + self.dk("XR"), q="pool")


def rope_table():
    t = np.arange(SEQ)
    row = (t // 64).astype(np.float32)
    col = (t % 64).astype(np.float32)
    m = 16
    inv = (np.float32(10000.0) ** (-np.arange(m, dtype=np.float32) / np.float32(m))).astype(np.float32)
    ar = (row[:, None] * inv).astype(np.float32)
    ac = (col[:, None] * inv).astype(np.float32)
    C = np.concatenate([np.cos(ar), np.cos(ar), np.cos(ac), np.cos(ac)], 1)
    S = np.concatenate([-np.sin(ar), np.sin(ar), -np.sin(ac), np.sin(ac)], 1)
    tab = np.zeros((L, 128), np.float32)
    tab[:LC, :64] = 1.0
    tab[LC:, :64] = C
    tab[LC:, 64:] = S
    return tab


def host_consts():
    return {
        "k_rope": rope_table(),
        "k_sel": np.kron(np.eye(2, dtype=np.float32), np.ones((1, 64), np.float32)),
        "k_iota": np.tile(np.arange(256, dtype=np.float32), (128, 1)),
        "k_tri": np.triu(np.ones((128, 128), np.float32)),
        "k_neg4": np.tile(np.tril(np.full((128, 128), -30000.0, np.float32), -1), (1, 4)),
        "k_ident": np.eye(128, dtype=np.float32),
        "k_flip": np.ascontiguousarray(np.eye(128, dtype=np.float32)[::-1]),
    }


def make_in_maps(inputs):
    consts = host_consts()
    maps = []
    for b in range(8):
        m = {}
        for k in INPUT_SHAPES:
            if k in consts:
                m[k] = consts[k]
            elif k in ("x", "c", "ctx"):
                m[k] = np.ascontiguousarray(np.asarray(inputs[k], dtype=np.float32)[b])
            else:
                m[k] = np.ascontiguousarray(np.asarray(inputs[k], dtype=np.float32))
        maps.append(m)
    return maps


def kernel(**inputs):
    b = Builder()
    nc = b.build()
    res = run_bass_kernel_spmd(nc, make_in_maps(inputs), core_ids=list(range(8)))
    return np.stack([r["out"] for r in res.results], axis=0).astype(np.float32)
```
